# Optimizing a Trainium2 kernel written in Bass

```python
import math
import jax
import jax.numpy as jnp
from jax import lax
import numpy as np

D_MODEL = 2048
BATCH = 4
SEQ = 8192
DEPTH = 2
DEC_BATCH = 2
DEC_SEQ = 16384
PAST_LEN = 128

HEAD_DIM = 64
A_HEADS = 12
A_BRANCHES = ((128, 1), (512, 4), (2048, 16))
A_BLOCK = 64
R_HEADS = 8
R_QK_DIM = 32
R_V_DIM = 64
R_CHUNK = 128
ROPE_BASE = 10000.0
C_HEADS = 12
C_KV_HEADS = 4
C_GROUP = C_HEADS // C_KV_HEADS
C_RADIUS = 128
C_BLOCK = 128
REL_BUCKETS = 32
REL_MAX_DIST = 1024
D_FF = ((8 * D_MODEL + 3 * 256 - 1) // (3 * 256)) * 256
A_W = A_HEADS * HEAD_DIM
R_QK_W = R_HEADS * R_QK_DIM
R_W = R_HEADS * R_V_DIM
C_W = C_HEADS * HEAD_DIM
C_KV_W = C_KV_HEADS * HEAD_DIM
MIX_W = A_W + R_W + C_W
SPLIT_SIZES = (A_W, A_W, A_W, R_QK_W, R_QK_W, R_W, R_W, C_W, C_KV_W, C_KV_W)
IN_COLS = 3 * A_W + 2 * R_QK_W + 2 * R_W + C_W + 2 * C_KV_W
EPS = 1e-6
GN_EPS = 1e-5
NEG = -1e30

kernel_name = 'hybrid_dilated_retention_swa_encoder'


def rms_norm(x, g):
    xf = x.astype(jnp.float32)
    y = xf * lax.rsqrt(jnp.mean(xf * xf, axis=-1, keepdims=True) + EPS) * g.astype(jnp.float32)
    return y.astype(x.dtype)


def t5_bucket(rel):
    nb = REL_BUCKETS // 2
    max_exact = nb // 2
    ret = np.where(rel > 0, nb, 0)
    n = np.abs(rel)
    nf = np.maximum(n, 1).astype(np.float32)
    large = max_exact + (np.log(nf / max_exact) / math.log(REL_MAX_DIST / max_exact) * (nb - max_exact)).astype(np.int32)
    large = np.minimum(large, nb - 1)
    return (ret + np.where(n < max_exact, n, large)).astype(np.int32)


def rel_bias(table, bucket):
    return jnp.take(table.astype(jnp.float32), jnp.asarray(bucket), axis=0).transpose(2, 0, 1)


def dilated_branch(q, k, v, window, dil, bias_tab):
    B, S, H, Dh = q.shape
    R = window // (2 * dil)
    L = S // dil
    nblk = -(-L // A_BLOCK)
    Lp = nblk * A_BLOCK
    pad = Lp - L

    def sub(t):
        return t.reshape(B, L, dil, H, Dh).transpose(0, 2, 1, 3, 4)

    qs = jnp.pad(sub(q), ((0, 0), (0, 0), (0, pad), (0, 0), (0, 0))).reshape(B, dil, nblk, A_BLOCK, H, Dh)

    def windows(t):
        tp = jnp.pad(sub(t), ((0, 0), (0, 0), (A_BLOCK, pad + A_BLOCK), (0, 0), (0, 0)))
        tp = tp.reshape(B, dil, nblk + 2, A_BLOCK, H, Dh)
        return jnp.concatenate([tp[:, :, :-2], tp[:, :, 1:-1], tp[:, :, 2:]], axis=3)

    kw, vw = windows(k), windows(v)
    qq = np.arange(A_BLOCK)[:, None]
    kk = np.arange(3 * A_BLOCK)[None, :]
    off = kk - A_BLOCK - qq
    kpos = np.arange(nblk)[:, None, None] * A_BLOCK + kk[None] - A_BLOCK
    valid = (np.abs(off) <= R)[None] & (kpos >= 0) & (kpos < L)
    bias = rel_bias(bias_tab, t5_bucket(off * dil))
    s = jnp.einsum('bdnqhe,bdnkhe->bdnhqk', qs, kw, preferred_element_type=jnp.float32) * (HEAD_DIM ** -0.5) + bias
    s = jnp.where(jnp.asarray(valid)[:, None], s, NEG)
    m = jnp.max(s, axis=-1)
    p = jnp.exp(s - m[..., None])
    l = jnp.sum(p, axis=-1)
    num = jnp.einsum('bdnhqk,bdnkhe->bdnqhe', p, vw.astype(jnp.float32))
    num = num.reshape(B, dil, Lp, H, Dh)[:, :, :L].transpose(0, 2, 1, 3, 4).reshape(B, S, H, Dh)

    def back(t):
        t = t.transpose(0, 1, 2, 4, 3).reshape(B, dil, Lp, H)[:, :, :L]
        return t.transpose(0, 2, 1, 3).reshape(B, S, H)

    return num, back(m), back(l)


def dilated_attention(q, k, v, bias_tab):
    parts = [dilated_branch(q, k, v, w, d, bias_tab) for (w, d) in A_BRANCHES]
    M = parts[0][1]
    for _, m, _ in parts[1:]:
        M = jnp.maximum(M, m)
    num = 0.0
    den = 0.0
    for n_i, m_i, l_i in parts:
        w_i = jnp.exp(m_i - M)
        num = num + n_i * w_i[..., None]
        den = den + l_i * w_i
    return num / den[..., None]


def rope(x, pos):
    half = R_QK_DIM // 2
    freqs = ROPE_BASE ** (-jnp.arange(half, dtype=jnp.float32) / half)
    ang = pos[:, None] * freqs[None]
    cos = jnp.cos(ang)[None, :, None, :]
    sin = jnp.sin(ang)[None, :, None, :]
    x1, x2 = x[..., :half], x[..., half:]
    return jnp.concatenate([x1 * cos - x2 * sin, x1 * sin + x2 * cos], axis=-1)


def retention_dir(q, k, v, lg, inclusive):
    B, S, H, dk = q.shape
    dv = v.shape[-1]
    C = R_CHUNK
    n = S // C
    qc = q.reshape(B, n, C, H, dk)
    kc = k.reshape(B, n, C, H, dk)
    vc = v.reshape(B, n, C, H, dv)
    t = np.arange(C)
    diff = t[:, None] - t[None, :]
    mask = (diff >= 0) if inclusive else (diff > 0)
    decay = jnp.where(jnp.asarray(mask)[None], jnp.exp(lg[:, None, None] * np.maximum(diff, 0).astype(np.float32)), 0.0)
    s = jnp.einsum('bnthd,bnshd->bnhts', qc, kc) * decay
    intra = jnp.einsum('bnhts,bnshe->bnthe', s, vc)
    tf = t.astype(np.float32)
    kdec = jnp.exp(lg[None, :] * (C - 1 - tf)[:, None])
    kv = jnp.einsum('bnshd,sh,bnshe->bnhde', kc, kdec, vc)
    g_chunk = jnp.exp(lg * C)

    def step(R, kv_i):
        return R * g_chunk[:, None, None] + kv_i, R

    _, r_prev = lax.scan(step, jnp.zeros((B, H, dk, dv), jnp.float32), kv.transpose(1, 0, 2, 3, 4))
    r_prev = r_prev.transpose(1, 0, 2, 3, 4)
    qdec = jnp.exp(lg[None, :] * (tf + 1.0)[:, None])
    cross = jnp.einsum('bnthd,th,bnhde->bnthe', qc, qdec, r_prev)
    return (intra + cross).reshape(B, S, H, dv)


def bidir_retention(q, k, v, gate, a_fwd, a_bwd):
    B, S, H, _ = q.shape
    pos = jnp.arange(S, dtype=jnp.float32)
    q = rope(q.astype(jnp.float32), pos)
    k = rope(k.astype(jnp.float32), pos) * (R_QK_DIM ** -0.5)
    v = v.astype(jnp.float32)
    lg_f = jnp.log1p(-jnp.exp2(-a_fwd.astype(jnp.float32)))
    lg_b = jnp.log1p(-jnp.exp2(-a_bwd.astype(jnp.float32)))
    fwd = retention_dir(q, k, v, lg_f, True)
    bwd = jnp.flip(retention_dir(jnp.flip(q, 1), jnp.flip(k, 1), jnp.flip(v, 1), lg_b, False), 1)
    o = fwd + bwd
    mu = jnp.mean(o, axis=-1, keepdims=True)
    var = jnp.mean(jnp.square(o - mu), axis=-1, keepdims=True)
    o = (o - mu) * lax.rsqrt(var + GN_EPS)
    return o.reshape(B, S, H * R_V_DIM) * jax.nn.silu(gate.astype(jnp.float32))


def window_gqa_sink(q, k, v, bias_tab, sink):
    B, S, _, Dh = q.shape
    nblk = S // C_BLOCK
    qb = q.reshape(B, nblk, C_BLOCK, C_KV_HEADS, C_GROUP, Dh)

    def windows(t):
        tp = jnp.pad(t, ((0, 0), (C_BLOCK, C_BLOCK), (0, 0), (0, 0))).reshape(B, nblk + 2, C_BLOCK, C_KV_HEADS, Dh)
        return jnp.concatenate([tp[:, :-2], tp[:, 1:-1], tp[:, 2:]], axis=2)

    kw, vw = windows(k), windows(v)
    qq = np.arange(C_BLOCK)[:, None]
    kk = np.arange(3 * C_BLOCK)[None, :]
    off = kk - C_BLOCK - qq
    kpos = np.arange(nblk)[:, None, None] * C_BLOCK + kk[None] - C_BLOCK
    valid = (np.abs(off) <= C_RADIUS)[None] & (kpos >= 0) & (kpos < S)
    bias = rel_bias(bias_tab, t5_bucket(off)).reshape(C_KV_HEADS, C_GROUP, C_BLOCK, 3 * C_BLOCK)
    s = jnp.einsum('bnqhge,bnkhe->bnhgqk', qb, kw, preferred_element_type=jnp.float32) * (HEAD_DIM ** -0.5) + bias
    s = jnp.where(jnp.asarray(valid)[:, None, None], s, NEG)
    sk = sink.astype(jnp.float32).reshape(C_KV_HEADS, C_GROUP)[:, :, None]
    m = jnp.maximum(jnp.max(s, axis=-1), sk)
    p = jnp.exp(s - m[..., None])
    den = jnp.sum(p, axis=-1) + jnp.exp(sk - m)
    o = jnp.einsum('bnhgqk,bnkhe->bnqhge', p, vw.astype(jnp.float32))
    o = o / den.transpose(0, 1, 4, 2, 3)[..., None]
    return o.reshape(B, S, C_W)


def encoder_layer(x, bias_tab, g1, w_in, a_fwd, a_bwd, sink, w_out, g2, w_gate, w_up, w_down):
    B, S, _ = x.shape
    h = rms_norm(x, g1)
    proj = h @ w_in
    idx = []
    acc = 0
    for sz in SPLIT_SIZES[:-1]:
        acc += sz
        idx.append(acc)
    qa, ka, va, qr, kr, vr, gr, qc, kc, vc = jnp.split(proj, idx, axis=-1)
    a = dilated_attention(qa.reshape(B, S, A_HEADS, HEAD_DIM), ka.reshape(B, S, A_HEADS, HEAD_DIM),
                          va.reshape(B, S, A_HEADS, HEAD_DIM), bias_tab[:, :A_HEADS]).reshape(B, S, A_W)
    r = bidir_retention(qr.reshape(B, S, R_HEADS, R_QK_DIM), kr.reshape(B, S, R_HEADS, R_QK_DIM),
                        vr.reshape(B, S, R_HEADS, R_V_DIM), gr, a_fwd, a_bwd)
    c = window_gqa_sink(qc.reshape(B, S, C_HEADS, HEAD_DIM), kc.reshape(B, S, C_KV_HEADS, HEAD_DIM),
                        vc.reshape(B, S, C_KV_HEADS, HEAD_DIM), bias_tab[:, A_HEADS:], sink)
    mix = jnp.concatenate([a, r, c], axis=-1).astype(x.dtype) @ w_out
    x = x + mix
    h = rms_norm(x, g2)
    x = x + (jax.nn.silu(h @ w_gate) * (h @ w_up)) @ w_down
    return x


def setup_inputs(seed: int = 0) -> dict:
    key = jax.random.key(seed)
    ks = jax.random.split(key, 14)
    f32 = jnp.float32
    base_decay = 5.0 + jnp.arange(R_HEADS, dtype=f32)
    return {
        'x_prompt': jax.random.normal(ks[0], (BATCH, SEQ, D_MODEL), f32),
        'x_sample': jax.random.normal(ks[1], (DEC_BATCH, DEC_SEQ, D_MODEL), f32),
        'rel_bias': 0.5 * jax.random.normal(ks[2], (REL_BUCKETS, A_HEADS + C_HEADS), f32),
        'norm1_g': 1.0 + 0.01 * jax.random.normal(ks[3], (DEPTH, D_MODEL), f32),
        'w_in': jax.random.normal(ks[4], (DEPTH, D_MODEL, IN_COLS), f32) * D_MODEL ** -0.5,
        'ret_decay_fwd': base_decay + 0.1 * jax.random.normal(ks[5], (DEPTH, R_HEADS), f32),
        'ret_decay_bwd': base_decay + 0.1 * jax.random.normal(ks[6], (DEPTH, R_HEADS), f32),
        'attn_sink': 0.5 * jax.random.normal(ks[7], (DEPTH, C_HEADS), f32),
        'w_out': jax.random.normal(ks[8], (DEPTH, MIX_W, D_MODEL), f32) * MIX_W ** -0.5,
        'norm2_g': 1.0 + 0.01 * jax.random.normal(ks[9], (DEPTH, D_MODEL), f32),
        'w_gate': jax.random.normal(ks[10], (DEPTH, D_MODEL, D_FF), f32) * D_MODEL ** -0.5,
        'w_up': jax.random.normal(ks[11], (DEPTH, D_MODEL, D_FF), f32) * D_MODEL ** -0.5,
        'w_down': jax.random.normal(ks[12], (DEPTH, D_FF, D_MODEL), f32) * D_FF ** -0.5,
        'final_norm_g': 1.0 + 0.01 * jax.random.normal(ks[13], (D_MODEL,), f32),
    }


def reference(x_prompt, x_sample, rel_bias, norm1_g, w_in, ret_decay_fwd, ret_decay_bwd, attn_sink,
              w_out, norm2_g, w_gate, w_up, w_down, final_norm_g):
    def trunk(x):
        for i in range(DEPTH):
            x = encoder_layer(x, rel_bias, norm1_g[i], w_in[i], ret_decay_fwd[i], ret_decay_bwd[i],
                              attn_sink[i], w_out[i], norm2_g[i], w_gate[i], w_up[i], w_down[i])
        return rms_norm(x, final_norm_g)

    y_prompt = trunk(x_prompt)
    y_sample = trunk(x_sample)
    return (y_prompt, y_sample)
```

```python
import math
import os
from contextlib import ExitStack
import numpy as np
import ml_dtypes
import concourse.bass as bass
import concourse.mybir as mybir
from concourse.bass_utils import run_bass_kernel_spmd

F32 = mybir.dt.float32
BF16 = mybir.dt.bfloat16
AF = mybir.ActivationFunctionType
ALU = mybir.AluOpType
AX = mybir.AxisListType

D = 2048
KC = 16
DFF = 5632
FC = 44
NL = 2
ST = 512
HA = 1024
HC = 128
EPS = 1e-6
GN_EPS = 1e-5
NEGB = -30000.0
DILS = (1, 4, 16)


SCOPE = [None]


class SemSlot:
    __slots__ = ("sem", "cnt")


class Buf:
    __slots__ = ("name", "writers", "readers", "sem", "scope", "war")

    def __init__(self, name):
        self.name = name
        self.writers = {}
        self.readers = {}
        self.sem = None
        self.war = set()
        self.scope = SCOPE[0]


class Op:
    __slots__ = ("eng", "fn", "deps", "flag", "done", "dma", "inc")


class Prog:
    def __init__(self, nc, es):
        self.nc = nc
        self.es = es
        self.ges = es
        self.ops = []
        self.engs = {"pe": nc.tensor, "act": nc.scalar, "dve": nc.vector, "pool": nc.gpsimd, "sp": nc.sync}
        self.esem = {e: es.enter_context(nc.semaphore("s_" + e)) for e in ("pe", "act", "dve", "pool")}
        self.nsem = 0
        self.last = {}
        self.free_sems = []
        self.enabled = True

    def sb(self, name, shape, dt):
        self.nsem += 1
        return self.es.enter_context(self.nc.sbuf_tensor("s%d_%s" % (self.nsem, name), list(shape), dt))

    def ps(self, name, shape, dt=F32):
        self.nsem += 1
        return self.es.enter_context(self.nc.psum_tensor("p%d_%s" % (self.nsem, name), list(shape), dt))

    def op(self, eng, fn, R=(), W=(), dma=False, inc=16):
        if not self.enabled:
            return None
        o = Op()
        o.eng, o.fn, o.flag, o.dma, o.inc, o.done = eng, fn, False, dma, inc, None
        deps = set()
        for b in R:
            deps.update(b.writers.values())
        for b in W:
            if b.readers:
                b.war = set(b.readers.values()) | set(b.writers.values())
                b.writers = {}
                b.readers = {}
            deps.update(b.war)
        if dma:
            b0 = W[0]
            if b0.sem is None:
                if self.free_sems:
                    b0.sem = self.free_sems.pop()
                else:
                    sl = SemSlot()
                    sl.sem = self.ges.enter_context(self.nc.semaphore("d%d" % self.nsem))
                    sl.cnt = 0
                    self.nsem += 1
                    b0.sem = sl
                if b0.scope is not None:
                    b0.scope.append(b0.sem)
            b0.sem.cnt += inc
            o.done = (b0.sem.sem, b0.sem.cnt)
            key = id(b0.sem.sem)
        else:
            key = eng
        for b in R:
            b.readers[key] = o
        for b in W:
            b.writers[key] = o
        deps.discard(o)
        o.deps = deps
        self.ops.append(o)
        self.last[key] = o
        return o

    def barrier(self):
        if not self.enabled:
            return
        allops = list(self.last.values())
        for e in ("pe", "act", "dve", "pool", "sp"):
            o = Op()
            o.eng, o.fn, o.flag, o.dma, o.inc, o.done = e, (lambda E: None), False, False, 16, None
            o.deps = set(allops)
            self.ops.append(o)

    def emit(self):
        for o in self.ops:
            for d in o.deps:
                d.flag = True
        cnt = {e: 0 for e in self.esem}
        for o in self.ops:
            if (not o.dma) and o.flag:
                cnt[o.eng] += 1
                o.done = (self.esem[o.eng], cnt[o.eng])
        waited = {e: {} for e in self.engs}
        for o in self.ops:
            E = self.engs[o.eng]
            need = {}
            for d in o.deps:
                if (not d.dma) and d.eng == "pe" and o.eng == "pe":
                    continue
                sem, val = d.done
                k = id(sem)
                if k not in need or need[k][1] < val:
                    need[k] = (sem, val)
            wd = waited[o.eng]
            for k, (sem, val) in need.items():
                if wd.get(k, 0) >= val:
                    continue
                E.wait_ge(sem, val)
                wd[k] = val
            ins = o.fn(E)
            if ins is None:
                continue
            if o.dma:
                ins.then_inc(o.done[0], o.inc)
            elif o.flag:
                ins.then_inc(o.done[0], 1)


class Rot:
    def __init__(self, items):
        self.items = items
        self.i = 0

    def next(self):
        it = self.items[self.i % len(self.items)]
        self.i += 1
        return it


def t5_bucket(rel):
    nb = 16
    max_exact = 8
    ret = np.where(rel > 0, nb, 0)
    n = np.abs(rel)
    nf = np.maximum(n, 1).astype(np.float32)
    large = max_exact + (np.log(nf / max_exact) / math.log(1024 / max_exact) * (nb - max_exact)).astype(np.int32)
    large = np.minimum(large, nb - 1)
    return (ret + np.where(n < max_exact, n, large)).astype(np.int32)


VARIANTS = [(1, -64, -128, 256, 64), (4, -64, -128, 256, 64), (16, -64, -128, 256, 64), (1, -128, -256, 384, 128)]
NM = 512


def host_consts(T):
    c = {}
    oh = np.zeros((32, 4, NM), np.float32)
    masks = np.zeros((4, 128, 384), np.float32)
    for v, (dil, koff, qoff, NQ, rad) in enumerate(VARIANTS):
        m = np.arange(NM)
        off = (127 - m) + (koff - qoff)
        b = t5_bucket(off * dil)
        oh[b, v, m] = 1.0
        kk = np.arange(128)[:, None]
        qq = np.arange(NQ)[None, :]
        o2 = kk - qq + (koff - qoff)
        masks[v, :, :NQ] = (np.abs(o2) <= rad).astype(np.float32)
    c["oh"] = oh.reshape(32, 4 * NM)
    c["amask"] = np.ascontiguousarray(masks.transpose(1, 0, 2)).reshape(128, 4 * 384)
    c["ident"] = np.eye(128, dtype=np.float32)
    c["jrev"] = np.ascontiguousarray(np.eye(128, dtype=np.float32)[::-1])
    perm = np.zeros((128, 128), np.float32)
    for p in range(128):
        h, i = divmod(p, 32)
        perm[32 * h + (i + 16) % 32, p] = 1.0
    c["perm"] = perm
    sel = np.zeros((128, 64), np.float32)
    sel[64, :] = 1.0
    c["sel"] = sel
    s = np.arange(128)[:, None]
    t = np.arange(128)[None, :]
    rt = np.zeros((128, 4 * 128 + 4), np.float32)
    rt[:, 0:128] = np.maximum(t - s, 0)
    rt[:, 128:256] = (t >= s)
    rt[:, 256:384] = np.maximum(s - t, 0)
    rt[:, 384:512] = (s > t)
    rt[:, 512] = 127 - np.arange(128)
    rt[:, 513] = np.arange(128)
    c["rtab"] = rt
    qe = np.zeros((128, 256), np.float32)
    qe[:, 0:128] = np.arange(128)[None, :] + 1.0
    qe[:, 128:256] = 128.0 - np.arange(128)[None, :]
    c["qexp"] = qe
    nn = np.arange(T // 128, dtype=np.float32)
    c["nidx"] = np.tile(np.concatenate([nn, nn[::-1]])[None, :], (128, 1))
    return c


def rope_tables(pos0, T):
    half = 16
    freqs = (10000.0 ** (-np.arange(half, dtype=np.float32) / half)).astype(np.float32)
    pos = (pos0 + np.arange(T)).astype(np.float32)
    ang = (pos[:, None] * freqs[None]).astype(np.float32)
    cos = np.cos(ang).astype(np.float32)
    sin = np.sin(ang).astype(np.float32)
    cF = np.zeros((128, T), np.float32)
    sF = np.zeros((128, T), np.float32)
    for p in range(128):
        i = p % 32
        cF[p] = cos[:, i % 16]
        sF[p] = -sin[:, i % 16] if i < 16 else sin[:, i % 16]
    cT = np.tile(cos, (1, 8))
    sT = np.tile(sin, (1, 8))
    return cF, sF, cT, sT


WIN_F = 26
WIN_T = 5


SMALLW = bool(int(os.environ.get('KSMALLW', '0')))


def build(T):
    nb_ = (lambda n: 1) if SMALLW else (lambda n: n)
    nc = bass.Bass("TRN2", target_bir_lowering=False)
    es = ExitStack()
    P = Prog(nc, es)
    NT = T // 128
    NS = T // ST
    SUB = ST // 128
    TA = HA + T + HA
    TCx = HC + T + HC

    def din(name, shape, dt=F32):
        return nc.dram_tensor(name, list(shape), dt, kind="ExternalInput").ap()

    DBG = set(os.environ.get('KDBG', '').split(','))

    def dscr(name, shape, dt):
        if name in DBG:
            return nc.dram_tensor(name, list(shape), dt, kind="ExternalOutput").ap()
        return nc.dram_tensor(name, list(shape), dt).ap()

    x_in = din("x", [T, D])
    y_out = nc.dram_tensor("y", [T, D], F32, kind="ExternalOutput").ap()
    wFin = [din("wFin%d" % l, [nb_(WIN_F), 128, KC * 128]) for l in range(NL)]
    wTin = [din("wTin%d" % l, [nb_(WIN_T), 128, KC * 512]) for l in range(NL)]
    wOut = [din("wOut%d" % l, [nb_(16), 128, KC * 128]) for l in range(NL)]
    wGa = [din("wGa%d" % l, [nb_(FC), 128, KC * 128]) for l in range(NL)]
    wUp = [din("wUp%d" % l, [nb_(FC), 128, KC * 128]) for l in range(NL)]
    wDn = [din("wDn%d" % l, [nb_(16), 128, FC * 128]) for l in range(NL)]
    gtab_in = din("gtab", [128, 5 * KC])
    dec_in = din("dec", [128, NL * 3 * 2])
    dec8_in = din("dec8", [128, NL * 2 * 8])
    sink_in = din("sink", [128, NL * 12])
    relb_in = din("relb", [32, 24])
    flags_in = din("flags", [128, 8])
    oh_in = din("oh", [32, 4 * NM])
    amask_in = din("amask", [128, 4 * 384])
    ident_in = din("ident", [128, 128])
    jrev_in = din("jrev", [128, 128])
    perm_in = din("perm", [128, 128])
    sel_in = din("sel", [128, 64])
    rtab_in = din("rtab", [128, 516])
    qexp_in = din("qexp", [128, 256])
    nidx_in = din("nidx", [128, 2 * NT])
    cF_in = din("cF", [128, T])
    sF_in = din("sF", [128, T])
    cT_in = din("cT", [T, 128])
    sT_in = din("sT", [T, 128])

    wFb = [dscr("wFb%d" % l, [WIN_F, 128, KC * 128], BF16) for l in range(NL)]
    wTb = [dscr("wTb%d" % l, [WIN_T, 128, KC * 512], BF16) for l in range(NL)]
    wOb = [dscr("wOb%d" % l, [16, 128, KC * 128], BF16) for l in range(NL)]
    wGb = [dscr("wGb%d" % l, [FC, 128, KC * 128], BF16) for l in range(NL)]
    wUb = [dscr("wUb%d" % l, [FC, 128, KC * 128], BF16) for l in range(NL)]
    wDb = [dscr("wDb%d" % l, [16, 128, FC * 128], BF16) for l in range(NL)]
    B_w = Buf("wscratch")
    xT_d = dscr("xT_d", [D, T], F32)
    B_xT = [Buf("xT%d" % j) for j in range(NS)]
    QA_d = dscr("QA_d", [768, T], BF16)
    KA_d = dscr("KA_d", [768, TA], BF16)
    VA_d = dscr("VA_d", [TA, 780], BF16)
    QC_d = dscr("QC_d", [768, T], BF16)
    KCx_d = dscr("KC_d", [256, TCx], BF16)
    VC_d = dscr("VC_d", [TCx, 260], BF16)
    QR_d = dscr("QR_d", [384, T], BF16)
    KR_d = dscr("KR_d", [384, T], BF16)
    VR_d = dscr("VR_d", [T, 512], BF16)
    GS_d = dscr("GS_d", [T, 512], BF16)
    KV_d = dscr("KV_d", [96, 6, NT, 64], F32)
    mixT_d = dscr("mixT_d", [D, T], BF16)
    EB_d = dscr("EB_d", [4, 12, 128, 384], BF16)
    ub_d = dscr("ub_d", [4, 24, NM], F32)
    B_QA, B_KA, B_VA, B_QC, B_KC, B_VC = (Buf(n) for n in ("QA", "KA", "VA", "QC", "KC", "VC"))
    B_QR, B_KR, B_VR, B_GS, B_KV, B_mix, B_EB, B_ub = (Buf(n) for n in ("QR", "KR", "VR", "GS", "KV", "mix", "EB", "ub"))
    PK_ROWS = 768 * 2 + 2 * 1024 + 256 * 2 + 2 * 128
    SEC_ROWS = [768, 768, 1024, 1024, 768]
    pins = [dscr("pin%d" % i, [r, 1024], BF16) for i, r in enumerate(SEC_ROWS)]
    pouts = [dscr("pout%d" % i, [2 * r, 1024], BF16) for i, r in enumerate(SEC_ROWS)]
    B_pins = [Buf("pin%d" % i) for i in range(5)]
    B_pouts = [Buf("pout%d" % i) for i in range(5)]
    sin_ = dscr("sin_", [96, 6 * 64], F32)
    sout = dscr("sout", [2 * 96, 6 * 64], F32)
    B_sin, B_sout = Buf("sin"), Buf("sout")
    B_y = Buf("y")

    def cp(E, out, in_):
        return E.tensor_copy(out, in_)

    def dma(q, out, in_, R, W):
        return P.op(q, lambda E, o=out, i=in_: E.dma_start(out=o, in_=i), R=R, W=W, dma=True)

    def const(name, src, shape, dt=F32):
        t = P.sb(name, shape, dt)
        b = Buf(name)
        dma("sp", t[:], src, [], [b])
        return t, b

    ident, B_ident = const("ident", ident_in[:, :], [128, 128])
    jrev, B_jrev = const("jrev", jrev_in[:, :], [128, 128])
    perm, B_perm = const("perm", perm_in[:, :], [128, 128])
    sel, B_sel = const("sel", sel_in[:, :], [128, 64])
    gtab, B_gtab = const("gtab", gtab_in[:, :], [128, 5 * KC])
    flags, B_flags = const("flags", flags_in[:, :], [128, 8])
    decp, B_decp = const("decp", dec_in[:, :], [128, NL * 6])
    dec8, B_dec8 = const("dec8", dec8_in[:, :], [128, NL * 16])
    sinkt, B_sink = const("sinkt", sink_in[:, :], [128, NL * 12])
    rtab, B_rtab = const("rtab", rtab_in[:, :], [128, 516])
    qexp, B_qexp = const("qexp", qexp_in[:, :], [128, 256])
    nidx, B_nidx = const("nidx", nidx_in[:, :], [128, 2 * NT])
    ones_bf = P.sb("ones_bf", [128, 128], BF16)
    B_ones = Buf("ones")
    P.op("pool", lambda E: E.memset(ones_bf[:], 1.0), W=[B_ones])
    identb = P.sb("identb", [128, 128], BF16)
    B_identb = Buf("identb")
    P.op("dve", lambda E: cp(E, identb[:], ident[:]), R=[B_ident], W=[B_identb])
    zcol = P.sb("zcol", [128, 1], F32)
    B_zcol = Buf("zcol")
    P.op("pool", lambda E: E.memset(zcol[:], 0.0), W=[B_zcol])

    psb = Rot([(P.ps("ps%d" % i, [128, 512]), Buf("ps%d" % i)) for i in range(7)])
    psbf = Rot([(P.ps("psbf", [128, 1024], BF16), Buf("psbf"))])
    ones32 = P.sb("ones32", [128, 32], F32)
    B_ones32 = Buf("ones32")
    P.op("pool", lambda E: E.memset(ones32[:], 1.0), W=[B_ones32])

    def lgcalc(name, src, Bsrc, n):
        t = P.sb(name, [128, n], F32)
        b = Buf(name)
        P.op("act", lambda E: E.activation(out=t[:], in_=src[:], func=AF.Exp, scale=-math.log(2.0)), R=[Bsrc], W=[b])
        P.op("dve", lambda E: E.tensor_scalar(out=t[:], in0=t[:], scalar1=-1.0, scalar2=1.0, op0=ALU.mult, op1=ALU.add), R=[b], W=[b])
        P.op("act", lambda E: E.activation(out=t[:], in_=t[:], func=AF.Ln), R=[b], W=[b])
        return t, b

    lgp, B_lgp = lgcalc("lgp", decp, B_decp, NL * 6)
    lg8, B_lg8 = lgcalc("lg8", dec8, B_dec8, NL * 16)
    sinke = P.sb("sinke", [128, NL * 12], F32)
    B_sinke = Buf("sinke")
    P.op("act", lambda E: E.activation(out=sinke[:], in_=sinkt[:], func=AF.Exp), R=[B_sink], W=[B_sinke])

    PH = set(os.environ.get('KPH', 'B,W,0,P1,X,R,A,P3').split(','))
    P.enabled = 'B' in PH
    with ExitStack() as bes:
        sv = P.es
        P.es = bes
        relb, B_relb = const("relb", relb_in[:, :], [32, 24])
        oh, B_oh = const("oh", oh_in[:, :], [32, 4 * NM])
        amask, B_amask = const("amask", amask_in[:, :], [128, 4 * 384])
        usb = P.sb("usb", [24, NM], F32)
        B_usb = Buf("usb")
        hk = Rot([(P.sb("hk%d" % i, [128, 384], F32), Buf("hk%d" % i)) for i in range(2)])
        ee = Rot([(P.sb("ee%d" % i, [128, 384], F32), Buf("ee%d" % i)) for i in range(2)])
        eb = Rot([(P.sb("eb%d" % i, [128, 384], BF16), Buf("eb%d" % i)) for i in range(2)])
        for v, (dil, koff, qoff, NQ, rad) in enumerate(VARIANTS):
            pt, pb = psb.next()
            P.op("pe", lambda E, pt=pt, v=v: E.matmul(pt[0:24, 0:NM], relb[:, :], oh[:, v * NM:(v + 1) * NM], start=True, stop=True),
                 R=[B_relb, B_oh], W=[pb])
            P.op("act", lambda E, pt=pt: E.activation(out=usb[:], in_=pt[0:24, 0:NM], func=AF.Copy), R=[pb], W=[B_usb])
            dma("sp", ub_d[v], usb[:], [B_usb], [B_ub])
            for h in range(12):
                hh = h if v < 3 else 12 + h
                ht, hb = hk.next()
                src = bass.AP(ub_d.tensor, (v * 24 + hh) * NM, [[1, 128], [1, NQ]])
                dma("sp", ht[:, 0:NQ], src, [B_ub], [hb])
                pt, pb = psb.next()
                P.op("pe", lambda E, pt=pt, ht=ht, NQ=NQ: E.matmul(pt[:, 0:NQ], jrev[:, :], ht[:, 0:NQ], start=True, stop=True),
                     R=[hb, B_jrev], W=[pb])
                et, ebf = ee.next()
                P.op("act", lambda E, pt=pt, et=et, NQ=NQ: E.activation(out=et[:, 0:NQ], in_=pt[:, 0:NQ], func=AF.Exp), R=[pb], W=[ebf])
                bt, bb = eb.next()
                P.op("dve", lambda E, et=et, bt=bt, NQ=NQ, v=v: E.tensor_tensor(out=bt[:, 0:NQ], in0=et[:, 0:NQ], in1=amask[:, v * 384:v * 384 + NQ], op=ALU.mult),
                     R=[ebf, B_amask], W=[bb])
                dma("pool", EB_d[v, h, :, 0:NQ], bt[:, 0:NQ], [bb], [B_EB])
        P.es = sv

    class Scope:
        def __enter__(self):
            P.barrier()
            self.sems = []
            SCOPE[0] = self.sems
            self.sv = P.es
            self.st = ExitStack()
            P.es = self.st
            return self

        def __exit__(self, *a):
            P.barrier()
            self.st.close()
            P.es = self.sv
            SCOPE[0] = None
            P.free_sems.extend(self.sems)
            return False

    def rot(name, n, shape, dt, ps=False):
        return Rot([((P.ps if ps else P.sb)("%s%d" % (name, i), shape, dt), Buf("%s%d" % (name, i))) for i in range(n)])

    cnt = [0]

    def alt(engs):
        cnt[0] += 1
        return engs[cnt[0] % len(engs)]

    P.enabled = 'W' in PH
    with Scope():
        wst = rot("wst", 2, [128, 8192], F32)
        wsb = rot("wsb", 2, [128, 8192], BF16)
        for l in range(NL):
            fams = [(wFin[l], wFb[l], WIN_F, KC, 128, l * KC), (wTin[l], wTb[l], WIN_T, KC, 512, l * KC),
                    (wOut[l], wOb[l], 16, KC, 128, None), (wGa[l], wGb[l], FC, KC, 128, (2 + l) * KC),
                    (wUp[l], wUb[l], FC, KC, 128, (2 + l) * KC), (wDn[l], wDb[l], 16, FC, 128, None)]
            for (src, dst, nblk, nk, ncol, gcol) in fams:
                n = nk * ncol
                for blk in range(nblk):
                    a, ab = wst.next()
                    b, bb = wsb.next()
                    dma("sp", a[:, 0:n], src[0 if SMALLW else blk], [], [ab])
                    eng = alt(["dve", "act"])
                    if gcol is None:
                        if eng == "dve":
                            P.op("dve", lambda E, a=a, b=b, n=n: E.tensor_copy(b[:, 0:n], a[:, 0:n]), R=[ab], W=[bb])
                        else:
                            P.op("act", lambda E, a=a, b=b, n=n: E.activation(out=b[:, 0:n], in_=a[:, 0:n], func=AF.Copy), R=[ab], W=[bb])
                    else:
                        for kc in range(nk):
                            sl = slice(kc * ncol, (kc + 1) * ncol)
                            gs_ = gtab[:, gcol + kc:gcol + kc + 1]
                            if eng == "dve":
                                P.op("dve", lambda E, a=a, b=b, sl=sl, gs_=gs_: E.tensor_scalar(out=b[:, sl], in0=a[:, sl], scalar1=gs_, scalar2=None, op0=ALU.mult),
                                     R=[ab, B_gtab], W=[bb])
                            else:
                                P.op("act", lambda E, a=a, b=b, sl=sl, gs_=gs_: E.activation(out=b[:, sl], in_=a[:, sl], func=AF.Copy, scale=gs_),
                                     R=[ab, B_gtab], W=[bb])
                    dma("pool", dst[blk], b[:, 0:n], [bb], [B_w])

    P.enabled = '0' in PH
    xT_v = xT_d.rearrange("(k p) t -> p k t", p=128)
    with Scope():
        xin = rot("xin", 2, [128, D], F32)
        xo = rot("xo", 2, [128, KC, 128], F32)
        for i in range(NT):
            a, ab = xin.next()
            o, ob = xo.next()
            dma("sp", a[:], x_in[i * 128:(i + 1) * 128, :], [], [ab])
            for g in range(4):
                pt, pb = psb.next()
                for q in range(4):
                    kc = 4 * g + q
                    P.op("pe", lambda E, pt=pt, a=a, q=q, kc=kc: E.transpose(pt[:, q * 128:(q + 1) * 128], a[:, kc * 128:(kc + 1) * 128], ident[:]),
                         R=[ab, B_ident], W=[pb])
                eng = alt(["dve", "act"])
                if eng == "dve":
                    P.op("dve", lambda E, pt=pt, o=o, g=g: E.tensor_copy(o[:, 4 * g:4 * g + 4, :], pt[:].rearrange("p (q t) -> p q t", q=4)), R=[pb], W=[ob])
                else:
                    P.op("act", lambda E, pt=pt, o=o, g=g: E.activation(out=o[:, 4 * g:4 * g + 4, :], in_=pt[:].rearrange("p (q t) -> p q t", q=4), func=AF.Copy), R=[pb], W=[ob])
            dma("pool", xT_v[:, :, i * 128:(i + 1) * 128], o[:], [ob], [B_xT[(i * 128) // ST]])

    def rms_stats(xt, xb, sqr, rsr):
        pt, pb = psb.next()
        for kc in range(KC):
            s_, sb_ = sqr.next()
            P.op("act", lambda E, s_=s_, kc=kc: E.activation(out=s_[:], in_=xt[:, kc, :], func=AF.Square), R=[xb], W=[sb_])
            P.op("pe", lambda E, s_=s_, kc=kc, pt=pt: E.matmul(pt[:, 0:ST], ones_bf[:], s_[:], start=(kc == 0), stop=(kc == KC - 1)),
                 R=[sb_, B_ones], W=[pb])
        r_, rb_ = rsr.next()
        P.op("dve", lambda E, r_=r_, pt=pt: E.tensor_scalar(out=r_[:], in0=pt[:, 0:ST], scalar1=1.0 / D, scalar2=EPS, op0=ALU.mult, op1=ALU.add), R=[pb], W=[rb_])
        P.op("act", lambda E, r_=r_: E.activation(out=r_[:], in_=r_[:], func=AF.Sqrt), R=[rb_], W=[rb_])
        P.op("dve", lambda E, r_=r_: E.reciprocal(r_[:], r_[:]), R=[rb_], W=[rb_])
        return r_, rb_

    def evac(pt_ap, out_ap, R, W, func=AF.Copy, eng=None):
        if eng is None:
            eng = alt(["act", "dve"])
        if eng == "dve" and func == AF.Copy:
            P.op("dve", lambda E: E.tensor_copy(out_ap, pt_ap), R=R, W=W)
        else:
            P.op("act", lambda E: E.activation(out=out_ap, in_=pt_ap, func=func), R=R, W=W)

    DKS = 32.0 ** -0.5
    HENG = os.environ.get('KHENG', 'dve,pool').split(',')
    OCS = set(int(v) for v in os.environ.get('KOCS', ','.join(str(i) for i in range(WIN_F))).split(','))
    KF = os.environ.get('KF', 'E,D').split(',')

    def P1(l):
        with Scope():
            xtr = rot("xt", 2, [128, KC, ST], F32)
            hTr = rot("hT", 1, [128, KC, ST], BF16)
            sqr = rot("sq", 2, [128, ST], BF16)
            rsr = rot("rs", 1, [128, ST], F32)
            wfr = rot("wf", 3, [128, KC * 128], BF16)
            wtr = rot("wt", 2, [128, KC * 512], BF16)
            otr = rot("ot", 3, [128, ST], BF16)
            y32r = rot("y32", 2, [128, ST], F32)
            t1r = rot("t1", 2, [128, ST], F32)
            t2r = rot("t2", 2, [128, ST], F32)
            cFt, B_cFt = P.sb("cFt", [128, ST], F32), Buf("cFt")
            sFt, B_sFt = P.sb("sFt", [128, ST], F32), Buf("sFt")
            cTt, B_cTt = P.sb("cTt", [128, SUB, 128], F32), Buf("cTt")
            sTt, B_sTt = P.sb("sTt", [128, SUB, 128], F32), Buf("sTt")
            vaug, B_vaug = P.sb("vaug", [128, SUB, 780], BF16), Buf("vaug")
            vcaug, B_vcaug = P.sb("vcaug", [128, SUB, 260], BF16), Buf("vcaug")
            vrt, B_vrt = P.sb("vrt", [128, SUB, 512], BF16), Buf("vrt")
            gst, B_gst = P.sb("gst", [128, SUB, 512], BF16), Buf("gst")
            kraw, B_kraw = P.sb("kraw", [128, 256], F32), Buf("kraw")
            kro, B_kro = P.sb("kro", [128, 256], F32), Buf("kro")
            kta, B_kta = P.sb("kta", [128, 128], F32), Buf("kta")
            ktb, B_ktb = P.sb("ktb", [128, 128], F32), Buf("ktb")
            kd, B_kd = P.sb("kd", [128, 2, 256], BF16), Buf("kd")
            kvt, B_kvt = P.sb("kvt", [96, 6, SUB, 64], F32), Buf("kvt")
            kdfull, B_kdf = P.sb("kdfull", [128, 2, 256], F32), Buf("kdfull")
            e8, B_e8 = P.sb("e8", [128, 16], F32), Buf("e8")
            P.op("pool", lambda E: E.memset(vaug[:], 1.0), W=[B_vaug])
            P.op("pool", lambda E: E.memset(vcaug[:], 1.0), W=[B_vcaug])
            P.op("pool", lambda E: E.memset(kvt[:], 0.0), W=[B_kvt])
            for dr in range(2):
                P.op("dve", lambda E, dr=dr: E.tensor_scalar(out=e8[:, dr * 8:(dr + 1) * 8], in0=lg8[:, (l * 2 + dr) * 8:(l * 2 + dr) * 8 + 8],
                                                            scalar1=rtab[:, 512 + dr:513 + dr], scalar2=None, op0=ALU.mult), R=[B_lg8, B_rtab], W=[B_e8])
            P.op("act", lambda E: E.activation(out=e8[:], in_=e8[:], func=AF.Exp), R=[B_e8], W=[B_e8])
            P.op("dve", lambda E: E.tensor_scalar(out=e8[:], in0=e8[:], scalar1=DKS, scalar2=None, op0=ALU.mult), R=[B_e8], W=[B_e8])
            for dr in range(2):
                for h in range(8):
                    P.op("act", lambda E, dr=dr, h=h: E.activation(out=kdfull[:, dr, h * 32:(h + 1) * 32], in_=ones32[:, :], func=AF.Copy,
                                                                  scale=e8[:, dr * 8 + h:dr * 8 + h + 1]), R=[B_e8, B_ones32], W=[B_kdf])
            vaug_v = vaug[:].rearrange("p s (h e) -> p s h e", e=65)
            vcaug_v = vcaug[:].rearrange("p s (h e) -> p s h e", e=65)
            for j in range(NS):
                ts = slice(j * ST, (j + 1) * ST)
                xt, xb = xtr.next()
                dma("sp", xt[:], xT_v[:, :, ts], [B_xT[j]], [xb])
                dma("sp", cFt[:], cF_in[:, ts], [], [B_cFt])
                dma("sp", sFt[:], sF_in[:, ts], [], [B_sFt])
                dma("sp", cTt[:], cT_in[ts, :].rearrange("(s p) f -> p s f", p=128), [], [B_cTt])
                dma("sp", sTt[:], sT_in[ts, :].rearrange("(s p) f -> p s f", p=128), [], [B_sTt])
                SP1 = set(os.environ.get('KP1', 'S,H,F,FR,T0,T1,T2,T3,T4,KV').split(','))
                en0 = P.enabled
                P.enabled = en0 and 'S' in SP1
                rs, rb = rms_stats(xt, xb, sqr, rsr)
                P.enabled = en0 and 'H' in SP1
                hT, hb = hTr.next()
                for kc in range(KC):
                    P.op(alt(HENG), lambda E, kc=kc, hT=hT, xt=xt, rs=rs: E.tensor_tensor(out=hT[:, kc, :], in0=xt[:, kc, :], in1=rs[:], op=ALU.mult),
                         R=[xb, rb], W=[hb])
                P.enabled = en0
                for oc in range(WIN_F):
                    P.enabled = en0 and (('F' in SP1) if oc < 20 else ('FR' in SP1)) and oc in OCS
                    wf, wb = wfr.next()
                    dma("sp", wf[:], wFb[l][oc], [B_w], [wb])
                    pt, pb = psb.next()
                    for kc in range(KC):
                        P.op("pe", lambda E, pt=pt, wf=wf, hT=hT, kc=kc: E.matmul(pt[:, 0:ST], wf[:, kc * 128:(kc + 1) * 128], hT[:, kc, :], start=(kc == 0), stop=(kc == KC - 1)),
                             R=[wb, hb], W=[pb])
                    ot, ob = otr.next()
                    if oc < 20:
                        if 'E' in KF:
                            evac(pt[:, 0:ST], ot[:], [pb], [ob])
                        if 'D' not in KF:
                            continue
                        if oc < 6:
                            dma("pool", QA_d[oc * 128:(oc + 1) * 128, ts], ot[:], [ob], [B_QA])
                        elif oc < 12:
                            dma("pool", KA_d[(oc - 6) * 128:(oc - 5) * 128, HA + j * ST:HA + (j + 1) * ST], ot[:], [ob], [B_KA])
                        elif oc < 18:
                            dma("pool", QC_d[(oc - 12) * 128:(oc - 11) * 128, ts], ot[:], [ob], [B_QC])
                        else:
                            dma("pool", KCx_d[(oc - 18) * 128:(oc - 17) * 128, HC + j * ST:HC + (j + 1) * ST], ot[:], [ob], [B_KC])
                    else:
                        y32, yb = y32r.next()
                        P.op("act", lambda E, y32=y32, pt=pt: E.activation(out=y32[:], in_=pt[:, 0:ST], func=AF.Copy), R=[pb], W=[yb])
                        pz, pzb = psb.next()
                        P.op("pe", lambda E, pz=pz, y32=y32: E.matmul(pz[:, 0:ST], perm[:, :], y32[:], start=True, stop=True), R=[yb, B_perm], W=[pzb])
                        t1, t1b = t1r.next()
                        t2, t2b = t2r.next()
                        P.op("pool", lambda E, t1=t1, y32=y32: E.tensor_tensor(out=t1[:], in0=y32[:], in1=cFt[:], op=ALU.mult), R=[yb, B_cFt], W=[t1b])
                        P.op("dve", lambda E, t2=t2, pz=pz: E.tensor_tensor(out=t2[:], in0=pz[:, 0:ST], in1=sFt[:], op=ALU.mult), R=[pzb, B_sFt], W=[t2b])
                        P.op("dve", lambda E, t1=t1, t2=t2, ot=ot: E.tensor_tensor(out=ot[:], in0=t1[:], in1=t2[:], op=ALU.add), R=[t1b, t2b], W=[ob])
                        if oc < 23:
                            dma("pool", QR_d[(oc - 20) * 128:(oc - 19) * 128, ts], ot[:], [ob], [B_QR])
                        else:
                            dma("pool", KR_d[(oc - 23) * 128:(oc - 22) * 128, ts], ot[:], [ob], [B_KR])
                for cc in (0, 1, 2, 3, 4):
                    P.enabled = en0 and ('T%d' % cc in SP1)
                    wt, wtb = wtr.next()
                    dma("sp", wt[:], wTb[l][cc], [B_w], [wtb])
                    for s in range(SUB):
                        pt, pb = psb.next()
                        for kc in range(KC):
                            P.op("pe", lambda E, pt=pt, wt=wt, hT=hT, kc=kc, s=s: E.matmul(pt[:, :], hT[:, kc, s * 128:(s + 1) * 128], wt[:, kc * 512:(kc + 1) * 512], start=(kc == 0), stop=(kc == KC - 1)),
                                 R=[wtb, hb], W=[pb])
                        if cc == 0:
                            evac(pt[:].rearrange("p (h e) -> p h e", e=64), vaug_v[:, s, 0:8, 0:64], [pb], [B_vaug])
                        elif cc == 1:
                            e_ = alt(["act", "dve"])
                            evac(pt[:, 0:256].rearrange("p (h e) -> p h e", e=64), vaug_v[:, s, 8:12, 0:64], [pb], [B_vaug], eng=e_)
                            evac(pt[:, 256:512].rearrange("p (h e) -> p h e", e=64), vcaug_v[:, s, :, 0:64], [pb], [B_vcaug], eng=e_)
                        elif cc == 2:
                            evac(pt[:, :], vrt[:, s, :], [pb], [B_vrt])
                        elif cc == 3:
                            P.op("act", lambda E, pt=pt, s=s: E.activation(out=gst[:, s, :], in_=pt[:, :], func=AF.Silu), R=[pb], W=[B_gst])
                        else:
                            P.op("act", lambda E, pt=pt: E.activation(out=kraw[:], in_=pt[:, 0:256], func=AF.Copy), R=[pb], W=[B_kraw])
                            kv4 = kraw[:].rearrange("p (h t i) -> p h t i", h=8, t=2)
                            ko4 = kro[:].rearrange("p (h t i) -> p h t i", h=8, t=2)
                            c3 = cTt[:, s, :].rearrange("p (h i) -> p h i", h=8)
                            s3 = sTt[:, s, :].rearrange("p (h i) -> p h i", h=8)
                            ta3 = kta[:].rearrange("p (h i) -> p h i", h=8)
                            tb3 = ktb[:].rearrange("p (h i) -> p h i", h=8)
                            RR = [B_kraw, B_cTt, B_sTt]
                            P.op("pool", lambda E, kv4=kv4, c3=c3, ta3=ta3: E.tensor_tensor(out=ta3, in0=kv4[:, :, 0, :], in1=c3, op=ALU.mult), R=RR, W=[B_kta])
                            P.op("pool", lambda E, kv4=kv4, s3=s3, tb3=tb3: E.tensor_tensor(out=tb3, in0=kv4[:, :, 1, :], in1=s3, op=ALU.mult), R=RR, W=[B_ktb])
                            P.op("pool", lambda E, ko4=ko4, ta3=ta3, tb3=tb3: E.tensor_tensor(out=ko4[:, :, 0, :], in0=ta3, in1=tb3, op=ALU.subtract), R=[B_kta, B_ktb], W=[B_kro])
                            P.op("pool", lambda E, kv4=kv4, s3=s3, ta3=ta3: E.tensor_tensor(out=ta3, in0=kv4[:, :, 0, :], in1=s3, op=ALU.mult), R=RR + [B_kro], W=[B_kta])
                            P.op("pool", lambda E, kv4=kv4, c3=c3, tb3=tb3: E.tensor_tensor(out=tb3, in0=kv4[:, :, 1, :], in1=c3, op=ALU.mult), R=RR + [B_kro], W=[B_ktb])
                            P.op("pool", lambda E, ko4=ko4, ta3=ta3, tb3=tb3: E.tensor_tensor(out=ko4[:, :, 1, :], in0=ta3, in1=tb3, op=ALU.add), R=[B_kta, B_ktb], W=[B_kro])
                            for dr in range(2):
                                P.op("dve", lambda E, dr=dr: E.tensor_tensor(out=kd[:, dr, :], in0=kro[:], in1=kdfull[:, dr, :], op=ALU.mult), R=[B_kro, B_kdf], W=[B_kd])
                            P.enabled = en0 and ('KV' in SP1) and ('T4' in SP1)
                            for dr in range(2):
                                pk, pkb = psb.next()
                                for g in range(3):
                                    nh = 3 if g < 2 else 2
                                    P.op("pe", lambda E, pk=pk, dr=dr, g=g, nh=nh, s=s: E.matmul(pk[0:32 * nh, g * 192:g * 192 + 64 * nh],
                                                                                                 kd[:, dr, g * 96:g * 96 + 32 * nh], vrt[:, s, g * 192:g * 192 + 64 * nh], start=True, stop=True),
                                         R=[B_kd, B_vrt], W=[pkb])
                                e_ = alt(["act", "dve"])
                                for g in range(3):
                                    nh = 3 if g < 2 else 2
                                    base = g * 192
                                    for jj in range(nh):
                                        evac(pk[32 * jj:32 * jj + 32, base + 64 * jj:base + 64 * jj + 64], kvt[32 * jj:32 * jj + 32, dr * 3 + g, s, :], [pkb], [B_kvt], eng=e_)
                    if cc == 1:
                        KT1 = os.environ.get('KT1', 'A,C')
                        if 'A' in KT1:
                            dma("pool", VA_d[HA + j * ST:HA + (j + 1) * ST, :].rearrange("(s p) c -> p s c", p=128), vaug[:], [B_vaug], [B_VA])
                        if 'C' in KT1:
                            dma("pool", VC_d[HC + j * ST:HC + (j + 1) * ST, :].rearrange("(s p) c -> p s c", p=128), vcaug[:], [B_vcaug], [B_VC])
                    elif cc == 2:
                        dma("pool", VR_d[ts, :].rearrange("(s p) c -> p s c", p=128), vrt[:], [B_vrt], [B_VR])
                    elif cc == 3:
                        dma("pool", GS_d[ts, :].rearrange("(s p) c -> p s c", p=128), gst[:], [B_gst], [B_GS])
                    elif cc == 4:
                        dma("pool", KV_d[:, :, j * SUB:(j + 1) * SUB, :], kvt[:], [B_kvt], [B_KV])
                P.enabled = en0

    PKR = PK_ROWS

    def exchange_kv():
        secs = [
            (768, [(slice(0, 768), slice(0, 1024), KA_d[:, HA + T - 1024:HA + T], B_KA)],
                  [(KA_d[:, 0:HA], 0, slice(0, 768), slice(0, 1024), B_KA)]),
            (768, [(slice(0, 768), slice(0, 1024), KA_d[:, HA:HA + 1024], B_KA)],
                  [(KA_d[:, HA + T:TA], 1, slice(0, 768), slice(0, 1024), B_KA)]),
            (1024, [(slice(0, 1024), slice(0, 780), VA_d[HA + T - 1024:HA + T, :], B_VA)],
                   [(VA_d[0:HA, :], 0, slice(0, 1024), slice(0, 780), B_VA)]),
            (1024, [(slice(0, 1024), slice(0, 780), VA_d[HA:HA + 1024, :], B_VA)],
                   [(VA_d[HA + T:TA, :], 1, slice(0, 1024), slice(0, 780), B_VA)]),
            (768, [(slice(0, 256), slice(0, 128), KCx_d[:, HC + T - 128:HC + T], B_KC),
                   (slice(256, 512), slice(0, 128), KCx_d[:, HC:HC + 128], B_KC),
                   (slice(512, 640), slice(0, 260), VC_d[HC + T - 128:HC + T, :], B_VC),
                   (slice(640, 768), slice(0, 260), VC_d[HC:HC + 128, :], B_VC)],
                  [(KCx_d[:, 0:HC], 0, slice(0, 256), slice(0, 128), B_KC),
                   (KCx_d[:, HC + T:TCx], 1, slice(256, 512), slice(0, 128), B_KC),
                   (VC_d[0:HC, :], 0, slice(512, 640), slice(0, 260), B_VC),
                   (VC_d[HC + T:TCx, :], 1, slice(640, 768), slice(0, 260), B_VC)]),
        ]
        for i, (rows, ins_, outs_) in enumerate(secs):
            pi_, po_, bi_, bo_ = pins[i], pouts[i], B_pins[i], B_pouts[i]
            for (rs_, cs_, src, bsrc) in ins_:
                dma("pool", pi_[rs_, cs_], src, [bsrc], [bi_])
            P.op("pool", lambda E, pi_=pi_, po_=po_: E.collective_compute("AllGather", ALU.bypass, replica_groups=[[0, 1], [2, 3], [4, 5], [6, 7]],
                                                                      ins=[pi_.opt()], outs=[po_.opt()]), R=[bi_], W=[bo_], dma=True, inc=1)
            for (dst, slot, rs_, cs_, bdst) in outs_:
                r0 = slot * rows
                dma("pool", dst, po_[r0 + rs_.start:r0 + rs_.stop, cs_], [bo_], [bdst])

    def retention(l):
        with Scope():
            kva, B_kva = P.sb("kva", [96, 6, NT, 64], F32), Buf("kva")
            Rb, B_Rb = P.sb("Rb", [96, 6, NT, 64], BF16), Buf("Rbf")
            Rcur, B_Rcur = [P.sb("Rcur%d" % i, [96, 3, 64], F32) for i in range(2)], [Buf("Rcur%d" % i) for i in range(2)]
            Rin, B_Rin = P.sb("Rin", [96, 6, 64], F32), Buf("Rin")
            gc, B_gc = P.sb("gc", [128, 6], F32), Buf("gc")
            gpow, B_gpow = P.sb("gpow", [128, 6, NT], F32), Buf("gpow")
            qdtab, B_qdt = P.sb("qdtab", [128, 6, 128], F32), Buf("qdtab")
            DT, B_DT = P.sb("DT", [128, 8, 128], F32), Buf("DT")
            dtmp, B_dtmp = P.sb("dtmp", [128, 128], F32), Buf("dtmp")
            dma("sp", kva[:], KV_d[:, :, :, :], [B_KV], [B_kva])
            lcol = lambda g, dr: l * 6 + g * 2 + dr
            P.op("act", lambda E: E.activation(out=gc[:], in_=lgp[:, l * 6:l * 6 + 6], func=AF.Exp, scale=128.0), R=[B_lgp], W=[B_gc])
            lg128, B_lg128 = P.sb("lg128", [128, 6], F32), Buf("lg128")
            P.op("dve", lambda E: E.tensor_scalar(out=lg128[:], in0=lgp[:, l * 6:l * 6 + 6], scalar1=128.0, scalar2=None, op0=ALU.mult), R=[B_lgp], W=[B_lg128])
            for g in range(3):
                for dr in range(2):
                    c = g * 2 + dr
                    P.op("act", lambda E, c=c, dr=dr: E.activation(out=gpow[:, c, :], in_=nidx[:, dr * NT:(dr + 1) * NT], func=AF.Exp, scale=lg128[:, c:c + 1]),
                         R=[B_nidx, B_lg128], W=[B_gpow])
                    P.op("act", lambda E, c=c, dr=dr: E.activation(out=qdtab[:, c, :], in_=qexp[:, dr * 128:(dr + 1) * 128], func=AF.Exp, scale=lgp[:, l * 6 + c:l * 6 + c + 1]),
                         R=[B_qexp, B_lgp], W=[B_qdt])
            for h in range(8):
                P.op("act", lambda E, h=h: E.activation(out=DT[:, h, :], in_=rtab[:, 0:128], func=AF.Exp, scale=lg8[:, (l * 2) * 8 + h:(l * 2) * 8 + h + 1]), R=[B_rtab, B_lg8], W=[B_DT])
                P.op("dve", lambda E, h=h: E.tensor_tensor(out=DT[:, h, :], in0=DT[:, h, :], in1=rtab[:, 128:256], op=ALU.mult), R=[B_DT, B_rtab], W=[B_DT])
                P.op("act", lambda E, h=h: E.activation(out=dtmp[:], in_=rtab[:, 256:384], func=AF.Exp, scale=lg8[:, (l * 2 + 1) * 8 + h:(l * 2 + 1) * 8 + h + 1]), R=[B_rtab, B_lg8, B_DT], W=[B_dtmp])
                P.op("dve", lambda E, h=h: E.tensor_tensor(out=dtmp[:], in0=dtmp[:], in1=rtab[:, 384:512], op=ALU.mult), R=[B_dtmp, B_rtab], W=[B_dtmp])
                P.op("dve", lambda E, h=h: E.tensor_tensor(out=DT[:, h, :], in0=DT[:, h, :], in1=dtmp[:], op=ALU.add), R=[B_DT, B_dtmp], W=[B_DT])
                P.op("dve", lambda E, h=h: E.tensor_scalar(out=DT[:, h, :], in0=DT[:, h, :], scalar1=DKS, scalar2=None, op0=ALU.mult), R=[B_DT], W=[B_DT])
            for dr, eng in ((0, "dve"), (1, "dve")):
                Rc, Bc = Rcur[dr], B_Rcur[dr]
                P.op(eng, lambda E, Rc=Rc: E.memset(Rc[:], 0.0), W=[Bc])
                order = range(NT) if dr == 0 else range(NT - 1, -1, -1)
                for n in order:
                    P.op(eng, lambda E, Rc=Rc, n=n, dr=dr: E.tensor_copy(Rb[:, dr * 3:dr * 3 + 3, n, :], Rc[:]), R=[Bc], W=[B_Rb])
                    for g in range(3):
                        c = g * 2 + dr
                        P.op(eng, lambda E, Rc=Rc, n=n, g=g, c=c, dr=dr: E.scalar_tensor_tensor(out=Rc[:, g, :], in0=Rc[:, g, :], scalar=gc[0:96, c:c + 1], in1=kva[:, dr * 3 + g, n, :],
                                                                                               op0=ALU.mult, op1=ALU.add), R=[Bc, B_gc, B_kva], W=[Bc])
                dma("pool", sin_[:, dr * 192:(dr + 1) * 192], Rc[:].rearrange("p g e -> p (g e)"), [Bc], [B_sin])
            P.op("pool", lambda E: E.collective_compute("AllGather", ALU.bypass, replica_groups=[[0, 1], [2, 3], [4, 5], [6, 7]],
                                                        ins=[sin_.opt()], outs=[sout.opt()]), R=[B_sin], W=[B_sout], dma=True, inc=1)
            dma("sp", Rin[:, 0:3, :], sout[0:96, 0:192].rearrange("p (g e) -> p g e", g=3), [B_sout], [B_Rin])
            dma("sp", Rin[:, 3:6, :], sout[96:192, 192:384].rearrange("p (g e) -> p g e", g=3), [B_sout], [B_Rin])
            for dr in range(2):
                P.op("dve", lambda E, dr=dr: E.tensor_scalar(out=Rin[:, dr * 3:dr * 3 + 3, :], in0=Rin[:, dr * 3:dr * 3 + 3, :], scalar1=flags[0:96, 4 + dr:5 + dr], scalar2=None, op0=ALU.mult),
                     R=[B_Rin, B_flags], W=[B_Rin])
            for dr, eng in ((0, "dve"), (1, "dve")):
                for g in range(3):
                    c = g * 2 + dr
                    for n in range(NT):
                        P.op(eng, lambda E, dr=dr, g=g, c=c, n=n: E.scalar_tensor_tensor(out=Rb[:, dr * 3 + g, n, :], in0=Rin[:, dr * 3 + g, :], scalar=gpow[0:96, c, n:n + 1], in1=Rb[:, dr * 3 + g, n, :],
                                                                                        op0=ALU.mult, op1=ALU.add), R=[B_Rin, B_gpow, B_Rb], W=[B_Rb])
            KR_ = os.environ.get('KR', 'pre,out,cc')
            P.enabled = P.enabled and 'out' in KR_
            qrt, B_qrt = P.sb("qrt", [128, 3, ST], BF16), Buf("qrt")
            krt_, B_krt = P.sb("krt_", [128, 3, ST], BF16), Buf("krt_")
            vr_, B_vr = P.sb("vr_", [128, SUB, 512], BF16), Buf("vr_")
            gs_t, B_gs = P.sb("gs_t", [128, SUB, 512], BF16), Buf("gs_t")
            qd, B_qd = P.sb("qd", [128, 6, 128], BF16), Buf("qd")
            wts, B_wts = P.sb("wts", [128, 8, 128], BF16), Buf("wts")
            osb, B_osb = P.sb("osb", [128, 512], F32), Buf("osb")
            osq, B_osq = P.sb("osq", [128, 512], F32), Buf("osq")
            st8, B_st8 = P.sb("st8", [128, 4, 8], F32), Buf("st8")
            on, B_on = P.sb("on", [128, 512], F32), Buf("on")
            mixr, B_mixr = P.sb("mixr", [128, 512], BF16), Buf("mixr")
            mro, B_mro = P.sb("mro", [128, 4, ST], BF16), Buf("mro")
            en_r = P.enabled
            KRO = os.environ.get('KRO', 'qd,S,O,G,TR').split(',')
            for j in range(NS):
                ts = slice(j * ST, (j + 1) * ST)
                dma("sp", qrt[:], QR_d[:, ts].rearrange("(g p) t -> p g t", p=128), [B_QR], [B_qrt])
                dma("sp", krt_[:], KR_d[:, ts].rearrange("(g p) t -> p g t", p=128), [B_KR], [B_krt])
                dma("sp", vr_[:], VR_d[ts, :].rearrange("(s p) c -> p s c", p=128), [B_VR], [B_vr])
                dma("sp", gs_t[:], GS_d[ts, :].rearrange("(s p) c -> p s c", p=128), [B_GS], [B_gs])
                for s in range(SUB):
                    n = j * SUB + s
                    tt = slice(s * 128, (s + 1) * 128)
                    P.enabled = en_r and 'qd' in KRO
                    for c in range(6):
                        P.op("pool", lambda E, c=c, tt=tt: E.tensor_tensor(out=qd[:, c, :], in0=qrt[:, c // 2, tt], in1=qdtab[:, c, :], op=ALU.mult), R=[B_qrt, B_qdt], W=[B_qd])
                    P.enabled = en_r and 'S' in KRO
                    pts = [psb.next() for _ in range(3)]
                    for h in range(8):
                        g, jj = divmod(h, 3)
                        pt, pb = pts[jj]
                        P.op("pe", lambda E, pt=pt, g=g, jj=jj, tt=tt: E.matmul(pt[:, g * 128:(g + 1) * 128], krt_[32 * jj:32 * jj + 32, g, tt], qrt[32 * jj:32 * jj + 32, g, tt], start=True, stop=True),
                             R=[B_qrt, B_krt], W=[pb])
                    for jj in range(3):
                        pt, pb = pts[jj]
                        ng = 3 if jj < 2 else 2
                        P.op("dve", lambda E, pt=pt, jj=jj, ng=ng: E.tensor_tensor(out=wts[:, jj:8:3, :], in0=pt[:, 0:ng * 128].rearrange("p (h t) -> p h t", h=ng), in1=DT[:, jj:8:3, :], op=ALU.mult),
                             R=[pb, B_DT], W=[B_wts])
                    P.enabled = en_r and 'O' in KRO
                    po, pob = psb.next()
                    for h in range(8):
                        g, jj = divmod(h, 3)
                        pr = slice(32 * jj, 32 * jj + 32)
                        P.op("pe", lambda E, po=po, h=h, s=s: E.matmul(po[:, h * 64:(h + 1) * 64], wts[:, h, :], vr_[:, s, h * 64:(h + 1) * 64], start=True, stop=False), R=[B_wts, B_vr], W=[pob])
                        P.op("pe", lambda E, po=po, h=h, g=g, pr=pr, n=n: E.matmul(po[:, h * 64:(h + 1) * 64], qd[pr, g * 2, :], Rb[pr, g, n, :], start=False, stop=False), R=[B_qd, B_Rb], W=[pob])
                        P.op("pe", lambda E, po=po, h=h, g=g, pr=pr, n=n: E.matmul(po[:, h * 64:(h + 1) * 64], qd[pr, g * 2 + 1, :], Rb[pr, 3 + g, n, :], start=False, stop=True), R=[B_qd, B_Rb], W=[pob])
                    P.enabled = en_r and 'G' in KRO
                    P.op("act", lambda E, po=po: E.activation(out=osb[:], in_=po[:], func=AF.Copy), R=[pob], W=[B_osb])
                    P.op("act", lambda E: E.activation(out=osq[:], in_=osb[:], func=AF.Square), R=[B_osb], W=[B_osq])
                    P.op("dve", lambda E: E.tensor_reduce(out=st8[:, 0, :], in_=osb[:].rearrange("p (h e) -> p h e", h=8), axis=AX.X, op=ALU.add), R=[B_osb], W=[B_st8])
                    P.op("dve", lambda E: E.tensor_reduce(out=st8[:, 1, :], in_=osq[:].rearrange("p (h e) -> p h e", h=8), axis=AX.X, op=ALU.add), R=[B_osq], W=[B_st8])
                    P.op("dve", lambda E: E.tensor_scalar(out=st8[:, 0, :], in0=st8[:, 0, :], scalar1=1.0 / 64, scalar2=None, op0=ALU.mult), R=[B_st8], W=[B_st8])
                    P.op("dve", lambda E: E.tensor_tensor(out=st8[:, 2, :], in0=st8[:, 0, :], in1=st8[:, 0, :], op=ALU.mult), R=[B_st8], W=[B_st8])
                    P.op("dve", lambda E: E.scalar_tensor_tensor(out=st8[:, 3, :], in0=st8[:, 1, :], scalar=1.0 / 64, in1=st8[:, 2, :], op0=ALU.mult, op1=ALU.subtract), R=[B_st8], W=[B_st8])
                    P.op("dve", lambda E: E.tensor_scalar(out=st8[:, 3, :], in0=st8[:, 3, :], scalar1=1.0, scalar2=GN_EPS, op0=ALU.mult, op1=ALU.add), R=[B_st8], W=[B_st8])
                    P.op("act", lambda E: E.activation(out=st8[:, 3, :], in_=st8[:, 3, :], func=AF.Sqrt), R=[B_st8], W=[B_st8])
                    P.op("dve", lambda E: E.reciprocal(st8[:, 3, :], st8[:, 3, :]), R=[B_st8], W=[B_st8])
                    for h in range(8):
                        P.op("pool", lambda E, h=h: E.tensor_scalar(out=on[:, h * 64:(h + 1) * 64], in0=osb[:, h * 64:(h + 1) * 64], scalar1=st8[:, 0, h:h + 1], scalar2=st8[:, 3, h:h + 1],
                                                                    op0=ALU.subtract, op1=ALU.mult), R=[B_osb, B_st8], W=[B_on])
                    P.op("pool", lambda E, s=s: E.tensor_tensor(out=mixr[:], in0=on[:], in1=gs_t[:, s, :], op=ALU.mult), R=[B_on, B_gs], W=[B_mixr])
                    P.enabled = en_r and 'TR' in KRO
                    ptb, ptbb = psbf.next()
                    for q in range(4):
                        P.op("pe", lambda E, ptb=ptb, q=q: E.transpose(ptb[:, q * 128:(q + 1) * 128], mixr[:, q * 128:(q + 1) * 128], identb[:]), R=[B_mixr, B_identb], W=[ptbb])
                    evac(ptb[:, 0:512].rearrange("p (q t) -> p q t", q=4), mro[:, :, tt], [ptbb], [B_mro])
                dma("pool", mixT_d[768:1280, ts].rearrange("(q p) t -> p q t", p=128), mro[:], [B_mro], [B_mix])
                P.enabled = en_r

    def attention(l):
        with Scope():
            qt, B_qt = P.sb("qt", [64, T], BF16), Buf("qt")
            kt, B_kt = P.sb("kt", [64, TA], BF16), Buf("kt")
            acc, B_acc = P.sb("acc", [65, T], F32), Buf("acc")
            mo, B_mo = P.sb("mo", [64, T], BF16), Buf("mo")
            racc, B_racc = P.sb("racc", [65, 512], F32), Buf("racc")
            ebr = rot("ebt", 2, [128, 384], BF16)
            vtr = rot("vt", 2, [128, 80, 65], BF16)
            ptr_ = rot("pt", 4, [128, 384], BF16)
            pt2r = rot("pt2", 6, [128, 384], BF16)
            pend = []
            LA = 3
            sc_rot = Rot(psb.items[0:4])
            zb, B_zb = P.sb("zb", [1, 512], BF16), Buf("zb")
            P.op("pool", lambda E: E.memset(zb[:], 0.0), W=[B_zb])
            gr_rot = Rot(psb.items[4:7])
            P.op("pool", lambda E: E.memset(racc[:], 0.0), W=[B_racc])
            jobs = []
            for h in range(12):
                jobs.append((QA_d[h * 64:(h + 1) * 64, :], KA_d[h * 64:(h + 1) * 64, :], TA, VA_d, 780, h * 65, HA, (0, 1, 2), h, (0, 1), None, h * 64, B_QA, B_KA, B_VA))
            for h in range(12):
                g = h // 3
                jobs.append((QC_d[h * 64:(h + 1) * 64, :], KCx_d[g * 64:(g + 1) * 64, :], TCx, VC_d, 260, g * 65, HC, (3,), h, (2, 3), l * 12 + h, 1280 + h * 64, B_QC, B_KC, B_VC))
            for (Qs, Ks, Text, Vd, rowlen, vcol0, H, vars_, h, fcols, sinkc, mrow, BQ, BK, BV) in jobs:
                dma("sp", qt[:], Qs, [BQ], [B_qt])
                dma("sp", kt[:, 0:Text], Ks, [BK], [B_kt])
                P.op("pool", lambda E: E.memset(acc[:], 0.0), W=[B_acc])
                for v in vars_:
                    dil, koff, qoff, NQ, rad = VARIANTS[v]
                    L = T // dil
                    nj = L // 128 + (1 if koff == -64 else 2)
                    ebt, ebb = ebr.next()
                    dma("sp", ebt[:, 0:NQ], EB_d[v, h, :, 0:NQ], [B_EB], [ebb])
                    for r in range(dil):
                        vt, vb = vtr.next()
                        off = (koff * dil + r + H) * rowlen + vcol0
                        nchunk = 16
                        for j0 in range(0, nj, nchunk):
                            j1 = min(nj, j0 + nchunk)
                            src = bass.AP(Vd.tensor, off + j0 * 128 * dil * rowlen, [[dil * rowlen, 128], [128 * dil * rowlen, j1 - j0], [1, 65]])
                            dma("sp", vt[:, j0:j1, :], src, [BV], [vb])
                        groups = {}
                        for j in range(nj):
                            k0 = 128 * j + koff
                            q_lo = max(0, 128 * j + qoff)
                            q_hi = min(L, 128 * j + qoff + NQ)
                            nq = q_hi - q_lo
                            if nq <= 0:
                                continue
                            qq0 = q_lo - (128 * j + qoff)
                            ks = k0 * dil + r + H
                            qs = q_lo * dil + r
                            if k0 < 0:
                                bcol, Bb = flags[:, fcols[0]:fcols[0] + 1], B_flags
                            elif k0 + 128 > L:
                                bcol, Bb = flags[:, fcols[1]:fcols[1] + 1], B_flags
                            else:
                                bcol, Bb = zcol[:, 0:1], B_zcol
                            ps_, psb_ = sc_rot.next()
                            P.op("pe", lambda E, ps_=ps_, ks=ks, qs=qs, nq=nq, dil=dil: E.matmul(ps_[:, 0:nq], kt[0:64, ks:ks + 127 * dil + 1:dil], qt[0:64, qs:qs + (nq - 1) * dil + 1:dil], start=True, stop=True),
                                 R=[B_kt, B_qt], W=[psb_])
                            p1, p1b = ptr_.next()
                            P.op("act", lambda E, ps_=ps_, p1=p1, nq=nq, bcol=bcol: E.activation(out=p1[:, 0:nq], in_=ps_[:, 0:nq], func=AF.Exp, bias=bcol, scale=0.125), R=[psb_, Bb], W=[p1b])
                            p2, p2b = pt2r.next()
                            P.op("pool", lambda E, p1=p1, p2=p2, nq=nq, qq0=qq0, ebt=ebt: E.tensor_tensor(out=p2[:, 0:nq], in0=p1[:, 0:nq], in1=ebt[:, qq0:qq0 + nq], op=ALU.mult), R=[p1b, ebb], W=[p2b])

                            def st2(vt=vt, vb=vb, j=j, p2=p2, p2b=p2b, qq0=qq0, dil=dil, r=r, NB=NQ // 128, nblk=L // 128, groups=groups):
                                for i in range(NB):
                                    b_ = j - (NB - 1) + i
                                    if b_ < 0 or b_ >= nblk:
                                        continue
                                    g_ = b_ // 4
                                    if g_ not in groups:
                                        groups[g_] = gr_rot.next()
                                        po, pob = groups[g_]
                                        P.op("pe", lambda E, po=po: E.matmul(po[0:65, 0:512], zb[0:1, 0:65], zb[0:1, 0:512], start=True, stop=False), R=[B_zb], W=[pob])
                                    po, pob = groups[g_]
                                    c0 = 128 * i - qq0
                                    col = (b_ % 4) * 128
                                    P.op("pe", lambda E, po=po, vt=vt, j=j, p2=p2, c0=c0, col=col, i=i, NB=NB: E.matmul(po[0:65, col:col + 128], vt[:, j, :], p2[:, c0:c0 + 128], start=False, stop=True),
                                         R=[vb, p2b], W=[pob])
                                    if i == 0 and (b_ % 4 == 3 or b_ == nblk - 1):
                                        n_ = (b_ % 4 + 1) * 128
                                        qs_ = (512 * g_) * dil + r
                                        P.op("dve", lambda E, po=po, qs_=qs_, n_=n_, dil=dil: E.tensor_tensor(out=acc[:, qs_:qs_ + (n_ - 1) * dil + 1:dil], in0=acc[:, qs_:qs_ + (n_ - 1) * dil + 1:dil], in1=po[0:65, 0:n_], op=ALU.add),
                                             R=[B_acc, pob], W=[B_acc])
                                        del groups[g_]
                            pend.append(st2)
                            while len(pend) > LA:
                                pend.pop(0)()
                while pend:
                    pend.pop(0)()
                if sinkc is not None:
                    P.op("dve", lambda E, sinkc=sinkc: E.tensor_scalar(out=acc[64:65, :], in0=acc[64:65, :], scalar1=sinke[64:65, sinkc:sinkc + 1], scalar2=None, op0=ALU.add), R=[B_acc, B_sinke], W=[B_acc])
                for c in range(T // 512):
                    cs = slice(c * 512, (c + 1) * 512)
                    P.op("dve", lambda E, cs=cs: E.reciprocal(racc[64:65, :], acc[64:65, cs]), R=[B_acc], W=[B_racc])
                    pb_, pbb_ = gr_rot.next()
                    P.op("pe", lambda E, pb_=pb_: E.matmul(pb_[0:64, :], sel[0:65, :], racc[0:65, :], start=True, stop=True), R=[B_racc, B_sel], W=[pbb_])
                    P.op("dve", lambda E, pb_=pb_, cs=cs: E.tensor_tensor(out=mo[:, cs], in0=acc[0:64, cs], in1=pb_[0:64, :], op=ALU.mult), R=[B_acc, pbb_], W=[B_mo])
                dma("pool", mixT_d[mrow:mrow + 64, :], mo[:], [B_mo], [B_mix])

    mixT_v = mixT_d.rearrange("(k p) t -> p k t", p=128)

    def P3(l, last):
        with Scope():
            xt, xb = P.sb("xt3", [128, KC, ST], F32), Buf("xt3")
            mt, mb = P.sb("mt3", [128, KC, ST], BF16), Buf("mt3")
            h2, h2b = P.sb("h2", [128, KC, ST], BF16), Buf("h2")
            actT, actb = P.sb("actT", [128, FC, ST], BF16), Buf("actT")
            wfr = rot("wf3", 4, [128, KC * 128], BF16)
            wdr = rot("wd3", 2, [128, FC * 128], BF16)
            sqr = rot("sq3", 2, [128, ST], BF16)
            rsr = rot("rs3", 1, [128, ST], F32)
            sgr = rot("sg3", 2, [128, ST], F32)
            yor = rot("yo3", 2, [128, D], F32)
            for j in range(NS):
                ts = slice(j * ST, (j + 1) * ST)
                dma("sp", xt[:], xT_v[:, :, ts], [B_xT[j]], [xb])
                dma("sp", mt[:], mixT_v[:, :, ts], [B_mix], [mb])
                for oc in range(16):
                    wf, wb = wfr.next()
                    dma("sp", wf[:], wOb[l][oc], [B_w], [wb])
                    pt, pb = psb.next()
                    for kc in range(KC):
                        P.op("pe", lambda E, pt=pt, wf=wf, kc=kc: E.matmul(pt[:, 0:ST], wf[:, kc * 128:(kc + 1) * 128], mt[:, kc, :], start=(kc == 0), stop=(kc == KC - 1)), R=[wb, mb], W=[pb])
                    P.op("dve", lambda E, pt=pt, oc=oc: E.tensor_tensor(out=xt[:, oc, :], in0=xt[:, oc, :], in1=pt[:, 0:ST], op=ALU.add), R=[xb, pb], W=[xb])
                rs, rb = rms_stats(xt, xb, sqr, rsr)
                for kc in range(KC):
                    P.op(alt(["dve", "pool"]), lambda E, kc=kc, rs=rs: E.tensor_tensor(out=h2[:, kc, :], in0=xt[:, kc, :], in1=rs[:], op=ALU.mult), R=[xb, rb], W=[h2b])
                for fc in range(FC):
                    wg, wgb = wfr.next()
                    dma("sp", wg[:], wGb[l][fc], [B_w], [wgb])
                    wu, wub = wfr.next()
                    dma("sp", wu[:], wUb[l][fc], [B_w], [wub])
                    pg, pgb = psb.next()
                    for kc in range(KC):
                        P.op("pe", lambda E, pg=pg, wg=wg, kc=kc: E.matmul(pg[:, 0:ST], wg[:, kc * 128:(kc + 1) * 128], h2[:, kc, :], start=(kc == 0), stop=(kc == KC - 1)), R=[wgb, h2b], W=[pgb])
                    pu, pub = psb.next()
                    for kc in range(KC):
                        P.op("pe", lambda E, pu=pu, wu=wu, kc=kc: E.matmul(pu[:, 0:ST], wu[:, kc * 128:(kc + 1) * 128], h2[:, kc, :], start=(kc == 0), stop=(kc == KC - 1)), R=[wub, h2b], W=[pub])
                    sg, sgb = sgr.next()
                    P.op("act", lambda E, pg=pg, sg=sg: E.activation(out=sg[:], in_=pg[:, 0:ST], func=AF.Silu), R=[pgb], W=[sgb])
                    P.op("dve", lambda E, pu=pu, sg=sg, fc=fc: E.tensor_tensor(out=actT[:, fc, :], in0=sg[:], in1=pu[:, 0:ST], op=ALU.mult), R=[sgb, pub], W=[actb])
                for oc in range(16):
                    wd, wdb = wdr.next()
                    dma("sp", wd[:], wDb[l][oc], [B_w], [wdb])
                    pt, pb = psb.next()
                    for fc in range(FC):
                        P.op("pe", lambda E, pt=pt, wd=wd, fc=fc: E.matmul(pt[:, 0:ST], wd[:, fc * 128:(fc + 1) * 128], actT[:, fc, :], start=(fc == 0), stop=(fc == FC - 1)), R=[wdb, actb], W=[pb])
                    P.op("dve", lambda E, pt=pt, oc=oc: E.tensor_tensor(out=xt[:, oc, :], in0=xt[:, oc, :], in1=pt[:, 0:ST], op=ALU.add), R=[xb, pb], W=[xb])
                if not last:
                    dma("pool", xT_v[:, :, ts], xt[:], [xb], [B_xT[j]])
                else:
                    rs, rb = rms_stats(xt, xb, sqr, rsr)
                    for kc in range(KC):
                        P.op("dve", lambda E, kc=kc, rs=rs: E.scalar_tensor_tensor(out=xt[:, kc, :], in0=xt[:, kc, :], scalar=gtab[:, 4 * KC + kc:4 * KC + kc + 1], in1=rs[:],
                                                                                                 op0=ALU.mult, op1=ALU.mult), R=[xb, rb, B_gtab], W=[xb])
                    for s in range(SUB):
                        yo, yob = yor.next()
                        for g in range(4):
                            pt, pb = psb.next()
                            for q in range(4):
                                kc = 4 * g + q
                                P.op("pe", lambda E, pt=pt, q=q, kc=kc, s=s: E.transpose(pt[:, q * 128:(q + 1) * 128], xt[:, kc, s * 128:(s + 1) * 128], ident[:]), R=[xb, B_ident], W=[pb])
                            evac(pt[:, :], yo[:, g * 512:(g + 1) * 512], [pb], [yob])
                        dma("pool", y_out[j * ST + s * 128:j * ST + (s + 1) * 128, :], yo[:], [yob], [B_y])

    NLR = int(os.environ.get('KNL', NL))
    for l in range(NLR):
        P.enabled = 'P1' in PH
        P1(l)
        P.enabled = 'X' in PH
        exchange_kv()
        P.enabled = 'R' in PH
        retention(l)
        P.enabled = 'A' in PH
        attention(l)
        P.enabled = 'P3' in PH
        P3(l, l == NL - 1)
    P.enabled = True
    P.barrier()
    P.emit()
    return nc, es


def _fblocks(W, cols_list):
    K = W.shape[0]
    kc = K // 128
    out = np.zeros((len(cols_list), 128, kc * 128), np.float32)
    Wr = W.reshape(kc, 128, W.shape[1])
    for i, cols in enumerate(cols_list):
        cols = np.asarray(cols)
        blk = np.zeros((kc, 128, 128), np.float32)
        ok = cols >= 0
        blk[:, :, ok] = Wr[:, :, cols[ok]]
        out[i] = blk.transpose(1, 0, 2).reshape(128, kc * 128)
    return out


def _tblocks(W, cols):
    K = W.shape[0]
    kc = K // 128
    cols = np.asarray(cols)
    n = len(cols) // 512
    Wc = np.zeros((K, len(cols)), np.float32)
    ok = cols >= 0
    Wc[:, ok] = W[:, cols[ok]]
    return np.ascontiguousarray(Wc.reshape(kc, 128, n, 512).transpose(2, 1, 0, 3)).reshape(n, 128, kc * 512)


_CACHE = {}


def kernel(x_prompt, x_sample, rel_bias, norm1_g, w_in, ret_decay_fwd, ret_decay_bwd, attn_sink,
           w_out, norm2_g, w_gate, w_up, w_down, final_norm_g):
    f = lambda a: np.asarray(a, dtype=np.float32)
    x_prompt, x_sample = f(x_prompt), f(x_sample)
    T = x_prompt.shape[1]
    assert x_prompt.shape[0] == 4 and x_sample.shape[0] == 2 and x_sample.shape[1] == 2 * T
    if T not in _CACHE:
        _CACHE[T] = build(T)
    nc, es = _CACHE[T]
    w_in, w_out, w_gate, w_up, w_down = f(w_in), f(w_out), f(w_gate), f(w_up), f(w_down)
    ar = np.arange
    common = {}
    for l in range(NL):
        fl = [ar(oc * 128, (oc + 1) * 128) for oc in range(6)]
        fl += [768 + ar(oc * 128, (oc + 1) * 128) for oc in range(6)]
        fl += [3840 + ar(oc * 128, (oc + 1) * 128) for oc in range(6)]
        fl += [4608 + ar(oc * 128, (oc + 1) * 128) for oc in range(2)]
        for base in (2304, 2560):
            for g in range(3):
                cols = np.full(128, -1)
                nh = 3 if g < 2 else 2
                cols[:32 * nh] = base + g * 96 + ar(32 * nh)
                fl.append(cols)
        common["wFin%d" % l] = _fblocks(w_in[l], fl)
        tcols = np.concatenate([1536 + ar(768), 4864 + ar(256), 2816 + ar(512), 3328 + ar(512), 2560 + ar(256), np.full(256, -1)])
        common["wTin%d" % l] = _tblocks(w_in[l], tcols)
        common["wOut%d" % l] = _fblocks(w_out[l], [ar(oc * 128, (oc + 1) * 128) for oc in range(16)])
        common["wGa%d" % l] = _fblocks(w_gate[l], [ar(oc * 128, (oc + 1) * 128) for oc in range(FC)])
        common["wUp%d" % l] = _fblocks(w_up[l], [ar(oc * 128, (oc + 1) * 128) for oc in range(FC)])
        common["wDn%d" % l] = _fblocks(w_down[l], [ar(oc * 128, (oc + 1) * 128) for oc in range(16)])
    g1, g2, gf = f(norm1_g), f(norm2_g), f(final_norm_g)
    gt = [g1[0], g1[1], g2[0], g2[1], gf]
    common["gtab"] = np.concatenate([g.reshape(KC, 128).T for g in gt], axis=1).astype(np.float32)
    df, db = f(ret_decay_fwd), f(ret_decay_bwd)
    dec = np.ones((128, NL * 6), np.float32) * 8.0
    dec8 = np.zeros((128, NL * 16), np.float32)
    for l in range(NL):
        for g in range(3):
            for dr, dd in enumerate((df, db)):
                for p in range(96):
                    h = g * 3 + p // 32
                    if h < 8:
                        dec[p, l * 6 + g * 2 + dr] = dd[l, h]
        for dr, dd in enumerate((df, db)):
            dec8[:, (l * 2 + dr) * 8:(l * 2 + dr) * 8 + 8] = dd[l][None, :]
    common["dec"] = dec
    common["dec8"] = dec8
    common["sink"] = np.tile(f(attn_sink).reshape(1, NL * 12), (128, 1))
    common["relb"] = f(rel_bias)
    common.update(host_consts(T))
    if SMALLW:
        for k_ in list(common):
            if k_[:2] in ('wF', 'wT', 'wO', 'wG', 'wU', 'wD'):
                common[k_] = np.ascontiguousarray(common[k_][:1])
    in_maps = []
    for c in range(8):
        m = dict(common)
        if c < 4:
            m["x"] = np.ascontiguousarray(x_prompt[c])
            lc, rc, pos0 = 0.0, 0.0, 0
        else:
            sq, hf = divmod(c - 4, 2)
            m["x"] = np.ascontiguousarray(x_sample[sq, hf * T:(hf + 1) * T])
            lc, rc, pos0 = float(hf == 1), float(hf == 0), hf * T
        fl_ = np.zeros((128, 8), np.float32)
        nl, nr = (0.0 if lc else NEGB), (0.0 if rc else NEGB)
        fl_[0:64, 0] = nl
        fl_[64:128, 1] = nr
        fl_[:, 2] = nl
        fl_[:, 3] = nr
        fl_[:, 4] = lc
        fl_[:, 5] = rc
        m["flags"] = fl_
        cF, sF, cT, sT = rope_tables(pos0, T)
        m["cF"], m["sF"], m["cT"], m["sT"] = cF, sF, cT, sT
        in_maps.append(m)
    res = run_bass_kernel_spmd(nc, in_maps, core_ids=list(range(8)))
    if os.environ.get('KDBG'):
        kernel.dbg = res.results
    ys = [np.asarray(r["y"], dtype=np.float32) for r in res.results]
    y_prompt = np.stack(ys[0:4], axis=0)
    y_sample = np.stack([np.concatenate(ys[4:6], axis=0), np.concatenate(ys[6:8], axis=0)], axis=0)
    return (y_prompt, y_sample)
```

```python
import math
import os
from contextlib import ExitStack
import numpy as np
import ml_dtypes
import concourse.bass as bass
import concourse.mybir as mybir
from concourse.bass_utils import run_bass_kernel_spmd

F32 = mybir.dt.float32
BF16 = mybir.dt.bfloat16
AF = mybir.ActivationFunctionType
ALU = mybir.AluOpType
AX = mybir.AxisListType

D = 2048
KC = 16
DFF = 5632
FC = 44
NL = 2
ST = 512
HA = 1024
HC = 128
EPS = 1e-6
GN_EPS = 1e-5
NEGB = -30000.0
DILS = (1, 4, 16)


SCOPE = [None]


class SemSlot:
    __slots__ = ("sem", "cnt")


class Buf:
    __slots__ = ("name", "writers", "readers", "sem", "scope", "war")

    def __init__(self, name):
        self.name = name
        self.writers = {}
        self.readers = {}
        self.sem = None
        self.war = set()
        self.scope = SCOPE[0]


class Op:
    __slots__ = ("eng", "fn", "deps", "flag", "done", "dma", "inc")


class Prog:
    def __init__(self, nc, es):
        self.nc = nc
        self.es = es
        self.ges = es
        self.ops = []
        self.engs = {"pe": nc.tensor, "act": nc.scalar, "dve": nc.vector, "pool": nc.gpsimd, "sp": nc.sync}
        self.esem = {e: es.enter_context(nc.semaphore("s_" + e)) for e in ("pe", "act", "dve", "pool")}
        self.nsem = 0
        self.last = {}
        self.free_sems = []
        self.enabled = True

    def sb(self, name, shape, dt):
        self.nsem += 1
        return self.es.enter_context(self.nc.sbuf_tensor("s%d_%s" % (self.nsem, name), list(shape), dt))

    def ps(self, name, shape, dt=F32):
        self.nsem += 1
        return self.es.enter_context(self.nc.psum_tensor("p%d_%s" % (self.nsem, name), list(shape), dt))

    def op(self, eng, fn, R=(), W=(), dma=False, inc=16):
        if not self.enabled:
            return None
        o = Op()
        o.eng, o.fn, o.flag, o.dma, o.inc, o.done = eng, fn, False, dma, inc, None
        deps = set()
        for b in R:
            deps.update(b.writers.values())
        for b in W:
            if b.readers:
                b.war = set(b.readers.values()) | set(b.writers.values())
                b.writers = {}
                b.readers = {}
            deps.update(b.war)
        if dma:
            b0 = W[0]
            if b0.sem is None:
                if self.free_sems:
                    b0.sem = self.free_sems.pop()
                else:
                    sl = SemSlot()
                    sl.sem = self.ges.enter_context(self.nc.semaphore("d%d" % self.nsem))
                    sl.cnt = 0
                    self.nsem += 1
                    b0.sem = sl
                if b0.scope is not None:
                    b0.scope.append(b0.sem)
            b0.sem.cnt += inc
            o.done = (b0.sem.sem, b0.sem.cnt)
            key = id(b0.sem.sem)
        else:
            key = eng
        for b in R:
            b.readers[key] = o
        for b in W:
            b.writers[key] = o
        deps.discard(o)
        o.deps = deps
        self.ops.append(o)
        self.last[key] = o
        return o

    def barrier(self):
        if not self.enabled:
            return
        allops = list(self.last.values())
        for e in ("pe", "act", "dve", "pool", "sp"):
            o = Op()
            o.eng, o.fn, o.flag, o.dma, o.inc, o.done = e, (lambda E: None), False, False, 16, None
            o.deps = set(allops)
            self.ops.append(o)

    def emit(self):
        for o in self.ops:
            for d in o.deps:
                d.flag = True
        cnt = {e: 0 for e in self.esem}
        for o in self.ops:
            if (not o.dma) and o.flag:
                cnt[o.eng] += 1
                o.done = (self.esem[o.eng], cnt[o.eng])
        waited = {e: {} for e in self.engs}
        for o in self.ops:
            E = self.engs[o.eng]
            need = {}
            for d in o.deps:
                if (not d.dma) and d.eng == "pe" and o.eng == "pe":
                    continue
                sem, val = d.done
                k = id(sem)
                if k not in need or need[k][1] < val:
                    need[k] = (sem, val)
            wd = waited[o.eng]
            for k, (sem, val) in need.items():
                if wd.get(k, 0) >= val:
                    continue
                E.wait_ge(sem, val)
                wd[k] = val
            ins = o.fn(E)
            if ins is None:
                continue
            if o.dma:
                ins.then_inc(o.done[0], o.inc)
            elif o.flag:
                ins.then_inc(o.done[0], 1)


class Rot:
    def __init__(self, items):
        self.items = items
        self.i = 0

    def next(self):
        it = self.items[self.i % len(self.items)]
        self.i += 1
        return it


def t5_bucket(rel):
    nb = 16
    max_exact = 8
    ret = np.where(rel > 0, nb, 0)
    n = np.abs(rel)
    nf = np.maximum(n, 1).astype(np.float32)
    large = max_exact + (np.log(nf / max_exact) / math.log(1024 / max_exact) * (nb - max_exact)).astype(np.int32)
    large = np.minimum(large, nb - 1)
    return (ret + np.where(n < max_exact, n, large)).astype(np.int32)


VARIANTS = [(1, -64, -128, 256, 64), (4, -64, -128, 256, 64), (16, -64, -128, 256, 64), (1, -128, -256, 384, 128)]
NM = 512


def host_consts(T):
    c = {}
    oh = np.zeros((32, 4, NM), np.float32)
    masks = np.zeros((4, 128, 384), np.float32)
    for v, (dil, koff, qoff, NQ, rad) in enumerate(VARIANTS):
        m = np.arange(NM)
        off = (127 - m) + (koff - qoff)
        b = t5_bucket(off * dil)
        oh[b, v, m] = 1.0
        kk = np.arange(128)[:, None]
        qq = np.arange(NQ)[None, :]
        o2 = kk - qq + (koff - qoff)
        masks[v, :, :NQ] = (np.abs(o2) <= rad).astype(np.float32)
    c["oh"] = oh.reshape(32, 4 * NM)
    c["amask"] = np.ascontiguousarray(masks.transpose(1, 0, 2)).reshape(128, 4 * 384)
    c["ident"] = np.eye(128, dtype=np.float32)
    c["jrev"] = np.ascontiguousarray(np.eye(128, dtype=np.float32)[::-1])
    perm = np.zeros((128, 128), np.float32)
    for p in range(128):
        h, i = divmod(p, 32)
        perm[32 * h + (i + 16) % 32, p] = 1.0
    c["perm"] = perm
    sel = np.zeros((128, 64), np.float32)
    sel[64, :] = 1.0
    c["sel"] = sel
    s = np.arange(128)[:, None]
    t = np.arange(128)[None, :]
    rt = np.zeros((128, 4 * 128 + 4), np.float32)
    rt[:, 0:128] = np.maximum(t - s, 0)
    rt[:, 128:256] = (t >= s)
    rt[:, 256:384] = np.maximum(s - t, 0)
    rt[:, 384:512] = (s > t)
    rt[:, 512] = 127 - np.arange(128)
    rt[:, 513] = np.arange(128)
    c["rtab"] = rt
    qe = np.zeros((128, 256), np.float32)
    qe[:, 0:128] = np.arange(128)[None, :] + 1.0
    qe[:, 128:256] = 128.0 - np.arange(128)[None, :]
    c["qexp"] = qe
    nn = np.arange(T // 128, dtype=np.float32)
    c["nidx"] = np.tile(np.concatenate([nn, nn[::-1]])[None, :], (128, 1))
    return c


def rope_tables(pos0, T):
    half = 16
    freqs = (10000.0 ** (-np.arange(half, dtype=np.float32) / half)).astype(np.float32)
    pos = (pos0 + np.arange(T)).astype(np.float32)
    ang = (pos[:, None] * freqs[None]).astype(np.float32)
    cos = np.cos(ang).astype(np.float32)
    sin = np.sin(ang).astype(np.float32)
    cF = np.zeros((128, T), np.float32)
    sF = np.zeros((128, T), np.float32)
    for p in range(128):
        i = p % 32
        cF[p] = cos[:, i % 16]
        sF[p] = -sin[:, i % 16] if i < 16 else sin[:, i % 16]
    cT = np.tile(cos, (1, 8))
    sT = np.tile(sin, (1, 8))
    return cF, sF, cT, sT


WIN_F = 26
WIN_T = 5


SMALLW = bool(int(os.environ.get('KSMALLW', '0')))


def build(T):
    nb_ = (lambda n: 1) if SMALLW else (lambda n: n)
    nc = bass.Bass("TRN2", target_bir_lowering=False)
    es = ExitStack()
    P = Prog(nc, es)
    NT = T // 128
    NS = T // ST
    SUB = ST // 128
    TA = HA + T + HA
    TCx = HC + T + HC

    def din(name, shape, dt=F32):
        return nc.dram_tensor(name, list(shape), dt, kind="ExternalInput").ap()

    DBG = set(os.environ.get('KDBG', '').split(','))

    def dscr(name, shape, dt):
        if name in DBG:
            return nc.dram_tensor(name, list(shape), dt, kind="ExternalOutput").ap()
        return nc.dram_tensor(name, list(shape), dt).ap()

    x_in = din("x", [T, D])
    y_out = nc.dram_tensor("y", [T, D], F32, kind="ExternalOutput").ap()
    wFin = [din("wFin%d" % l, [nb_(WIN_F), 128, KC * 128]) for l in range(NL)]
    wTin = [din("wTin%d" % l, [nb_(WIN_T), 128, KC * 512]) for l in range(NL)]
    wOut = [din("wOut%d" % l, [nb_(16), 128, KC * 128]) for l in range(NL)]
    wGa = [din("wGa%d" % l, [nb_(FC), 128, KC * 128]) for l in range(NL)]
    wUp = [din("wUp%d" % l, [nb_(FC), 128, KC * 128]) for l in range(NL)]
    wDn = [din("wDn%d" % l, [nb_(16), 128, FC * 128]) for l in range(NL)]
    gtab_in = din("gtab", [128, 5 * KC])
    dec_in = din("dec", [128, NL * 3 * 2])
    dec8_in = din("dec8", [128, NL * 2 * 8])
    sink_in = din("sink", [128, NL * 12])
    relb_in = din("relb", [32, 24])
    flags_in = din("flags", [128, 8])
    oh_in = din("oh", [32, 4 * NM])
    amask_in = din("amask", [128, 4 * 384])
    ident_in = din("ident", [128, 128])
    jrev_in = din("jrev", [128, 128])
    perm_in = din("perm", [128, 128])
    sel_in = din("sel", [128, 64])
    rtab_in = din("rtab", [128, 516])
    qexp_in = din("qexp", [128, 256])
    nidx_in = din("nidx", [128, 2 * NT])
    cF_in = din("cF", [128, T])
    sF_in = din("sF", [128, T])
    cT_in = din("cT", [T, 128])
    sT_in = din("sT", [T, 128])

    wFb = [dscr("wFb%d" % l, [WIN_F, 128, KC * 128], BF16) for l in range(NL)]
    wTb = [dscr("wTb%d" % l, [WIN_T, 128, KC * 512], BF16) for l in range(NL)]
    wOb = [dscr("wOb%d" % l, [16, 128, KC * 128], BF16) for l in range(NL)]
    wGb = [dscr("wGb%d" % l, [FC, 128, KC * 128], BF16) for l in range(NL)]
    wUb = [dscr("wUb%d" % l, [FC, 128, KC * 128], BF16) for l in range(NL)]
    wDb = [dscr("wDb%d" % l, [16, 128, FC * 128], BF16) for l in range(NL)]
    B_w = Buf("wscratch")
    xT_d = dscr("xT_d", [D, T], F32)
    B_xT = [Buf("xT%d" % j) for j in range(NS)]
    QA_d = dscr("QA_d", [768, T], BF16)
    KA_d = dscr("KA_d", [768, TA], BF16)
    VA_d = dscr("VA_d", [TA, 780], BF16)
    QC_d = dscr("QC_d", [768, T], BF16)
    KCx_d = dscr("KC_d", [256, TCx], BF16)
    VC_d = dscr("VC_d", [TCx, 260], BF16)
    QR_d = dscr("QR_d", [384, T], BF16)
    KR_d = dscr("KR_d", [384, T], BF16)
    VR_d = dscr("VR_d", [T, 512], BF16)
    GS_d = dscr("GS_d", [T, 512], BF16)
    KV_d = dscr("KV_d", [96, 6, NT, 64], F32)
    mixT_d = dscr("mixT_d", [D, T], BF16)
    EB_d = dscr("EB_d", [4, 12, 128, 384], BF16)
    ub_d = dscr("ub_d", [4, 24, NM], F32)
    B_QA, B_KA, B_VA, B_QC, B_KC, B_VC = (Buf(n) for n in ("QA", "KA", "VA", "QC", "KC", "VC"))
    B_QR, B_KR, B_VR, B_GS, B_KV, B_mix, B_EB, B_ub = (Buf(n) for n in ("QR", "KR", "VR", "GS", "KV", "mix", "EB", "ub"))
    PK_ROWS = 768 * 2 + 2 * 1024 + 256 * 2 + 2 * 128
    SEC_ROWS = [768, 768, 1024, 1024, 768]
    pins = [dscr("pin%d" % i, [r, 1024], BF16) for i, r in enumerate(SEC_ROWS)]
    pouts = [dscr("pout%d" % i, [2 * r, 1024], BF16) for i, r in enumerate(SEC_ROWS)]
    B_pins = [Buf("pin%d" % i) for i in range(5)]
    B_pouts = [Buf("pout%d" % i) for i in range(5)]
    sin_ = dscr("sin_", [96, 6 * 64], F32)
    sout = dscr("sout", [2 * 96, 6 * 64], F32)
    B_sin, B_sout = Buf("sin"), Buf("sout")
    B_y = Buf("y")

    def cp(E, out, in_):
        return E.tensor_copy(out, in_)

    def dma(q, out, in_, R, W):
        return P.op(q, lambda E, o=out, i=in_: E.dma_start(out=o, in_=i), R=R, W=W, dma=True)

    def const(name, src, shape, dt=F32):
        t = P.sb(name, shape, dt)
        b = Buf(name)
        dma("sp", t[:], src, [], [b])
        return t, b

    ident, B_ident = const("ident", ident_in[:, :], [128, 128])
    jrev, B_jrev = const("jrev", jrev_in[:, :], [128, 128])
    perm, B_perm = const("perm", perm_in[:, :], [128, 128])
    sel, B_sel = const("sel", sel_in[:, :], [128, 64])
    gtab, B_gtab = const("gtab", gtab_in[:, :], [128, 5 * KC])
    flags, B_flags = const("flags", flags_in[:, :], [128, 8])
    decp, B_decp = const("decp", dec_in[:, :], [128, NL * 6])
    dec8, B_dec8 = const("dec8", dec8_in[:, :], [128, NL * 16])
    sinkt, B_sink = const("sinkt", sink_in[:, :], [128, NL * 12])
    rtab, B_rtab = const("rtab", rtab_in[:, :], [128, 516])
    qexp, B_qexp = const("qexp", qexp_in[:, :], [128, 256])
    nidx, B_nidx = const("nidx", nidx_in[:, :], [128, 2 * NT])
    ones_bf = P.sb("ones_bf", [128, 128], BF16)
    B_ones = Buf("ones")
    P.op("pool", lambda E: E.memset(ones_bf[:], 1.0), W=[B_ones])
    identb = P.sb("identb", [128, 128], BF16)
    B_identb = Buf("identb")
    P.op("dve", lambda E: cp(E, identb[:], ident[:]), R=[B_ident], W=[B_identb])
    zcol = P.sb("zcol", [128, 1], F32)
    B_zcol = Buf("zcol")
    P.op("pool", lambda E: E.memset(zcol[:], 0.0), W=[B_zcol])

    psb = Rot([(P.ps("ps%d" % i, [128, 512]), Buf("ps%d" % i)) for i in range(7)])
    psbf = Rot([(P.ps("psbf", [128, 1024], BF16), Buf("psbf"))])
    ones32 = P.sb("ones32", [128, 32], F32)
    B_ones32 = Buf("ones32")
    P.op("pool", lambda E: E.memset(ones32[:], 1.0), W=[B_ones32])

    def lgcalc(name, src, Bsrc, n):
        t = P.sb(name, [128, n], F32)
        b = Buf(name)
        P.op("act", lambda E: E.activation(out=t[:], in_=src[:], func=AF.Exp, scale=-math.log(2.0)), R=[Bsrc], W=[b])
        P.op("dve", lambda E: E.tensor_scalar(out=t[:], in0=t[:], scalar1=-1.0, scalar2=1.0, op0=ALU.mult, op1=ALU.add), R=[b], W=[b])
        P.op("act", lambda E: E.activation(out=t[:], in_=t[:], func=AF.Ln), R=[b], W=[b])
        return t, b

    lgp, B_lgp = lgcalc("lgp", decp, B_decp, NL * 6)
    lg8, B_lg8 = lgcalc("lg8", dec8, B_dec8, NL * 16)
    sinke = P.sb("sinke", [128, NL * 12], F32)
    B_sinke = Buf("sinke")
    P.op("act", lambda E: E.activation(out=sinke[:], in_=sinkt[:], func=AF.Exp), R=[B_sink], W=[B_sinke])

    PH = set(os.environ.get('KPH', 'B,W,0,P1,X,R,A,P3').split(','))
    P.enabled = 'B' in PH
    with ExitStack() as bes:
        sv = P.es
        P.es = bes
        relb, B_relb = const("relb", relb_in[:, :], [32, 24])
        oh, B_oh = const("oh", oh_in[:, :], [32, 4 * NM])
        amask, B_amask = const("amask", amask_in[:, :], [128, 4 * 384])
        usb = P.sb("usb", [24, NM], F32)
        B_usb = Buf("usb")
        hk = Rot([(P.sb("hk%d" % i, [128, 384], F32), Buf("hk%d" % i)) for i in range(2)])
        ee = Rot([(P.sb("ee%d" % i, [128, 384], F32), Buf("ee%d" % i)) for i in range(2)])
        eb = Rot([(P.sb("eb%d" % i, [128, 384], BF16), Buf("eb%d" % i)) for i in range(2)])
        for v, (dil, koff, qoff, NQ, rad) in enumerate(VARIANTS):
            pt, pb = psb.next()
            P.op("pe", lambda E, pt=pt, v=v: E.matmul(pt[0:24, 0:NM], relb[:, :], oh[:, v * NM:(v + 1) * NM], start=True, stop=True),
                 R=[B_relb, B_oh], W=[pb])
            P.op("act", lambda E, pt=pt: E.activation(out=usb[:], in_=pt[0:24, 0:NM], func=AF.Copy), R=[pb], W=[B_usb])
            dma("sp", ub_d[v], usb[:], [B_usb], [B_ub])
            for h in range(12):
                hh = h if v < 3 else 12 + h
                ht, hb = hk.next()
                src = bass.AP(ub_d.tensor, (v * 24 + hh) * NM, [[1, 128], [1, NQ]])
                dma("sp", ht[:, 0:NQ], src, [B_ub], [hb])
                pt, pb = psb.next()
                P.op("pe", lambda E, pt=pt, ht=ht, NQ=NQ: E.matmul(pt[:, 0:NQ], jrev[:, :], ht[:, 0:NQ], start=True, stop=True),
                     R=[hb, B_jrev], W=[pb])
                et, ebf = ee.next()
                P.op("act", lambda E, pt=pt, et=et, NQ=NQ: E.activation(out=et[:, 0:NQ], in_=pt[:, 0:NQ], func=AF.Exp), R=[pb], W=[ebf])
                bt, bb = eb.next()
                P.op("dve", lambda E, et=et, bt=bt, NQ=NQ, v=v: E.tensor_tensor(out=bt[:, 0:NQ], in0=et[:, 0:NQ], in1=amask[:, v * 384:v * 384 + NQ], op=ALU.mult),
                     R=[ebf, B_amask], W=[bb])
                dma("pool", EB_d[v, h, :, 0:NQ], bt[:, 0:NQ], [bb], [B_EB])
        P.es = sv

    class Scope:
        def __enter__(self):
            P.barrier()
            self.sems = []
            SCOPE[0] = self.sems
            self.sv = P.es
            self.st = ExitStack()
            P.es = self.st
            return self

        def __exit__(self, *a):
            P.barrier()
            self.st.close()
            P.es = self.sv
            SCOPE[0] = None
            P.free_sems.extend(self.sems)
            return False

    def rot(name, n, shape, dt, ps=False):
        return Rot([((P.ps if ps else P.sb)("%s%d" % (name, i), shape, dt), Buf("%s%d" % (name, i))) for i in range(n)])

    cnt = [0]

    def alt(engs):
        cnt[0] += 1
        return engs[cnt[0] % len(engs)]

    P.enabled = 'W' in PH
    with Scope():
        wst = rot("wst", 3, [128, 8192], F32)
        wsb = rot("wsb", 3, [128, 8192], BF16)
        for l in range(NL):
            fams = [(wFin[l], wFb[l], WIN_F, KC, 128, l * KC), (wTin[l], wTb[l], WIN_T, KC, 512, l * KC),
                    (wOut[l], wOb[l], 16, KC, 128, None), (wGa[l], wGb[l], FC, KC, 128, (2 + l) * KC),
                    (wUp[l], wUb[l], FC, KC, 128, (2 + l) * KC), (wDn[l], wDb[l], 16, FC, 128, None)]
            for (src, dst, nblk, nk, ncol, gcol) in fams:
                n = nk * ncol
                for blk in range(nblk):
                    a, ab = wst.next()
                    b, bb = wsb.next()
                    dma("sp", a[:, 0:n], src[0 if SMALLW else blk], [], [ab])
                    eng = alt(["dve", "act"])
                    if gcol is None:
                        if eng == "dve":
                            P.op("dve", lambda E, a=a, b=b, n=n: E.tensor_copy(b[:, 0:n], a[:, 0:n]), R=[ab], W=[bb])
                        else:
                            P.op("act", lambda E, a=a, b=b, n=n: E.activation(out=b[:, 0:n], in_=a[:, 0:n], func=AF.Copy), R=[ab], W=[bb])
                    else:
                        for kc in range(nk):
                            sl = slice(kc * ncol, (kc + 1) * ncol)
                            gs_ = gtab[:, gcol + kc:gcol + kc + 1]
                            if eng == "dve":
                                P.op("dve", lambda E, a=a, b=b, sl=sl, gs_=gs_: E.tensor_scalar(out=b[:, sl], in0=a[:, sl], scalar1=gs_, scalar2=None, op0=ALU.mult),
                                     R=[ab, B_gtab], W=[bb])
                            else:
                                P.op("act", lambda E, a=a, b=b, sl=sl, gs_=gs_: E.activation(out=b[:, sl], in_=a[:, sl], func=AF.Copy, scale=gs_),
                                     R=[ab, B_gtab], W=[bb])
                    dma("pool", dst[blk], b[:, 0:n], [bb], [B_w])

    P.enabled = '0' in PH
    xT_v = xT_d.rearrange("(k p) t -> p k t", p=128)
    with Scope():
        xin = rot("xin", 2, [128, D], F32)
        xo = rot("xo", 2, [128, KC, 128], F32)
        for i in range(NT):
            a, ab = xin.next()
            o, ob = xo.next()
            dma("sp", a[:], x_in[i * 128:(i + 1) * 128, :], [], [ab])
            for g in range(4):
                pt, pb = psb.next()
                for q in range(4):
                    kc = 4 * g + q
                    P.op("pe", lambda E, pt=pt, a=a, q=q, kc=kc: E.transpose(pt[:, q * 128:(q + 1) * 128], a[:, kc * 128:(kc + 1) * 128], ident[:]),
                         R=[ab, B_ident], W=[pb])
                eng = alt(["dve", "act"])
                if eng == "dve":
                    P.op("dve", lambda E, pt=pt, o=o, g=g: E.tensor_copy(o[:, 4 * g:4 * g + 4, :], pt[:].rearrange("p (q t) -> p q t", q=4)), R=[pb], W=[ob])
                else:
                    P.op("act", lambda E, pt=pt, o=o, g=g: E.activation(out=o[:, 4 * g:4 * g + 4, :], in_=pt[:].rearrange("p (q t) -> p q t", q=4), func=AF.Copy), R=[pb], W=[ob])
            dma("pool", xT_v[:, :, i * 128:(i + 1) * 128], o[:], [ob], [B_xT[(i * 128) // ST]])

    def rms_stats(xt, xb, sqr, rsr):
        pt, pb = psb.next()
        for kc in range(KC):
            s_, sb_ = sqr.next()
            P.op("act", lambda E, s_=s_, kc=kc: E.activation(out=s_[:], in_=xt[:, kc, :], func=AF.Square), R=[xb], W=[sb_])
            P.op("pe", lambda E, s_=s_, kc=kc, pt=pt: E.matmul(pt[:, 0:ST], ones_bf[:], s_[:], start=(kc == 0), stop=(kc == KC - 1)),
                 R=[sb_, B_ones], W=[pb])
        r_, rb_ = rsr.next()
        P.op("dve", lambda E, r_=r_, pt=pt: E.tensor_scalar(out=r_[:], in0=pt[:, 0:ST], scalar1=1.0 / D, scalar2=EPS, op0=ALU.mult, op1=ALU.add), R=[pb], W=[rb_])
        P.op("act", lambda E, r_=r_: E.activation(out=r_[:], in_=r_[:], func=AF.Sqrt), R=[rb_], W=[rb_])
        P.op("dve", lambda E, r_=r_: E.reciprocal(r_[:], r_[:]), R=[rb_], W=[rb_])
        return r_, rb_

    def evac(pt_ap, out_ap, R, W, func=AF.Copy, eng=None):
        if eng is None:
            eng = alt(["act", "dve"])
        if eng == "dve" and func == AF.Copy:
            P.op("dve", lambda E: E.tensor_copy(out_ap, pt_ap), R=R, W=W)
        else:
            P.op("act", lambda E: E.activation(out=out_ap, in_=pt_ap, func=func), R=R, W=W)

    DKS = 32.0 ** -0.5
    HENG = os.environ.get('KHENG', 'dve,pool').split(',')
    OCS = set(int(v) for v in os.environ.get('KOCS', ','.join(str(i) for i in range(WIN_F))).split(','))
    KF = os.environ.get('KF', 'E,D').split(',')

    def P1(l):
        with Scope():
            xtr = rot("xt", 2, [128, KC, ST], F32)
            hTr = rot("hT", 1, [128, KC, ST], BF16)
            sqr = rot("sq", 2, [128, ST], BF16)
            rsr = rot("rs", 1, [128, ST], F32)
            wfr = rot("wf", 3, [128, KC * 128], BF16)
            wtr = rot("wt", 2, [128, KC * 512], BF16)
            otr = rot("ot", 3, [128, ST], BF16)
            y32r = rot("y32", 2, [128, ST], F32)
            t1r = rot("t1", 2, [128, ST], F32)
            t2r = rot("t2", 2, [128, ST], F32)
            cFt, B_cFt = P.sb("cFt", [128, ST], F32), Buf("cFt")
            sFt, B_sFt = P.sb("sFt", [128, ST], F32), Buf("sFt")
            cTt, B_cTt = P.sb("cTt", [128, SUB, 128], F32), Buf("cTt")
            sTt, B_sTt = P.sb("sTt", [128, SUB, 128], F32), Buf("sTt")
            vaug, B_vaug = P.sb("vaug", [128, SUB, 780], BF16), Buf("vaug")
            vcaug, B_vcaug = P.sb("vcaug", [128, SUB, 260], BF16), Buf("vcaug")
            vrt, B_vrt = P.sb("vrt", [128, SUB, 512], BF16), Buf("vrt")
            gst, B_gst = P.sb("gst", [128, SUB, 512], BF16), Buf("gst")
            kraw, B_kraw = P.sb("kraw", [128, 256], F32), Buf("kraw")
            kro, B_kro = P.sb("kro", [128, 256], F32), Buf("kro")
            kta, B_kta = P.sb("kta", [128, 128], F32), Buf("kta")
            ktb, B_ktb = P.sb("ktb", [128, 128], F32), Buf("ktb")
            kd, B_kd = P.sb("kd", [128, 2, 256], BF16), Buf("kd")
            kvt, B_kvt = P.sb("kvt", [96, 6, SUB, 64], F32), Buf("kvt")
            kdfull, B_kdf = P.sb("kdfull", [128, 2, 256], F32), Buf("kdfull")
            e8, B_e8 = P.sb("e8", [128, 16], F32), Buf("e8")
            P.op("pool", lambda E: E.memset(vaug[:], 1.0), W=[B_vaug])
            P.op("pool", lambda E: E.memset(vcaug[:], 1.0), W=[B_vcaug])
            P.op("pool", lambda E: E.memset(kvt[:], 0.0), W=[B_kvt])
            for dr in range(2):
                P.op("dve", lambda E, dr=dr: E.tensor_scalar(out=e8[:, dr * 8:(dr + 1) * 8], in0=lg8[:, (l * 2 + dr) * 8:(l * 2 + dr) * 8 + 8],
                                                            scalar1=rtab[:, 512 + dr:513 + dr], scalar2=None, op0=ALU.mult), R=[B_lg8, B_rtab], W=[B_e8])
            P.op("act", lambda E: E.activation(out=e8[:], in_=e8[:], func=AF.Exp), R=[B_e8], W=[B_e8])
            P.op("dve", lambda E: E.tensor_scalar(out=e8[:], in0=e8[:], scalar1=DKS, scalar2=None, op0=ALU.mult), R=[B_e8], W=[B_e8])
            for dr in range(2):
                for h in range(8):
                    P.op("act", lambda E, dr=dr, h=h: E.activation(out=kdfull[:, dr, h * 32:(h + 1) * 32], in_=ones32[:, :], func=AF.Copy,
                                                                  scale=e8[:, dr * 8 + h:dr * 8 + h + 1]), R=[B_e8, B_ones32], W=[B_kdf])
            vaug_v = vaug[:].rearrange("p s (h e) -> p s h e", e=65)
            vcaug_v = vcaug[:].rearrange("p s (h e) -> p s h e", e=65)
            for j in range(NS):
                ts = slice(j * ST, (j + 1) * ST)
                xt, xb = xtr.next()
                dma("sp", xt[:], xT_v[:, :, ts], [B_xT[j]], [xb])
                dma("sp", cFt[:], cF_in[:, ts], [], [B_cFt])
                dma("sp", sFt[:], sF_in[:, ts], [], [B_sFt])
                dma("sp", cTt[:], cT_in[ts, :].rearrange("(s p) f -> p s f", p=128), [], [B_cTt])
                dma("sp", sTt[:], sT_in[ts, :].rearrange("(s p) f -> p s f", p=128), [], [B_sTt])
                SP1 = set(os.environ.get('KP1', 'S,H,F,FR,T0,T1,T2,T3,T4,KV').split(','))
                en0 = P.enabled
                P.enabled = en0 and 'S' in SP1
                rs, rb = rms_stats(xt, xb, sqr, rsr)
                P.enabled = en0 and 'H' in SP1
                hT, hb = hTr.next()
                for kc in range(KC):
                    P.op(alt(HENG), lambda E, kc=kc, hT=hT, xt=xt, rs=rs: E.tensor_tensor(out=hT[:, kc, :], in0=xt[:, kc, :], in1=rs[:], op=ALU.mult),
                         R=[xb, rb], W=[hb])
                P.enabled = en0
                for oc in range(WIN_F):
                    P.enabled = en0 and (('F' in SP1) if oc < 20 else ('FR' in SP1)) and oc in OCS
                    wf, wb = wfr.next()
                    dma("sp", wf[:], wFb[l][oc], [B_w], [wb])
                    pt, pb = psb.next()
                    for kc in range(KC):
                        P.op("pe", lambda E, pt=pt, wf=wf, hT=hT, kc=kc: E.matmul(pt[:, 0:ST], wf[:, kc * 128:(kc + 1) * 128], hT[:, kc, :], start=(kc == 0), stop=(kc == KC - 1)),
                             R=[wb, hb], W=[pb])
                    ot, ob = otr.next()
                    if oc < 20:
                        if 'E' in KF:
                            evac(pt[:, 0:ST], ot[:], [pb], [ob])
                        if 'D' not in KF:
                            continue
                        if oc < 6:
                            dma("pool", QA_d[oc * 128:(oc + 1) * 128, ts], ot[:], [ob], [B_QA])
                        elif oc < 12:
                            dma("pool", KA_d[(oc - 6) * 128:(oc - 5) * 128, HA + j * ST:HA + (j + 1) * ST], ot[:], [ob], [B_KA])
                        elif oc < 18:
                            dma("pool", QC_d[(oc - 12) * 128:(oc - 11) * 128, ts], ot[:], [ob], [B_QC])
                        else:
                            dma("pool", KCx_d[(oc - 18) * 128:(oc - 17) * 128, HC + j * ST:HC + (j + 1) * ST], ot[:], [ob], [B_KC])
                    else:
                        y32, yb = y32r.next()
                        P.op("act", lambda E, y32=y32, pt=pt: E.activation(out=y32[:], in_=pt[:, 0:ST], func=AF.Copy), R=[pb], W=[yb])
                        pz, pzb = psb.next()
                        P.op("pe", lambda E, pz=pz, y32=y32: E.matmul(pz[:, 0:ST], perm[:, :], y32[:], start=True, stop=True), R=[yb, B_perm], W=[pzb])
                        t1, t1b = t1r.next()
                        t2, t2b = t2r.next()
                        P.op("pool", lambda E, t1=t1, y32=y32: E.tensor_tensor(out=t1[:], in0=y32[:], in1=cFt[:], op=ALU.mult), R=[yb, B_cFt], W=[t1b])
                        P.op("dve", lambda E, t2=t2, pz=pz: E.tensor_tensor(out=t2[:], in0=pz[:, 0:ST], in1=sFt[:], op=ALU.mult), R=[pzb, B_sFt], W=[t2b])
                        P.op("dve", lambda E, t1=t1, t2=t2, ot=ot: E.tensor_tensor(out=ot[:], in0=t1[:], in1=t2[:], op=ALU.add), R=[t1b, t2b], W=[ob])
                        if oc < 23:
                            dma("pool", QR_d[(oc - 20) * 128:(oc - 19) * 128, ts], ot[:], [ob], [B_QR])
                        else:
                            dma("pool", KR_d[(oc - 23) * 128:(oc - 22) * 128, ts], ot[:], [ob], [B_KR])
                for cc in (0, 1, 2, 3, 4):
                    P.enabled = en0 and ('T%d' % cc in SP1)
                    wt, wtb = wtr.next()
                    dma("sp", wt[:], wTb[l][cc], [B_w], [wtb])
                    for s in range(SUB):
                        pt, pb = psb.next()
                        for kc in range(KC):
                            P.op("pe", lambda E, pt=pt, wt=wt, hT=hT, kc=kc, s=s: E.matmul(pt[:, :], hT[:, kc, s * 128:(s + 1) * 128], wt[:, kc * 512:(kc + 1) * 512], start=(kc == 0), stop=(kc == KC - 1)),
                                 R=[wtb, hb], W=[pb])
                        if cc == 0:
                            evac(pt[:].rearrange("p (h e) -> p h e", e=64), vaug_v[:, s, 0:8, 0:64], [pb], [B_vaug])
                        elif cc == 1:
                            e_ = alt(["act", "dve"])
                            evac(pt[:, 0:256].rearrange("p (h e) -> p h e", e=64), vaug_v[:, s, 8:12, 0:64], [pb], [B_vaug], eng=e_)
                            evac(pt[:, 256:512].rearrange("p (h e) -> p h e", e=64), vcaug_v[:, s, :, 0:64], [pb], [B_vcaug], eng=e_)
                        elif cc == 2:
                            evac(pt[:, :], vrt[:, s, :], [pb], [B_vrt])
                        elif cc == 3:
                            P.op("act", lambda E, pt=pt, s=s: E.activation(out=gst[:, s, :], in_=pt[:, :], func=AF.Silu), R=[pb], W=[B_gst])
                        else:
                            P.op("act", lambda E, pt=pt: E.activation(out=kraw[:], in_=pt[:, 0:256], func=AF.Copy), R=[pb], W=[B_kraw])
                            kv4 = kraw[:].rearrange("p (h t i) -> p h t i", h=8, t=2)
                            ko4 = kro[:].rearrange("p (h t i) -> p h t i", h=8, t=2)
                            c3 = cTt[:, s, :].rearrange("p (h i) -> p h i", h=8)
                            s3 = sTt[:, s, :].rearrange("p (h i) -> p h i", h=8)
                            ta3 = kta[:].rearrange("p (h i) -> p h i", h=8)
                            tb3 = ktb[:].rearrange("p (h i) -> p h i", h=8)
                            RR = [B_kraw, B_cTt, B_sTt]
                            P.op("pool", lambda E, kv4=kv4, c3=c3, ta3=ta3: E.tensor_tensor(out=ta3, in0=kv4[:, :, 0, :], in1=c3, op=ALU.mult), R=RR, W=[B_kta])
                            P.op("pool", lambda E, kv4=kv4, s3=s3, tb3=tb3: E.tensor_tensor(out=tb3, in0=kv4[:, :, 1, :], in1=s3, op=ALU.mult), R=RR, W=[B_ktb])
                            P.op("pool", lambda E, ko4=ko4, ta3=ta3, tb3=tb3: E.tensor_tensor(out=ko4[:, :, 0, :], in0=ta3, in1=tb3, op=ALU.subtract), R=[B_kta, B_ktb], W=[B_kro])
                            P.op("pool", lambda E, kv4=kv4, s3=s3, ta3=ta3: E.tensor_tensor(out=ta3, in0=kv4[:, :, 0, :], in1=s3, op=ALU.mult), R=RR + [B_kro], W=[B_kta])
                            P.op("pool", lambda E, kv4=kv4, c3=c3, tb3=tb3: E.tensor_tensor(out=tb3, in0=kv4[:, :, 1, :], in1=c3, op=ALU.mult), R=RR + [B_kro], W=[B_ktb])
                            P.op("pool", lambda E, ko4=ko4, ta3=ta3, tb3=tb3: E.tensor_tensor(out=ko4[:, :, 1, :], in0=ta3, in1=tb3, op=ALU.add), R=[B_kta, B_ktb], W=[B_kro])
                            for dr in range(2):
                                P.op("dve", lambda E, dr=dr: E.tensor_tensor(out=kd[:, dr, :], in0=kro[:], in1=kdfull[:, dr, :], op=ALU.mult), R=[B_kro, B_kdf], W=[B_kd])
                            P.enabled = en0 and ('KV' in SP1) and ('T4' in SP1)
                            for dr in range(2):
                                pk, pkb = psb.next()
                                for g in range(3):
                                    nh = 3 if g < 2 else 2
                                    P.op("pe", lambda E, pk=pk, dr=dr, g=g, nh=nh, s=s: E.matmul(pk[0:32 * nh, g * 192:g * 192 + 64 * nh],
                                                                                                 kd[:, dr, g * 96:g * 96 + 32 * nh], vrt[:, s, g * 192:g * 192 + 64 * nh], start=True, stop=True),
                                         R=[B_kd, B_vrt], W=[pkb])
                                e_ = alt(["act", "dve"])
                                for g in range(3):
                                    nh = 3 if g < 2 else 2
                                    base = g * 192
                                    for jj in range(nh):
                                        evac(pk[32 * jj:32 * jj + 32, base + 64 * jj:base + 64 * jj + 64], kvt[32 * jj:32 * jj + 32, dr * 3 + g, s, :], [pkb], [B_kvt], eng=e_)
                    if cc == 1:
                        KT1 = os.environ.get('KT1', 'A,C')
                        if 'A' in KT1:
                            dma("pool", VA_d[HA + j * ST:HA + (j + 1) * ST, :].rearrange("(s p) c -> p s c", p=128), vaug[:], [B_vaug], [B_VA])
                        if 'C' in KT1:
                            dma("pool", VC_d[HC + j * ST:HC + (j + 1) * ST, :].rearrange("(s p) c -> p s c", p=128), vcaug[:], [B_vcaug], [B_VC])
                    elif cc == 2:
                        dma("pool", VR_d[ts, :].rearrange("(s p) c -> p s c", p=128), vrt[:], [B_vrt], [B_VR])
                    elif cc == 3:
                        dma("pool", GS_d[ts, :].rearrange("(s p) c -> p s c", p=128), gst[:], [B_gst], [B_GS])
                    elif cc == 4:
                        dma("pool", KV_d[:, :, j * SUB:(j + 1) * SUB, :], kvt[:], [B_kvt], [B_KV])
                P.enabled = en0

    PKR = PK_ROWS

    def exchange_kv():
        secs = [
            (768, [(slice(0, 768), slice(0, 1024), KA_d[:, HA + T - 1024:HA + T], B_KA)],
                  [(KA_d[:, 0:HA], 0, slice(0, 768), slice(0, 1024), B_KA)]),
            (768, [(slice(0, 768), slice(0, 1024), KA_d[:, HA:HA + 1024], B_KA)],
                  [(KA_d[:, HA + T:TA], 1, slice(0, 768), slice(0, 1024), B_KA)]),
            (1024, [(slice(0, 1024), slice(0, 780), VA_d[HA + T - 1024:HA + T, :], B_VA)],
                   [(VA_d[0:HA, :], 0, slice(0, 1024), slice(0, 780), B_VA)]),
            (1024, [(slice(0, 1024), slice(0, 780), VA_d[HA:HA + 1024, :], B_VA)],
                   [(VA_d[HA + T:TA, :], 1, slice(0, 1024), slice(0, 780), B_VA)]),
            (768, [(slice(0, 256), slice(0, 128), KCx_d[:, HC + T - 128:HC + T], B_KC),
                   (slice(256, 512), slice(0, 128), KCx_d[:, HC:HC + 128], B_KC),
                   (slice(512, 640), slice(0, 260), VC_d[HC + T - 128:HC + T, :], B_VC),
                   (slice(640, 768), slice(0, 260), VC_d[HC:HC + 128, :], B_VC)],
                  [(KCx_d[:, 0:HC], 0, slice(0, 256), slice(0, 128), B_KC),
                   (KCx_d[:, HC + T:TCx], 1, slice(256, 512), slice(0, 128), B_KC),
                   (VC_d[0:HC, :], 0, slice(512, 640), slice(0, 260), B_VC),
                   (VC_d[HC + T:TCx, :], 1, slice(640, 768), slice(0, 260), B_VC)]),
        ]
        for i, (rows, ins_, outs_) in enumerate(secs):
            pi_, po_, bi_, bo_ = pins[i], pouts[i], B_pins[i], B_pouts[i]
            for (rs_, cs_, src, bsrc) in ins_:
                dma("pool", pi_[rs_, cs_], src, [bsrc], [bi_])
            P.op("pool", lambda E, pi_=pi_, po_=po_: E.collective_compute("AllGather", ALU.bypass, replica_groups=[[0, 1], [2, 3], [4, 5], [6, 7]],
                                                                      ins=[pi_.opt()], outs=[po_.opt()]), R=[bi_], W=[bo_], dma=True, inc=1)
            for (dst, slot, rs_, cs_, bdst) in outs_:
                r0 = slot * rows
                dma("pool", dst, po_[r0 + rs_.start:r0 + rs_.stop, cs_], [bo_], [bdst])

    def retention(l):
        with Scope():
            kva, B_kva = P.sb("kva", [96, 6, NT, 64], F32), Buf("kva")
            Rb, B_Rb = P.sb("Rb", [96, 6, NT, 64], BF16), Buf("Rbf")
            Rcur, B_Rcur = [P.sb("Rcur%d" % i, [96, 3, 64], F32) for i in range(2)], [Buf("Rcur%d" % i) for i in range(2)]
            Rin, B_Rin = P.sb("Rin", [96, 6, 64], F32), Buf("Rin")
            gc, B_gc = P.sb("gc", [128, 6], F32), Buf("gc")
            gpow, B_gpow = P.sb("gpow", [128, 6, NT], F32), Buf("gpow")
            qdtab, B_qdt = P.sb("qdtab", [128, 6, 128], F32), Buf("qdtab")
            DT, B_DT = P.sb("DT", [128, 8, 128], F32), Buf("DT")
            dtmp, B_dtmp = P.sb("dtmp", [128, 128], F32), Buf("dtmp")
            dma("sp", kva[:], KV_d[:, :, :, :], [B_KV], [B_kva])
            lcol = lambda g, dr: l * 6 + g * 2 + dr
            P.op("act", lambda E: E.activation(out=gc[:], in_=lgp[:, l * 6:l * 6 + 6], func=AF.Exp, scale=128.0), R=[B_lgp], W=[B_gc])
            lg128, B_lg128 = P.sb("lg128", [128, 6], F32), Buf("lg128")
            P.op("dve", lambda E: E.tensor_scalar(out=lg128[:], in0=lgp[:, l * 6:l * 6 + 6], scalar1=128.0, scalar2=None, op0=ALU.mult), R=[B_lgp], W=[B_lg128])
            for g in range(3):
                for dr in range(2):
                    c = g * 2 + dr
                    P.op("act", lambda E, c=c, dr=dr: E.activation(out=gpow[:, c, :], in_=nidx[:, dr * NT:(dr + 1) * NT], func=AF.Exp, scale=lg128[:, c:c + 1]),
                         R=[B_nidx, B_lg128], W=[B_gpow])
                    P.op("act", lambda E, c=c, dr=dr: E.activation(out=qdtab[:, c, :], in_=qexp[:, dr * 128:(dr + 1) * 128], func=AF.Exp, scale=lgp[:, l * 6 + c:l * 6 + c + 1]),
                         R=[B_qexp, B_lgp], W=[B_qdt])
            for h in range(8):
                P.op("act", lambda E, h=h: E.activation(out=DT[:, h, :], in_=rtab[:, 0:128], func=AF.Exp, scale=lg8[:, (l * 2) * 8 + h:(l * 2) * 8 + h + 1]), R=[B_rtab, B_lg8], W=[B_DT])
                P.op("dve", lambda E, h=h: E.tensor_tensor(out=DT[:, h, :], in0=DT[:, h, :], in1=rtab[:, 128:256], op=ALU.mult), R=[B_DT, B_rtab], W=[B_DT])
                P.op("act", lambda E, h=h: E.activation(out=dtmp[:], in_=rtab[:, 256:384], func=AF.Exp, scale=lg8[:, (l * 2 + 1) * 8 + h:(l * 2 + 1) * 8 + h + 1]), R=[B_rtab, B_lg8, B_DT], W=[B_dtmp])
                P.op("dve", lambda E, h=h: E.tensor_tensor(out=dtmp[:], in0=dtmp[:], in1=rtab[:, 384:512], op=ALU.mult), R=[B_dtmp, B_rtab], W=[B_dtmp])
                P.op("dve", lambda E, h=h: E.tensor_tensor(out=DT[:, h, :], in0=DT[:, h, :], in1=dtmp[:], op=ALU.add), R=[B_DT, B_dtmp], W=[B_DT])
                P.op("dve", lambda E, h=h: E.tensor_scalar(out=DT[:, h, :], in0=DT[:, h, :], scalar1=DKS, scalar2=None, op0=ALU.mult), R=[B_DT], W=[B_DT])
            for dr, eng in ((0, "dve"), (1, "dve")):
                Rc, Bc = Rcur[dr], B_Rcur[dr]
                P.op(eng, lambda E, Rc=Rc: E.memset(Rc[:], 0.0), W=[Bc])
                order = range(NT) if dr == 0 else range(NT - 1, -1, -1)
                for n in order:
                    P.op(eng, lambda E, Rc=Rc, n=n, dr=dr: E.tensor_copy(Rb[:, dr * 3:dr * 3 + 3, n, :], Rc[:]), R=[Bc], W=[B_Rb])
                    for g in range(3):
                        c = g * 2 + dr
                        P.op(eng, lambda E, Rc=Rc, n=n, g=g, c=c, dr=dr: E.scalar_tensor_tensor(out=Rc[:, g, :], in0=Rc[:, g, :], scalar=gc[0:96, c:c + 1], in1=kva[:, dr * 3 + g, n, :],
                                                                                               op0=ALU.mult, op1=ALU.add), R=[Bc, B_gc, B_kva], W=[Bc])
                dma("pool", sin_[:, dr * 192:(dr + 1) * 192], Rc[:].rearrange("p g e -> p (g e)"), [Bc], [B_sin])
            P.op("pool", lambda E: E.collective_compute("AllGather", ALU.bypass, replica_groups=[[0, 1], [2, 3], [4, 5], [6, 7]],
                                                        ins=[sin_.opt()], outs=[sout.opt()]), R=[B_sin], W=[B_sout], dma=True, inc=1)
            dma("sp", Rin[:, 0:3, :], sout[0:96, 0:192].rearrange("p (g e) -> p g e", g=3), [B_sout], [B_Rin])
            dma("sp", Rin[:, 3:6, :], sout[96:192, 192:384].rearrange("p (g e) -> p g e", g=3), [B_sout], [B_Rin])
            for dr in range(2):
                P.op("dve", lambda E, dr=dr: E.tensor_scalar(out=Rin[:, dr * 3:dr * 3 + 3, :], in0=Rin[:, dr * 3:dr * 3 + 3, :], scalar1=flags[0:96, 4 + dr:5 + dr], scalar2=None, op0=ALU.mult),
                     R=[B_Rin, B_flags], W=[B_Rin])
            for dr, eng in ((0, "dve"), (1, "dve")):
                for g in range(3):
                    c = g * 2 + dr
                    for n in range(NT):
                        P.op(eng, lambda E, dr=dr, g=g, c=c, n=n: E.scalar_tensor_tensor(out=Rb[:, dr * 3 + g, n, :], in0=Rin[:, dr * 3 + g, :], scalar=gpow[0:96, c, n:n + 1], in1=Rb[:, dr * 3 + g, n, :],
                                                                                        op0=ALU.mult, op1=ALU.add), R=[B_Rin, B_gpow, B_Rb], W=[B_Rb])
            KR_ = os.environ.get('KR', 'pre,out,cc')
            P.enabled = P.enabled and 'out' in KR_
            qrt, B_qrt = P.sb("qrt", [128, 3, ST], BF16), Buf("qrt")
            krt_, B_krt = P.sb("krt_", [128, 3, ST], BF16), Buf("krt_")
            vr_, B_vr = P.sb("vr_", [128, SUB, 512], BF16), Buf("vr_")
            gs_t, B_gs = P.sb("gs_t", [128, SUB, 512], BF16), Buf("gs_t")
            qd, B_qd = P.sb("qd", [128, 6, 128], BF16), Buf("qd")
            wts, B_wts = P.sb("wts", [128, 8, 128], BF16), Buf("wts")
            osb, B_osb = P.sb("osb", [128, 512], F32), Buf("osb")
            osq, B_osq = P.sb("osq", [128, 512], F32), Buf("osq")
            st8, B_st8 = P.sb("st8", [128, 4, 8], F32), Buf("st8")
            on, B_on = P.sb("on", [128, 512], F32), Buf("on")
            mixr, B_mixr = P.sb("mixr", [128, 512], BF16), Buf("mixr")
            mro, B_mro = P.sb("mro", [128, 4, ST], BF16), Buf("mro")
            en_r = P.enabled
            KRO = os.environ.get('KRO', 'qd,S,O,G,TR').split(',')
            for j in range(NS):
                ts = slice(j * ST, (j + 1) * ST)
                dma("sp", qrt[:], QR_d[:, ts].rearrange("(g p) t -> p g t", p=128), [B_QR], [B_qrt])
                dma("sp", krt_[:], KR_d[:, ts].rearrange("(g p) t -> p g t", p=128), [B_KR], [B_krt])
                dma("sp", vr_[:], VR_d[ts, :].rearrange("(s p) c -> p s c", p=128), [B_VR], [B_vr])
                dma("sp", gs_t[:], GS_d[ts, :].rearrange("(s p) c -> p s c", p=128), [B_GS], [B_gs])
                for s in range(SUB):
                    n = j * SUB + s
                    tt = slice(s * 128, (s + 1) * 128)
                    P.enabled = en_r and 'qd' in KRO
                    for c in range(6):
                        P.op("pool", lambda E, c=c, tt=tt: E.tensor_tensor(out=qd[:, c, :], in0=qrt[:, c // 2, tt], in1=qdtab[:, c, :], op=ALU.mult), R=[B_qrt, B_qdt], W=[B_qd])
                    P.enabled = en_r and 'S' in KRO
                    pts = [psb.next() for _ in range(3)]
                    for h in range(8):
                        g, jj = divmod(h, 3)
                        pt, pb = pts[jj]
                        P.op("pe", lambda E, pt=pt, g=g, jj=jj, tt=tt: E.matmul(pt[:, g * 128:(g + 1) * 128], krt_[32 * jj:32 * jj + 32, g, tt], qrt[32 * jj:32 * jj + 32, g, tt], start=True, stop=True),
                             R=[B_qrt, B_krt], W=[pb])
                    for jj in range(3):
                        pt, pb = pts[jj]
                        ng = 3 if jj < 2 else 2
                        P.op("dve", lambda E, pt=pt, jj=jj, ng=ng: E.tensor_tensor(out=wts[:, jj:8:3, :], in0=pt[:, 0:ng * 128].rearrange("p (h t) -> p h t", h=ng), in1=DT[:, jj:8:3, :], op=ALU.mult),
                             R=[pb, B_DT], W=[B_wts])
                    P.enabled = en_r and 'O' in KRO
                    po, pob = psb.next()
                    for h in range(8):
                        g, jj = divmod(h, 3)
                        pr = slice(32 * jj, 32 * jj + 32)
                        P.op("pe", lambda E, po=po, h=h, s=s: E.matmul(po[:, h * 64:(h + 1) * 64], wts[:, h, :], vr_[:, s, h * 64:(h + 1) * 64], start=True, stop=False), R=[B_wts, B_vr], W=[pob])
                        P.op("pe", lambda E, po=po, h=h, g=g, pr=pr, n=n: E.matmul(po[:, h * 64:(h + 1) * 64], qd[pr, g * 2, :], Rb[pr, g, n, :], start=False, stop=False), R=[B_qd, B_Rb], W=[pob])
                        P.op("pe", lambda E, po=po, h=h, g=g, pr=pr, n=n: E.matmul(po[:, h * 64:(h + 1) * 64], qd[pr, g * 2 + 1, :], Rb[pr, 3 + g, n, :], start=False, stop=True), R=[B_qd, B_Rb], W=[pob])
                    P.enabled = en_r and 'G' in KRO
                    P.op("act", lambda E, po=po: E.activation(out=osb[:], in_=po[:], func=AF.Copy), R=[pob], W=[B_osb])
                    P.op("act", lambda E: E.activation(out=osq[:], in_=osb[:], func=AF.Square), R=[B_osb], W=[B_osq])
                    P.op("dve", lambda E: E.tensor_reduce(out=st8[:, 0, :], in_=osb[:].rearrange("p (h e) -> p h e", h=8), axis=AX.X, op=ALU.add), R=[B_osb], W=[B_st8])
                    P.op("dve", lambda E: E.tensor_reduce(out=st8[:, 1, :], in_=osq[:].rearrange("p (h e) -> p h e", h=8), axis=AX.X, op=ALU.add), R=[B_osq], W=[B_st8])
                    P.op("dve", lambda E: E.tensor_scalar(out=st8[:, 0, :], in0=st8[:, 0, :], scalar1=1.0 / 64, scalar2=None, op0=ALU.mult), R=[B_st8], W=[B_st8])
                    P.op("dve", lambda E: E.tensor_tensor(out=st8[:, 2, :], in0=st8[:, 0, :], in1=st8[:, 0, :], op=ALU.mult), R=[B_st8], W=[B_st8])
                    P.op("dve", lambda E: E.scalar_tensor_tensor(out=st8[:, 3, :], in0=st8[:, 1, :], scalar=1.0 / 64, in1=st8[:, 2, :], op0=ALU.mult, op1=ALU.subtract), R=[B_st8], W=[B_st8])
                    P.op("dve", lambda E: E.tensor_scalar(out=st8[:, 3, :], in0=st8[:, 3, :], scalar1=1.0, scalar2=GN_EPS, op0=ALU.mult, op1=ALU.add), R=[B_st8], W=[B_st8])
                    P.op("act", lambda E: E.activation(out=st8[:, 3, :], in_=st8[:, 3, :], func=AF.Sqrt), R=[B_st8], W=[B_st8])
                    P.op("dve", lambda E: E.reciprocal(st8[:, 3, :], st8[:, 3, :]), R=[B_st8], W=[B_st8])
                    for h in range(8):
                        P.op("pool", lambda E, h=h: E.tensor_scalar(out=on[:, h * 64:(h + 1) * 64], in0=osb[:, h * 64:(h + 1) * 64], scalar1=st8[:, 0, h:h + 1], scalar2=st8[:, 3, h:h + 1],
                                                                    op0=ALU.subtract, op1=ALU.mult), R=[B_osb, B_st8], W=[B_on])
                    P.op("pool", lambda E, s=s: E.tensor_tensor(out=mixr[:], in0=on[:], in1=gs_t[:, s, :], op=ALU.mult), R=[B_on, B_gs], W=[B_mixr])
                    P.enabled = en_r and 'TR' in KRO
                    ptb, ptbb = psbf.next()
                    for q in range(4):
                        P.op("pe", lambda E, ptb=ptb, q=q: E.transpose(ptb[:, q * 128:(q + 1) * 128], mixr[:, q * 128:(q + 1) * 128], identb[:]), R=[B_mixr, B_identb], W=[ptbb])
                    evac(ptb[:, 0:512].rearrange("p (q t) -> p q t", q=4), mro[:, :, tt], [ptbb], [B_mro])
                dma("pool", mixT_d[768:1280, ts].rearrange("(q p) t -> p q t", p=128), mro[:], [B_mro], [B_mix])
                P.enabled = en_r

    def attention(l):
        with Scope():
            qt, B_qt = P.sb("qt", [64, T], BF16), Buf("qt")
            kt, B_kt = P.sb("kt", [64, TA], BF16), Buf("kt")
            acc, B_acc = P.sb("acc", [65, T], F32), Buf("acc")
            mo, B_mo = P.sb("mo", [64, T], BF16), Buf("mo")
            racc, B_racc = P.sb("racc", [65, 512], F32), Buf("racc")
            ebr = rot("ebt", 2, [128, 384], BF16)
            vtr = rot("vt", 2, [128, 80, 65], BF16)
            ptr_ = rot("pt", 4, [128, 384], BF16)
            pt2r = rot("pt2", 6, [128, 384], BF16)
            pend = []
            LA = 3
            sc_rot = Rot(psb.items[0:4])
            zb, B_zb = P.sb("zb", [1, 512], BF16), Buf("zb")
            P.op("pool", lambda E: E.memset(zb[:], 0.0), W=[B_zb])
            gr_rot = Rot(psb.items[4:7])
            P.op("pool", lambda E: E.memset(racc[:], 0.0), W=[B_racc])
            jobs = []
            for h in range(12):
                jobs.append((QA_d[h * 64:(h + 1) * 64, :], KA_d[h * 64:(h + 1) * 64, :], TA, VA_d, 780, h * 65, HA, (0, 1, 2), h, (0, 1), None, h * 64, B_QA, B_KA, B_VA))
            for h in range(12):
                g = h // 3
                jobs.append((QC_d[h * 64:(h + 1) * 64, :], KCx_d[g * 64:(g + 1) * 64, :], TCx, VC_d, 260, g * 65, HC, (3,), h, (2, 3), l * 12 + h, 1280 + h * 64, B_QC, B_KC, B_VC))
            for (Qs, Ks, Text, Vd, rowlen, vcol0, H, vars_, h, fcols, sinkc, mrow, BQ, BK, BV) in jobs:
                dma("sp", qt[:], Qs, [BQ], [B_qt])
                dma("sp", kt[:, 0:Text], Ks, [BK], [B_kt])
                P.op("pool", lambda E: E.memset(acc[:], 0.0), W=[B_acc])
                for v in vars_:
                    dil, koff, qoff, NQ, rad = VARIANTS[v]
                    L = T // dil
                    nj = L // 128 + (1 if koff == -64 else 2)
                    ebt, ebb = ebr.next()
                    dma("sp", ebt[:, 0:NQ], EB_d[v, h, :, 0:NQ], [B_EB], [ebb])
                    for r in range(dil):
                        vt, vb = vtr.next()
                        off = (koff * dil + r + H) * rowlen + vcol0
                        nchunk = 16
                        for j0 in range(0, nj, nchunk):
                            j1 = min(nj, j0 + nchunk)
                            src = bass.AP(Vd.tensor, off + j0 * 128 * dil * rowlen, [[dil * rowlen, 128], [128 * dil * rowlen, j1 - j0], [1, 65]])
                            dma("sp", vt[:, j0:j1, :], src, [BV], [vb])
                        groups = {}
                        for j in range(nj):
                            k0 = 128 * j + koff
                            q_lo = max(0, 128 * j + qoff)
                            q_hi = min(L, 128 * j + qoff + NQ)
                            nq = q_hi - q_lo
                            if nq <= 0:
                                continue
                            qq0 = q_lo - (128 * j + qoff)
                            ks = k0 * dil + r + H
                            qs = q_lo * dil + r
                            if k0 < 0:
                                bcol, Bb = flags[:, fcols[0]:fcols[0] + 1], B_flags
                            elif k0 + 128 > L:
                                bcol, Bb = flags[:, fcols[1]:fcols[1] + 1], B_flags
                            else:
                                bcol, Bb = zcol[:, 0:1], B_zcol
                            ps_, psb_ = sc_rot.next()
                            P.op("pe", lambda E, ps_=ps_, ks=ks, qs=qs, nq=nq, dil=dil: E.matmul(ps_[:, 0:nq], kt[0:64, ks:ks + 127 * dil + 1:dil], qt[0:64, qs:qs + (nq - 1) * dil + 1:dil], start=True, stop=True),
                                 R=[B_kt, B_qt], W=[psb_])
                            p1, p1b = ptr_.next()
                            P.op("act", lambda E, ps_=ps_, p1=p1, nq=nq, bcol=bcol: E.activation(out=p1[:, 0:nq], in_=ps_[:, 0:nq], func=AF.Exp, bias=bcol, scale=0.125), R=[psb_, Bb], W=[p1b])
                            p2, p2b = pt2r.next()
                            P.op(alt(["dve", "dve", "pool"]), lambda E, p1=p1, p2=p2, nq=nq, qq0=qq0, ebt=ebt: E.tensor_tensor(out=p2[:, 0:nq], in0=p1[:, 0:nq], in1=ebt[:, qq0:qq0 + nq], op=ALU.mult), R=[p1b, ebb], W=[p2b])

                            def st2(vt=vt, vb=vb, j=j, p2=p2, p2b=p2b, qq0=qq0, dil=dil, r=r, NB=NQ // 128, nblk=L // 128, groups=groups):
                                for i in range(NB):
                                    b_ = j - (NB - 1) + i
                                    if b_ < 0 or b_ >= nblk:
                                        continue
                                    g_ = b_ // 4
                                    if g_ not in groups:
                                        groups[g_] = gr_rot.next()
                                        po, pob = groups[g_]
                                        P.op("pe", lambda E, po=po: E.matmul(po[0:65, 0:512], zb[0:1, 0:65], zb[0:1, 0:512], start=True, stop=False), R=[B_zb], W=[pob])
                                    po, pob = groups[g_]
                                    c0 = 128 * i - qq0
                                    col = (b_ % 4) * 128
                                    P.op("pe", lambda E, po=po, vt=vt, j=j, p2=p2, c0=c0, col=col, i=i, NB=NB: E.matmul(po[0:65, col:col + 128], vt[:, j, :], p2[:, c0:c0 + 128], start=False, stop=True),
                                         R=[vb, p2b], W=[pob])
                                    if i == 0 and (b_ % 4 == 3 or b_ == nblk - 1):
                                        n_ = (b_ % 4 + 1) * 128
                                        qs_ = (512 * g_) * dil + r
                                        P.op("dve", lambda E, po=po, qs_=qs_, n_=n_, dil=dil: E.tensor_tensor(out=acc[:, qs_:qs_ + (n_ - 1) * dil + 1:dil], in0=acc[:, qs_:qs_ + (n_ - 1) * dil + 1:dil], in1=po[0:65, 0:n_], op=ALU.add),
                                             R=[B_acc, pob], W=[B_acc])
                                        del groups[g_]
                            pend.append(st2)
                            while len(pend) > LA:
                                pend.pop(0)()
                while pend:
                    pend.pop(0)()
                if sinkc is not None:
                    P.op("dve", lambda E, sinkc=sinkc: E.tensor_scalar(out=acc[64:65, :], in0=acc[64:65, :], scalar1=sinke[64:65, sinkc:sinkc + 1], scalar2=None, op0=ALU.add), R=[B_acc, B_sinke], W=[B_acc])
                for c in range(T // 512):
                    cs = slice(c * 512, (c + 1) * 512)
                    P.op("dve", lambda E, cs=cs: E.reciprocal(racc[64:65, :], acc[64:65, cs]), R=[B_acc], W=[B_racc])
                    pb_, pbb_ = gr_rot.next()
                    P.op("pe", lambda E, pb_=pb_: E.matmul(pb_[0:64, :], sel[0:65, :], racc[0:65, :], start=True, stop=True), R=[B_racc, B_sel], W=[pbb_])
                    P.op("dve", lambda E, pb_=pb_, cs=cs: E.tensor_tensor(out=mo[:, cs], in0=acc[0:64, cs], in1=pb_[0:64, :], op=ALU.mult), R=[B_acc, pbb_], W=[B_mo])
                dma("pool", mixT_d[mrow:mrow + 64, :], mo[:], [B_mo], [B_mix])

    mixT_v = mixT_d.rearrange("(k p) t -> p k t", p=128)

    def P3(l, last):
        with Scope():
            xt, xb = P.sb("xt3", [128, KC, ST], F32), Buf("xt3")
            mt, mb = P.sb("mt3", [128, KC, ST], BF16), Buf("mt3")
            h2, h2b = P.sb("h2", [128, KC, ST], BF16), Buf("h2")
            actT, actb = P.sb("actT", [128, FC, ST], BF16), Buf("actT")
            wfr = rot("wf3", 4, [128, KC * 128], BF16)
            wdr = rot("wd3", 2, [128, FC * 128], BF16)
            sqr = rot("sq3", 2, [128, ST], BF16)
            rsr = rot("rs3", 1, [128, ST], F32)
            sgr = rot("sg3", 2, [128, ST], F32)
            yor = rot("yo3", 2, [128, D], F32)
            for j in range(NS):
                ts = slice(j * ST, (j + 1) * ST)
                dma("sp", xt[:], xT_v[:, :, ts], [B_xT[j]], [xb])
                dma("sp", mt[:], mixT_v[:, :, ts], [B_mix], [mb])
                for oc in range(16):
                    wf, wb = wfr.next()
                    dma("sp", wf[:], wOb[l][oc], [B_w], [wb])
                    pt, pb = psb.next()
                    for kc in range(KC):
                        P.op("pe", lambda E, pt=pt, wf=wf, kc=kc: E.matmul(pt[:, 0:ST], wf[:, kc * 128:(kc + 1) * 128], mt[:, kc, :], start=(kc == 0), stop=(kc == KC - 1)), R=[wb, mb], W=[pb])
                    P.op("dve", lambda E, pt=pt, oc=oc: E.tensor_tensor(out=xt[:, oc, :], in0=xt[:, oc, :], in1=pt[:, 0:ST], op=ALU.add), R=[xb, pb], W=[xb])
                rs, rb = rms_stats(xt, xb, sqr, rsr)
                for kc in range(KC):
                    P.op(alt(["dve", "pool"]), lambda E, kc=kc, rs=rs: E.tensor_tensor(out=h2[:, kc, :], in0=xt[:, kc, :], in1=rs[:], op=ALU.mult), R=[xb, rb], W=[h2b])
                for fc in range(FC):
                    wg, wgb = wfr.next()
                    dma("sp", wg[:], wGb[l][fc], [B_w], [wgb])
                    wu, wub = wfr.next()
                    dma("sp", wu[:], wUb[l][fc], [B_w], [wub])
                    pg, pgb = psb.next()
                    for kc in range(KC):
                        P.op("pe", lambda E, pg=pg, wg=wg, kc=kc: E.matmul(pg[:, 0:ST], wg[:, kc * 128:(kc + 1) * 128], h2[:, kc, :], start=(kc == 0), stop=(kc == KC - 1)), R=[wgb, h2b], W=[pgb])
                    pu, pub = psb.next()
                    for kc in range(KC):
                        P.op("pe", lambda E, pu=pu, wu=wu, kc=kc: E.matmul(pu[:, 0:ST], wu[:, kc * 128:(kc + 1) * 128], h2[:, kc, :], start=(kc == 0), stop=(kc == KC - 1)), R=[wub, h2b], W=[pub])
                    sg, sgb = sgr.next()
                    P.op("act", lambda E, pg=pg, sg=sg: E.activation(out=sg[:], in_=pg[:, 0:ST], func=AF.Silu), R=[pgb], W=[sgb])
                    P.op("dve", lambda E, pu=pu, sg=sg, fc=fc: E.tensor_tensor(out=actT[:, fc, :], in0=sg[:], in1=pu[:, 0:ST], op=ALU.mult), R=[sgb, pub], W=[actb])
                for oc in range(16):
                    wd, wdb = wdr.next()
                    dma("sp", wd[:], wDb[l][oc], [B_w], [wdb])
                    pt, pb = psb.next()
                    for fc in range(FC):
                        P.op("pe", lambda E, pt=pt, wd=wd, fc=fc: E.matmul(pt[:, 0:ST], wd[:, fc * 128:(fc + 1) * 128], actT[:, fc, :], start=(fc == 0), stop=(fc == FC - 1)), R=[wdb, actb], W=[pb])
                    P.op("dve", lambda E, pt=pt, oc=oc: E.tensor_tensor(out=xt[:, oc, :], in0=xt[:, oc, :], in1=pt[:, 0:ST], op=ALU.add), R=[xb, pb], W=[xb])
                if not last:
                    dma("pool", xT_v[:, :, ts], xt[:], [xb], [B_xT[j]])
                else:
                    rs, rb = rms_stats(xt, xb, sqr, rsr)
                    for kc in range(KC):
                        P.op("dve", lambda E, kc=kc, rs=rs: E.scalar_tensor_tensor(out=xt[:, kc, :], in0=xt[:, kc, :], scalar=gtab[:, 4 * KC + kc:4 * KC + kc + 1], in1=rs[:],
                                                                                                 op0=ALU.mult, op1=ALU.mult), R=[xb, rb, B_gtab], W=[xb])
                    for s in range(SUB):
                        yo, yob = yor.next()
                        for g in range(4):
                            pt, pb = psb.next()
                            for q in range(4):
                                kc = 4 * g + q
                                P.op("pe", lambda E, pt=pt, q=q, kc=kc, s=s: E.transpose(pt[:, q * 128:(q + 1) * 128], xt[:, kc, s * 128:(s + 1) * 128], ident[:]), R=[xb, B_ident], W=[pb])
                            evac(pt[:, :], yo[:, g * 512:(g + 1) * 512], [pb], [yob])
                        dma("pool", y_out[j * ST + s * 128:j * ST + (s + 1) * 128, :], yo[:], [yob], [B_y])

    NLR = int(os.environ.get('KNL', NL))
    for l in range(NLR):
        P.enabled = 'P1' in PH
        P1(l)
        P.enabled = 'X' in PH
        exchange_kv()
        P.enabled = 'R' in PH
        retention(l)
        P.enabled = 'A' in PH
        attention(l)
        P.enabled = 'P3' in PH
        P3(l, l == NL - 1)
    P.enabled = True
    P.barrier()
    P.emit()
    return nc, es


def _fblocks(W, cols_list):
    K = W.shape[0]
    kc = K // 128
    out = np.zeros((len(cols_list), 128, kc * 128), np.float32)
    Wr = W.reshape(kc, 128, W.shape[1])
    for i, cols in enumerate(cols_list):
        cols = np.asarray(cols)
        blk = np.zeros((kc, 128, 128), np.float32)
        ok = cols >= 0
        blk[:, :, ok] = Wr[:, :, cols[ok]]
        out[i] = blk.transpose(1, 0, 2).reshape(128, kc * 128)
    return out


def _tblocks(W, cols):
    K = W.shape[0]
    kc = K // 128
    cols = np.asarray(cols)
    n = len(cols) // 512
    Wc = np.zeros((K, len(cols)), np.float32)
    ok = cols >= 0
    Wc[:, ok] = W[:, cols[ok]]
    return np.ascontiguousarray(Wc.reshape(kc, 128, n, 512).transpose(2, 1, 0, 3)).reshape(n, 128, kc * 512)


_CACHE = {}


def kernel(x_prompt, x_sample, rel_bias, norm1_g, w_in, ret_decay_fwd, ret_decay_bwd, attn_sink,
           w_out, norm2_g, w_gate, w_up, w_down, final_norm_g):
    f = lambda a: np.asarray(a, dtype=np.float32)
    x_prompt, x_sample = f(x_prompt), f(x_sample)
    T = x_prompt.shape[1]
    assert x_prompt.shape[0] == 4 and x_sample.shape[0] == 2 and x_sample.shape[1] == 2 * T
    if T not in _CACHE:
        _CACHE[T] = build(T)
    nc, es = _CACHE[T]
    w_in, w_out, w_gate, w_up, w_down = f(w_in), f(w_out), f(w_gate), f(w_up), f(w_down)
    ar = np.arange
    common = {}
    for l in range(NL):
        fl = [ar(oc * 128, (oc + 1) * 128) for oc in range(6)]
        fl += [768 + ar(oc * 128, (oc + 1) * 128) for oc in range(6)]
        fl += [3840 + ar(oc * 128, (oc + 1) * 128) for oc in range(6)]
        fl += [4608 + ar(oc * 128, (oc + 1) * 128) for oc in range(2)]
        for base in (2304, 2560):
            for g in range(3):
                cols = np.full(128, -1)
                nh = 3 if g < 2 else 2
                cols[:32 * nh] = base + g * 96 + ar(32 * nh)
                fl.append(cols)
        common["wFin%d" % l] = _fblocks(w_in[l], fl)
        tcols = np.concatenate([1536 + ar(768), 4864 + ar(256), 2816 + ar(512), 3328 + ar(512), 2560 + ar(256), np.full(256, -1)])
        common["wTin%d" % l] = _tblocks(w_in[l], tcols)
        common["wOut%d" % l] = _fblocks(w_out[l], [ar(oc * 128, (oc + 1) * 128) for oc in range(16)])
        common["wGa%d" % l] = _fblocks(w_gate[l], [ar(oc * 128, (oc + 1) * 128) for oc in range(FC)])
        common["wUp%d" % l] = _fblocks(w_up[l], [ar(oc * 128, (oc + 1) * 128) for oc in range(FC)])
        common["wDn%d" % l] = _fblocks(w_down[l], [ar(oc * 128, (oc + 1) * 128) for oc in range(16)])
    g1, g2, gf = f(norm1_g), f(norm2_g), f(final_norm_g)
    gt = [g1[0], g1[1], g2[0], g2[1], gf]
    common["gtab"] = np.concatenate([g.reshape(KC, 128).T for g in gt], axis=1).astype(np.float32)
    df, db = f(ret_decay_fwd), f(ret_decay_bwd)
    dec = np.ones((128, NL * 6), np.float32) * 8.0
    dec8 = np.zeros((128, NL * 16), np.float32)
    for l in range(NL):
        for g in range(3):
            for dr, dd in enumerate((df, db)):
                for p in range(96):
                    h = g * 3 + p // 32
                    if h < 8:
                        dec[p, l * 6 + g * 2 + dr] = dd[l, h]
        for dr, dd in enumerate((df, db)):
            dec8[:, (l * 2 + dr) * 8:(l * 2 + dr) * 8 + 8] = dd[l][None, :]
    common["dec"] = dec
    common["dec8"] = dec8
    common["sink"] = np.tile(f(attn_sink).reshape(1, NL * 12), (128, 1))
    common["relb"] = f(rel_bias)
    common.update(host_consts(T))
    if SMALLW:
        for k_ in list(common):
            if k_[:2] in ('wF', 'wT', 'wO', 'wG', 'wU', 'wD'):
                common[k_] = np.ascontiguousarray(common[k_][:1])
    in_maps = []
    for c in range(8):
        m = dict(common)
        if c < 4:
            m["x"] = np.ascontiguousarray(x_prompt[c])
            lc, rc, pos0 = 0.0, 0.0, 0
        else:
            sq, hf = divmod(c - 4, 2)
            m["x"] = np.ascontiguousarray(x_sample[sq, hf * T:(hf + 1) * T])
            lc, rc, pos0 = float(hf == 1), float(hf == 0), hf * T
        fl_ = np.zeros((128, 8), np.float32)
        nl, nr = (0.0 if lc else NEGB), (0.0 if rc else NEGB)
        fl_[0:64, 0] = nl
        fl_[64:128, 1] = nr
        fl_[:, 2] = nl
        fl_[:, 3] = nr
        fl_[:, 4] = lc
        fl_[:, 5] = rc
        m["flags"] = fl_
        cF, sF, cT, sT = rope_tables(pos0, T)
        m["cF"], m["sF"], m["cT"], m["sT"] = cF, sF, cT, sT
        in_maps.append(m)
    res = run_bass_kernel_spmd(nc, in_maps, core_ids=list(range(8)))
    if os.environ.get('KDBG'):
        kernel.dbg = res.results
    ys = [np.asarray(r["y"], dtype=np.float32) for r in res.results]
    y_prompt = np.stack(ys[0:4], axis=0)
    y_sample = np.stack([np.concatenate(ys[4:6], axis=0), np.concatenate(ys[6:8], axis=0)], axis=0)
    return (y_prompt, y_sample)
```

```python
import math
import os
from contextlib import ExitStack
import numpy as np
import ml_dtypes
import concourse.bass as bass
import concourse.mybir as mybir
from concourse.bass_utils import run_bass_kernel_spmd

F32 = mybir.dt.float32
BF16 = mybir.dt.bfloat16
AF = mybir.ActivationFunctionType
ALU = mybir.AluOpType
AX = mybir.AxisListType

D = 2048
KC = 16
DFF = 5632
FC = 44
NL = 2
ST = 512
HA = 1024
HC = 128
EPS = 1e-6
GN_EPS = 1e-5
NEGB = -30000.0
DILS = (1, 4, 16)


SCOPE = [None]


class SemSlot:
    __slots__ = ("sem", "cnt")


class Buf:
    __slots__ = ("name", "writers", "readers", "sem", "scope", "war")

    def __init__(self, name):
        self.name = name
        self.writers = {}
        self.readers = {}
        self.sem = None
        self.war = set()
        self.scope = SCOPE[0]


class Op:
    __slots__ = ("eng", "fn", "deps", "flag", "done", "dma", "inc")


class Prog:
    def __init__(self, nc, es):
        self.nc = nc
        self.es = es
        self.ges = es
        self.ops = []
        self.engs = {"pe": nc.tensor, "act": nc.scalar, "dve": nc.vector, "pool": nc.gpsimd, "sp": nc.sync}
        self.esem = {e: es.enter_context(nc.semaphore("s_" + e)) for e in ("pe", "act", "dve", "pool")}
        self.nsem = 0
        self.last = {}
        self.free_sems = []
        self.enabled = True

    def sb(self, name, shape, dt):
        self.nsem += 1
        return self.es.enter_context(self.nc.sbuf_tensor("s%d_%s" % (self.nsem, name), list(shape), dt))

    def ps(self, name, shape, dt=F32):
        self.nsem += 1
        return self.es.enter_context(self.nc.psum_tensor("p%d_%s" % (self.nsem, name), list(shape), dt))

    def op(self, eng, fn, R=(), W=(), dma=False, inc=16):
        if not self.enabled:
            return None
        o = Op()
        o.eng, o.fn, o.flag, o.dma, o.inc, o.done = eng, fn, False, dma, inc, None
        deps = set()
        for b in R:
            deps.update(b.writers.values())
        for b in W:
            if b.readers:
                b.war = set(b.readers.values()) | set(b.writers.values())
                b.writers = {}
                b.readers = {}
            deps.update(b.war)
        if dma:
            b0 = W[0]
            if b0.sem is None:
                if self.free_sems:
                    b0.sem = self.free_sems.pop()
                else:
                    sl = SemSlot()
                    sl.sem = self.ges.enter_context(self.nc.semaphore("d%d" % self.nsem))
                    sl.cnt = 0
                    self.nsem += 1
                    b0.sem = sl
                if b0.scope is not None:
                    b0.scope.append(b0.sem)
            b0.sem.cnt += inc
            o.done = (b0.sem.sem, b0.sem.cnt)
            key = id(b0.sem.sem)
        else:
            key = eng
        for b in R:
            b.readers[key] = o
        for b in W:
            b.writers[key] = o
        deps.discard(o)
        o.deps = deps
        self.ops.append(o)
        self.last[key] = o
        return o

    def barrier(self):
        if not self.enabled:
            return
        allops = list(self.last.values())
        for e in ("pe", "act", "dve", "pool", "sp"):
            o = Op()
            o.eng, o.fn, o.flag, o.dma, o.inc, o.done = e, (lambda E: None), False, False, 16, None
            o.deps = set(allops)
            self.ops.append(o)

    def emit(self):
        for o in self.ops:
            for d in o.deps:
                d.flag = True
        cnt = {e: 0 for e in self.esem}
        for o in self.ops:
            if (not o.dma) and o.flag:
                cnt[o.eng] += 1
                o.done = (self.esem[o.eng], cnt[o.eng])
        waited = {e: {} for e in self.engs}
        for o in self.ops:
            E = self.engs[o.eng]
            need = {}
            for d in o.deps:
                if (not d.dma) and d.eng == "pe" and o.eng == "pe":
                    continue
                sem, val = d.done
                k = id(sem)
                if k not in need or need[k][1] < val:
                    need[k] = (sem, val)
            wd = waited[o.eng]
            for k, (sem, val) in need.items():
                if wd.get(k, 0) >= val:
                    continue
                E.wait_ge(sem, val)
                wd[k] = val
            ins = o.fn(E)
            if ins is None:
                continue
            if o.dma:
                ins.then_inc(o.done[0], o.inc)
            elif o.flag:
                ins.then_inc(o.done[0], 1)


class Rot:
    def __init__(self, items):
        self.items = items
        self.i = 0

    def next(self):
        it = self.items[self.i % len(self.items)]
        self.i += 1
        return it


def t5_bucket(rel):
    nb = 16
    max_exact = 8
    ret = np.where(rel > 0, nb, 0)
    n = np.abs(rel)
    nf = np.maximum(n, 1).astype(np.float32)
    large = max_exact + (np.log(nf / max_exact) / math.log(1024 / max_exact) * (nb - max_exact)).astype(np.int32)
    large = np.minimum(large, nb - 1)
    return (ret + np.where(n < max_exact, n, large)).astype(np.int32)


VARIANTS = [(1, -64, -128, 256, 64), (4, -64, -128, 256, 64), (16, -64, -128, 256, 64), (1, -128, -256, 384, 128)]
NM = 512


def host_consts(T):
    c = {}
    oh = np.zeros((32, 4, NM), np.float32)
    masks = np.zeros((4, 128, 384), np.float32)
    for v, (dil, koff, qoff, NQ, rad) in enumerate(VARIANTS):
        m = np.arange(NM)
        off = (127 - m) + (koff - qoff)
        b = t5_bucket(off * dil)
        oh[b, v, m] = 1.0
        kk = np.arange(128)[:, None]
        qq = np.arange(NQ)[None, :]
        o2 = kk - qq + (koff - qoff)
        masks[v, :, :NQ] = (np.abs(o2) <= rad).astype(np.float32)
    c["oh"] = oh.reshape(32, 4 * NM)
    c["amask"] = np.ascontiguousarray(masks.transpose(1, 0, 2)).reshape(128, 4 * 384)
    c["ident"] = np.eye(128, dtype=np.float32)
    c["jrev"] = np.ascontiguousarray(np.eye(128, dtype=np.float32)[::-1])
    perm = np.zeros((128, 128), np.float32)
    for p in range(128):
        h, i = divmod(p, 32)
        perm[32 * h + (i + 16) % 32, p] = 1.0
    c["perm"] = perm
    sel = np.zeros((128, 64), np.float32)
    sel[64, :] = 1.0
    c["sel"] = sel
    s = np.arange(128)[:, None]
    t = np.arange(128)[None, :]
    rt = np.zeros((128, 4 * 128 + 4), np.float32)
    rt[:, 0:128] = np.maximum(t - s, 0)
    rt[:, 128:256] = (t >= s)
    rt[:, 256:384] = np.maximum(s - t, 0)
    rt[:, 384:512] = (s > t)
    rt[:, 512] = 127 - np.arange(128)
    rt[:, 513] = np.arange(128)
    c["rtab"] = rt
    qe = np.zeros((128, 256), np.float32)
    qe[:, 0:128] = np.arange(128)[None, :] + 1.0
    qe[:, 128:256] = 128.0 - np.arange(128)[None, :]
    c["qexp"] = qe
    nn = np.arange(T // 128, dtype=np.float32)
    c["nidx"] = np.tile(np.concatenate([nn, nn[::-1]])[None, :], (128, 1))
    return c


def rope_tables(pos0, T):
    half = 16
    freqs = (10000.0 ** (-np.arange(half, dtype=np.float32) / half)).astype(np.float32)
    pos = (pos0 + np.arange(T)).astype(np.float32)
    ang = (pos[:, None] * freqs[None]).astype(np.float32)
    cos = np.cos(ang).astype(np.float32)
    sin = np.sin(ang).astype(np.float32)
    cF = np.zeros((128, T), np.float32)
    sF = np.zeros((128, T), np.float32)
    for p in range(128):
        i = p % 32
        cF[p] = cos[:, i % 16]
        sF[p] = -sin[:, i % 16] if i < 16 else sin[:, i % 16]
    cT = np.tile(cos, (1, 8))
    sT = np.tile(sin, (1, 8))
    return cF, sF, cT, sT


WIN_F = 26
WIN_T = 5


SMALLW = bool(int(os.environ.get('KSMALLW', '0')))


def build(T):
    nb_ = (lambda n: 1) if SMALLW else (lambda n: n)
    nc = bass.Bass("TRN2", target_bir_lowering=False)
    es = ExitStack()
    P = Prog(nc, es)
    NT = T // 128
    NS = T // ST
    SUB = ST // 128
    TA = HA + T + HA
    TCx = HC + T + HC

    def din(name, shape, dt=F32):
        return nc.dram_tensor(name, list(shape), dt, kind="ExternalInput").ap()

    DBG = set(os.environ.get('KDBG', '').split(','))

    def dscr(name, shape, dt):
        if name in DBG:
            return nc.dram_tensor(name, list(shape), dt, kind="ExternalOutput").ap()
        return nc.dram_tensor(name, list(shape), dt).ap()

    x_in = din("x", [T, D])
    y_out = nc.dram_tensor("y", [T, D], F32, kind="ExternalOutput").ap()
    wFin = [din("wFin%d" % l, [nb_(WIN_F), 128, KC * 128]) for l in range(NL)]
    wTin = [din("wTin%d" % l, [nb_(WIN_T), 128, KC * 512]) for l in range(NL)]
    wOut = [din("wOut%d" % l, [nb_(16), 128, KC * 128]) for l in range(NL)]
    wGa = [din("wGa%d" % l, [nb_(FC), 128, KC * 128]) for l in range(NL)]
    wUp = [din("wUp%d" % l, [nb_(FC), 128, KC * 128]) for l in range(NL)]
    wDn = [din("wDn%d" % l, [nb_(16), 128, FC * 128]) for l in range(NL)]
    gtab_in = din("gtab", [128, 5 * KC])
    dec_in = din("dec", [128, NL * 3 * 2])
    dec8_in = din("dec8", [128, NL * 2 * 8])
    sink_in = din("sink", [128, NL * 12])
    relb_in = din("relb", [32, 24])
    flags_in = din("flags", [128, 8])
    oh_in = din("oh", [32, 4 * NM])
    amask_in = din("amask", [128, 4 * 384])
    ident_in = din("ident", [128, 128])
    jrev_in = din("jrev", [128, 128])
    perm_in = din("perm", [128, 128])
    sel_in = din("sel", [128, 64])
    rtab_in = din("rtab", [128, 516])
    qexp_in = din("qexp", [128, 256])
    nidx_in = din("nidx", [128, 2 * NT])
    cF_in = din("cF", [128, T])
    sF_in = din("sF", [128, T])
    cT_in = din("cT", [T, 128])
    sT_in = din("sT", [T, 128])

    wFb = [dscr("wFb%d" % l, [WIN_F, 128, KC * 128], BF16) for l in range(NL)]
    wTb = [dscr("wTb%d" % l, [WIN_T, 128, KC * 512], BF16) for l in range(NL)]
    wOb = [dscr("wOb%d" % l, [16, 128, KC * 128], BF16) for l in range(NL)]
    wGb = [dscr("wGb%d" % l, [FC, 128, KC * 128], BF16) for l in range(NL)]
    wUb = [dscr("wUb%d" % l, [FC, 128, KC * 128], BF16) for l in range(NL)]
    wDb = [dscr("wDb%d" % l, [16, 128, FC * 128], BF16) for l in range(NL)]
    B_w = Buf("wscratch")
    xT_d = dscr("xT_d", [D, T], F32)
    B_xT = [Buf("xT%d" % j) for j in range(NS)]
    QA_d = dscr("QA_d", [768, T], BF16)
    KA_d = dscr("KA_d", [768, TA], BF16)
    VA_d = dscr("VA_d", [TA, 780], BF16)
    QC_d = dscr("QC_d", [768, T], BF16)
    KCx_d = dscr("KC_d", [256, TCx], BF16)
    VC_d = dscr("VC_d", [TCx, 260], BF16)
    QR_d = dscr("QR_d", [384, T], BF16)
    KR_d = dscr("KR_d", [384, T], BF16)
    VR_d = dscr("VR_d", [T, 512], BF16)
    GS_d = dscr("GS_d", [T, 512], BF16)
    KV_d = dscr("KV_d", [96, 6, NT, 64], F32)
    mixT_d = dscr("mixT_d", [D, T], BF16)
    EB_d = dscr("EB_d", [4, 12, 128, 384], BF16)
    ub_d = dscr("ub_d", [4, 24, NM], F32)
    B_QA, B_KA, B_VA, B_QC, B_KC, B_VC = (Buf(n) for n in ("QA", "KA", "VA", "QC", "KC", "VC"))
    B_QR, B_KR, B_VR, B_GS, B_KV, B_mix, B_EB, B_ub = (Buf(n) for n in ("QR", "KR", "VR", "GS", "KV", "mix", "EB", "ub"))
    PK_ROWS = 768 * 2 + 2 * 1024 + 256 * 2 + 2 * 128
    SEC_ROWS = [768, 768, 1024, 1024, 768]
    pins = [dscr("pin%d" % i, [r, 1024], BF16) for i, r in enumerate(SEC_ROWS)]
    pouts = [dscr("pout%d" % i, [2 * r, 1024], BF16) for i, r in enumerate(SEC_ROWS)]
    B_pins = [Buf("pin%d" % i) for i in range(5)]
    B_pouts = [Buf("pout%d" % i) for i in range(5)]
    sin_ = dscr("sin_", [96, 6 * 64], F32)
    sout = dscr("sout", [2 * 96, 6 * 64], F32)
    B_sin, B_sout = Buf("sin"), Buf("sout")
    B_y = Buf("y")

    def cp(E, out, in_):
        return E.tensor_copy(out, in_)

    def dma(q, out, in_, R, W):
        return P.op(q, lambda E, o=out, i=in_: E.dma_start(out=o, in_=i), R=R, W=W, dma=True)

    def const(name, src, shape, dt=F32):
        t = P.sb(name, shape, dt)
        b = Buf(name)
        dma("sp", t[:], src, [], [b])
        return t, b

    ident, B_ident = const("ident", ident_in[:, :], [128, 128])
    jrev, B_jrev = const("jrev", jrev_in[:, :], [128, 128])
    perm, B_perm = const("perm", perm_in[:, :], [128, 128])
    sel, B_sel = const("sel", sel_in[:, :], [128, 64])
    gtab, B_gtab = const("gtab", gtab_in[:, :], [128, 5 * KC])
    flags, B_flags = const("flags", flags_in[:, :], [128, 8])
    decp, B_decp = const("decp", dec_in[:, :], [128, NL * 6])
    dec8, B_dec8 = const("dec8", dec8_in[:, :], [128, NL * 16])
    sinkt, B_sink = const("sinkt", sink_in[:, :], [128, NL * 12])
    rtab, B_rtab = const("rtab", rtab_in[:, :], [128, 516])
    qexp, B_qexp = const("qexp", qexp_in[:, :], [128, 256])
    nidx, B_nidx = const("nidx", nidx_in[:, :], [128, 2 * NT])
    ones_bf = P.sb("ones_bf", [128, 128], BF16)
    B_ones = Buf("ones")
    P.op("pool", lambda E: E.memset(ones_bf[:], 1.0), W=[B_ones])
    identb = P.sb("identb", [128, 128], BF16)
    B_identb = Buf("identb")
    P.op("dve", lambda E: cp(E, identb[:], ident[:]), R=[B_ident], W=[B_identb])
    zcol = P.sb("zcol", [128, 1], F32)
    B_zcol = Buf("zcol")
    P.op("pool", lambda E: E.memset(zcol[:], 0.0), W=[B_zcol])

    psb = Rot([(P.ps("ps%d" % i, [128, 512]), Buf("ps%d" % i)) for i in range(7)])
    psbf = Rot([(P.ps("psbf", [128, 1024], BF16), Buf("psbf"))])
    ones32 = P.sb("ones32", [128, 32], F32)
    B_ones32 = Buf("ones32")
    P.op("pool", lambda E: E.memset(ones32[:], 1.0), W=[B_ones32])

    def lgcalc(name, src, Bsrc, n):
        t = P.sb(name, [128, n], F32)
        b = Buf(name)
        P.op("act", lambda E: E.activation(out=t[:], in_=src[:], func=AF.Exp, scale=-math.log(2.0)), R=[Bsrc], W=[b])
        P.op("dve", lambda E: E.tensor_scalar(out=t[:], in0=t[:], scalar1=-1.0, scalar2=1.0, op0=ALU.mult, op1=ALU.add), R=[b], W=[b])
        P.op("act", lambda E: E.activation(out=t[:], in_=t[:], func=AF.Ln), R=[b], W=[b])
        return t, b

    lgp, B_lgp = lgcalc("lgp", decp, B_decp, NL * 6)
    lg8, B_lg8 = lgcalc("lg8", dec8, B_dec8, NL * 16)
    sinke = P.sb("sinke", [128, NL * 12], F32)
    B_sinke = Buf("sinke")
    P.op("act", lambda E: E.activation(out=sinke[:], in_=sinkt[:], func=AF.Exp), R=[B_sink], W=[B_sinke])

    PH = set(os.environ.get('KPH', 'B,W,0,P1,X,R,A,P3').split(','))
    P.enabled = 'B' in PH
    with ExitStack() as bes:
        sv = P.es
        P.es = bes
        relb, B_relb = const("relb", relb_in[:, :], [32, 24])
        oh, B_oh = const("oh", oh_in[:, :], [32, 4 * NM])
        amask, B_amask = const("amask", amask_in[:, :], [128, 4 * 384])
        usb = P.sb("usb", [24, NM], F32)
        B_usb = Buf("usb")
        hk = Rot([(P.sb("hk%d" % i, [128, 384], F32), Buf("hk%d" % i)) for i in range(2)])
        ee = Rot([(P.sb("ee%d" % i, [128, 384], F32), Buf("ee%d" % i)) for i in range(2)])
        eb = Rot([(P.sb("eb%d" % i, [128, 384], BF16), Buf("eb%d" % i)) for i in range(2)])
        for v, (dil, koff, qoff, NQ, rad) in enumerate(VARIANTS):
            pt, pb = psb.next()
            P.op("pe", lambda E, pt=pt, v=v: E.matmul(pt[0:24, 0:NM], relb[:, :], oh[:, v * NM:(v + 1) * NM], start=True, stop=True),
                 R=[B_relb, B_oh], W=[pb])
            P.op("act", lambda E, pt=pt: E.activation(out=usb[:], in_=pt[0:24, 0:NM], func=AF.Copy), R=[pb], W=[B_usb])
            dma("sp", ub_d[v], usb[:], [B_usb], [B_ub])
            for h in range(12):
                hh = h if v < 3 else 12 + h
                ht, hb = hk.next()
                src = bass.AP(ub_d.tensor, (v * 24 + hh) * NM, [[1, 128], [1, NQ]])
                dma("sp", ht[:, 0:NQ], src, [B_ub], [hb])
                pt, pb = psb.next()
                P.op("pe", lambda E, pt=pt, ht=ht, NQ=NQ: E.matmul(pt[:, 0:NQ], jrev[:, :], ht[:, 0:NQ], start=True, stop=True),
                     R=[hb, B_jrev], W=[pb])
                et, ebf = ee.next()
                P.op("act", lambda E, pt=pt, et=et, NQ=NQ: E.activation(out=et[:, 0:NQ], in_=pt[:, 0:NQ], func=AF.Exp), R=[pb], W=[ebf])
                bt, bb = eb.next()
                P.op("dve", lambda E, et=et, bt=bt, NQ=NQ, v=v: E.tensor_tensor(out=bt[:, 0:NQ], in0=et[:, 0:NQ], in1=amask[:, v * 384:v * 384 + NQ], op=ALU.mult),
                     R=[ebf, B_amask], W=[bb])
                dma("pool", EB_d[v, h, :, 0:NQ], bt[:, 0:NQ], [bb], [B_EB])
        P.es = sv

    class Scope:
        def __enter__(self):
            P.barrier()
            self.sems = []
            SCOPE[0] = self.sems
            self.sv = P.es
            self.st = ExitStack()
            P.es = self.st
            return self

        def __exit__(self, *a):
            P.barrier()
            self.st.close()
            P.es = self.sv
            SCOPE[0] = None
            P.free_sems.extend(self.sems)
            return False

    def rot(name, n, shape, dt, ps=False):
        return Rot([((P.ps if ps else P.sb)("%s%d" % (name, i), shape, dt), Buf("%s%d" % (name, i))) for i in range(n)])

    cnt = [0]

    def alt(engs):
        cnt[0] += 1
        return engs[cnt[0] % len(engs)]

    P.enabled = 'W' in PH
    with Scope():
        wst = rot("wst", 3, [128, 8192], F32)
        wsb = rot("wsb", 3, [128, 8192], BF16)
        for l in range(NL):
            fams = [(wFin[l], wFb[l], WIN_F, KC, 128, l * KC), (wTin[l], wTb[l], WIN_T, KC, 512, l * KC),
                    (wOut[l], wOb[l], 16, KC, 128, None), (wGa[l], wGb[l], FC, KC, 128, (2 + l) * KC),
                    (wUp[l], wUb[l], FC, KC, 128, (2 + l) * KC), (wDn[l], wDb[l], 16, FC, 128, None)]
            for (src, dst, nblk, nk, ncol, gcol) in fams:
                n = nk * ncol
                for blk in range(nblk):
                    a, ab = wst.next()
                    b, bb = wsb.next()
                    dma("sp", a[:, 0:n], src[0 if SMALLW else blk], [], [ab])
                    eng = alt(["dve", "act"])
                    if gcol is None:
                        if eng == "dve":
                            P.op("dve", lambda E, a=a, b=b, n=n: E.tensor_copy(b[:, 0:n], a[:, 0:n]), R=[ab], W=[bb])
                        else:
                            P.op("act", lambda E, a=a, b=b, n=n: E.activation(out=b[:, 0:n], in_=a[:, 0:n], func=AF.Copy), R=[ab], W=[bb])
                    else:
                        for kc in range(nk):
                            sl = slice(kc * ncol, (kc + 1) * ncol)
                            gs_ = gtab[:, gcol + kc:gcol + kc + 1]
                            if eng == "dve":
                                P.op("dve", lambda E, a=a, b=b, sl=sl, gs_=gs_: E.tensor_scalar(out=b[:, sl], in0=a[:, sl], scalar1=gs_, scalar2=None, op0=ALU.mult),
                                     R=[ab, B_gtab], W=[bb])
                            else:
                                P.op("act", lambda E, a=a, b=b, sl=sl, gs_=gs_: E.activation(out=b[:, sl], in_=a[:, sl], func=AF.Copy, scale=gs_),
                                     R=[ab, B_gtab], W=[bb])
                    dma("pool", dst[blk], b[:, 0:n], [bb], [B_w])

    P.enabled = '0' in PH
    xT_v = xT_d.rearrange("(k p) t -> p k t", p=128)
    with Scope():
        xin = rot("xin", 2, [128, D], F32)
        xo = rot("xo", 2, [128, KC, 128], F32)
        for i in range(NT):
            a, ab = xin.next()
            o, ob = xo.next()
            dma("sp", a[:], x_in[i * 128:(i + 1) * 128, :], [], [ab])
            for g in range(4):
                pt, pb = psb.next()
                for q in range(4):
                    kc = 4 * g + q
                    P.op("pe", lambda E, pt=pt, a=a, q=q, kc=kc: E.transpose(pt[:, q * 128:(q + 1) * 128], a[:, kc * 128:(kc + 1) * 128], ident[:]),
                         R=[ab, B_ident], W=[pb])
                eng = alt(["dve", "act"])
                if eng == "dve":
                    P.op("dve", lambda E, pt=pt, o=o, g=g: E.tensor_copy(o[:, 4 * g:4 * g + 4, :], pt[:].rearrange("p (q t) -> p q t", q=4)), R=[pb], W=[ob])
                else:
                    P.op("act", lambda E, pt=pt, o=o, g=g: E.activation(out=o[:, 4 * g:4 * g + 4, :], in_=pt[:].rearrange("p (q t) -> p q t", q=4), func=AF.Copy), R=[pb], W=[ob])
            dma("pool", xT_v[:, :, i * 128:(i + 1) * 128], o[:], [ob], [B_xT[(i * 128) // ST]])

    def rms_stats(xt, xb, sqr, rsr):
        pt, pb = psb.next()
        for kc in range(KC):
            s_, sb_ = sqr.next()
            P.op("act", lambda E, s_=s_, kc=kc: E.activation(out=s_[:], in_=xt[:, kc, :], func=AF.Square), R=[xb], W=[sb_])
            P.op("pe", lambda E, s_=s_, kc=kc, pt=pt: E.matmul(pt[:, 0:ST], ones_bf[:], s_[:], start=(kc == 0), stop=(kc == KC - 1)),
                 R=[sb_, B_ones], W=[pb])
        r_, rb_ = rsr.next()
        P.op("dve", lambda E, r_=r_, pt=pt: E.tensor_scalar(out=r_[:], in0=pt[:, 0:ST], scalar1=1.0 / D, scalar2=EPS, op0=ALU.mult, op1=ALU.add), R=[pb], W=[rb_])
        P.op("act", lambda E, r_=r_: E.activation(out=r_[:], in_=r_[:], func=AF.Sqrt), R=[rb_], W=[rb_])
        P.op("dve", lambda E, r_=r_: E.reciprocal(r_[:], r_[:]), R=[rb_], W=[rb_])
        return r_, rb_

    def evac(pt_ap, out_ap, R, W, func=AF.Copy, eng=None):
        if eng is None:
            eng = alt(["act", "dve"])
        if eng == "dve" and func == AF.Copy:
            P.op("dve", lambda E: E.tensor_copy(out_ap, pt_ap), R=R, W=W)
        else:
            P.op("act", lambda E: E.activation(out=out_ap, in_=pt_ap, func=func), R=R, W=W)

    DKS = 32.0 ** -0.5
    HENG = os.environ.get('KHENG', 'dve,pool').split(',')
    OCS = set(int(v) for v in os.environ.get('KOCS', ','.join(str(i) for i in range(WIN_F))).split(','))
    KF = os.environ.get('KF', 'E,D').split(',')

    def P1(l):
        with Scope():
            xtr = rot("xt", 2, [128, KC, ST], F32)
            hTr = rot("hT", 1, [128, KC, ST], BF16)
            sqr = rot("sq", 2, [128, ST], BF16)
            rsr = rot("rs", 1, [128, ST], F32)
            wfr = rot("wf", 3, [128, KC * 128], BF16)
            wtr = rot("wt", 2, [128, KC * 512], BF16)
            otr = rot("ot", 3, [128, ST], BF16)
            y32r = rot("y32", 2, [128, ST], F32)
            t1r = rot("t1", 2, [128, ST], F32)
            t2r = rot("t2", 2, [128, ST], F32)
            cFt, B_cFt = P.sb("cFt", [128, ST], F32), Buf("cFt")
            sFt, B_sFt = P.sb("sFt", [128, ST], F32), Buf("sFt")
            cTt, B_cTt = P.sb("cTt", [128, SUB, 128], F32), Buf("cTt")
            sTt, B_sTt = P.sb("sTt", [128, SUB, 128], F32), Buf("sTt")
            vaug, B_vaug = P.sb("vaug", [128, SUB, 780], BF16), Buf("vaug")
            vcaug, B_vcaug = P.sb("vcaug", [128, SUB, 260], BF16), Buf("vcaug")
            vrt, B_vrt = P.sb("vrt", [128, SUB, 512], BF16), Buf("vrt")
            gst, B_gst = P.sb("gst", [128, SUB, 512], BF16), Buf("gst")
            kraw, B_kraw = P.sb("kraw", [128, 256], F32), Buf("kraw")
            kro, B_kro = P.sb("kro", [128, 256], F32), Buf("kro")
            kta, B_kta = P.sb("kta", [128, 128], F32), Buf("kta")
            ktb, B_ktb = P.sb("ktb", [128, 128], F32), Buf("ktb")
            kd, B_kd = P.sb("kd", [128, 2, 256], BF16), Buf("kd")
            kvt, B_kvt = P.sb("kvt", [96, 6, SUB, 64], F32), Buf("kvt")
            kdfull, B_kdf = P.sb("kdfull", [128, 2, 256], F32), Buf("kdfull")
            e8, B_e8 = P.sb("e8", [128, 16], F32), Buf("e8")
            P.op("pool", lambda E: E.memset(vaug[:], 1.0), W=[B_vaug])
            P.op("pool", lambda E: E.memset(vcaug[:], 1.0), W=[B_vcaug])
            P.op("pool", lambda E: E.memset(kvt[:], 0.0), W=[B_kvt])
            for dr in range(2):
                P.op("dve", lambda E, dr=dr: E.tensor_scalar(out=e8[:, dr * 8:(dr + 1) * 8], in0=lg8[:, (l * 2 + dr) * 8:(l * 2 + dr) * 8 + 8],
                                                            scalar1=rtab[:, 512 + dr:513 + dr], scalar2=None, op0=ALU.mult), R=[B_lg8, B_rtab], W=[B_e8])
            P.op("act", lambda E: E.activation(out=e8[:], in_=e8[:], func=AF.Exp), R=[B_e8], W=[B_e8])
            P.op("dve", lambda E: E.tensor_scalar(out=e8[:], in0=e8[:], scalar1=DKS, scalar2=None, op0=ALU.mult), R=[B_e8], W=[B_e8])
            for dr in range(2):
                for h in range(8):
                    P.op("act", lambda E, dr=dr, h=h: E.activation(out=kdfull[:, dr, h * 32:(h + 1) * 32], in_=ones32[:, :], func=AF.Copy,
                                                                  scale=e8[:, dr * 8 + h:dr * 8 + h + 1]), R=[B_e8, B_ones32], W=[B_kdf])
            vaug_v = vaug[:].rearrange("p s (h e) -> p s h e", e=65)
            vcaug_v = vcaug[:].rearrange("p s (h e) -> p s h e", e=65)
            for j in range(NS):
                ts = slice(j * ST, (j + 1) * ST)
                xt, xb = xtr.next()
                dma("sp", xt[:], xT_v[:, :, ts], [B_xT[j]], [xb])
                dma("sp", cFt[:], cF_in[:, ts], [], [B_cFt])
                dma("sp", sFt[:], sF_in[:, ts], [], [B_sFt])
                dma("sp", cTt[:], cT_in[ts, :].rearrange("(s p) f -> p s f", p=128), [], [B_cTt])
                dma("sp", sTt[:], sT_in[ts, :].rearrange("(s p) f -> p s f", p=128), [], [B_sTt])
                SP1 = set(os.environ.get('KP1', 'S,H,F,FR,T0,T1,T2,T3,T4,KV').split(','))
                en0 = P.enabled
                P.enabled = en0 and 'S' in SP1
                rs, rb = rms_stats(xt, xb, sqr, rsr)
                P.enabled = en0 and 'H' in SP1
                hT, hb = hTr.next()
                for kc in range(KC):
                    P.op(alt(HENG), lambda E, kc=kc, hT=hT, xt=xt, rs=rs: E.tensor_tensor(out=hT[:, kc, :], in0=xt[:, kc, :], in1=rs[:], op=ALU.mult),
                         R=[xb, rb], W=[hb])
                P.enabled = en0
                for oc in range(WIN_F):
                    P.enabled = en0 and (('F' in SP1) if oc < 20 else ('FR' in SP1)) and oc in OCS
                    wf, wb = wfr.next()
                    dma("sp", wf[:], wFb[l][oc], [B_w], [wb])
                    pt, pb = psb.next()
                    for kc in range(KC):
                        P.op("pe", lambda E, pt=pt, wf=wf, hT=hT, kc=kc: E.matmul(pt[:, 0:ST], wf[:, kc * 128:(kc + 1) * 128], hT[:, kc, :], start=(kc == 0), stop=(kc == KC - 1)),
                             R=[wb, hb], W=[pb])
                    ot, ob = otr.next()
                    if oc < 20:
                        if 'E' in KF:
                            evac(pt[:, 0:ST], ot[:], [pb], [ob])
                        if 'D' not in KF:
                            continue
                        if oc < 6:
                            dma("pool", QA_d[oc * 128:(oc + 1) * 128, ts], ot[:], [ob], [B_QA])
                        elif oc < 12:
                            dma("pool", KA_d[(oc - 6) * 128:(oc - 5) * 128, HA + j * ST:HA + (j + 1) * ST], ot[:], [ob], [B_KA])
                        elif oc < 18:
                            dma("pool", QC_d[(oc - 12) * 128:(oc - 11) * 128, ts], ot[:], [ob], [B_QC])
                        else:
                            dma("pool", KCx_d[(oc - 18) * 128:(oc - 17) * 128, HC + j * ST:HC + (j + 1) * ST], ot[:], [ob], [B_KC])
                    else:
                        y32, yb = y32r.next()
                        P.op("act", lambda E, y32=y32, pt=pt: E.activation(out=y32[:], in_=pt[:, 0:ST], func=AF.Copy), R=[pb], W=[yb])
                        pz, pzb = psb.next()
                        P.op("pe", lambda E, pz=pz, y32=y32: E.matmul(pz[:, 0:ST], perm[:, :], y32[:], start=True, stop=True), R=[yb, B_perm], W=[pzb])
                        t1, t1b = t1r.next()
                        t2, t2b = t2r.next()
                        P.op("pool", lambda E, t1=t1, y32=y32: E.tensor_tensor(out=t1[:], in0=y32[:], in1=cFt[:], op=ALU.mult), R=[yb, B_cFt], W=[t1b])
                        P.op("dve", lambda E, t2=t2, pz=pz: E.tensor_tensor(out=t2[:], in0=pz[:, 0:ST], in1=sFt[:], op=ALU.mult), R=[pzb, B_sFt], W=[t2b])
                        P.op("dve", lambda E, t1=t1, t2=t2, ot=ot: E.tensor_tensor(out=ot[:], in0=t1[:], in1=t2[:], op=ALU.add), R=[t1b, t2b], W=[ob])
                        if oc < 23:
                            dma("pool", QR_d[(oc - 20) * 128:(oc - 19) * 128, ts], ot[:], [ob], [B_QR])
                        else:
                            dma("pool", KR_d[(oc - 23) * 128:(oc - 22) * 128, ts], ot[:], [ob], [B_KR])
                for cc in (0, 1, 2, 3, 4):
                    P.enabled = en0 and ('T%d' % cc in SP1)
                    wt, wtb = wtr.next()
                    dma("sp", wt[:], wTb[l][cc], [B_w], [wtb])
                    for s in range(SUB):
                        pt, pb = psb.next()
                        for kc in range(KC):
                            P.op("pe", lambda E, pt=pt, wt=wt, hT=hT, kc=kc, s=s: E.matmul(pt[:, :], hT[:, kc, s * 128:(s + 1) * 128], wt[:, kc * 512:(kc + 1) * 512], start=(kc == 0), stop=(kc == KC - 1)),
                                 R=[wtb, hb], W=[pb])
                        if cc == 0:
                            evac(pt[:].rearrange("p (h e) -> p h e", e=64), vaug_v[:, s, 0:8, 0:64], [pb], [B_vaug])
                        elif cc == 1:
                            e_ = alt(["act", "dve"])
                            evac(pt[:, 0:256].rearrange("p (h e) -> p h e", e=64), vaug_v[:, s, 8:12, 0:64], [pb], [B_vaug], eng=e_)
                            evac(pt[:, 256:512].rearrange("p (h e) -> p h e", e=64), vcaug_v[:, s, :, 0:64], [pb], [B_vcaug], eng=e_)
                        elif cc == 2:
                            evac(pt[:, :], vrt[:, s, :], [pb], [B_vrt])
                        elif cc == 3:
                            P.op("act", lambda E, pt=pt, s=s: E.activation(out=gst[:, s, :], in_=pt[:, :], func=AF.Silu), R=[pb], W=[B_gst])
                        else:
                            P.op("act", lambda E, pt=pt: E.activation(out=kraw[:], in_=pt[:, 0:256], func=AF.Copy), R=[pb], W=[B_kraw])
                            kv4 = kraw[:].rearrange("p (h t i) -> p h t i", h=8, t=2)
                            ko4 = kro[:].rearrange("p (h t i) -> p h t i", h=8, t=2)
                            c3 = cTt[:, s, :].rearrange("p (h i) -> p h i", h=8)
                            s3 = sTt[:, s, :].rearrange("p (h i) -> p h i", h=8)
                            ta3 = kta[:].rearrange("p (h i) -> p h i", h=8)
                            tb3 = ktb[:].rearrange("p (h i) -> p h i", h=8)
                            RR = [B_kraw, B_cTt, B_sTt]
                            P.op("pool", lambda E, kv4=kv4, c3=c3, ta3=ta3: E.tensor_tensor(out=ta3, in0=kv4[:, :, 0, :], in1=c3, op=ALU.mult), R=RR, W=[B_kta])
                            P.op("pool", lambda E, kv4=kv4, s3=s3, tb3=tb3: E.tensor_tensor(out=tb3, in0=kv4[:, :, 1, :], in1=s3, op=ALU.mult), R=RR, W=[B_ktb])
                            P.op("pool", lambda E, ko4=ko4, ta3=ta3, tb3=tb3: E.tensor_tensor(out=ko4[:, :, 0, :], in0=ta3, in1=tb3, op=ALU.subtract), R=[B_kta, B_ktb], W=[B_kro])
                            P.op("pool", lambda E, kv4=kv4, s3=s3, ta3=ta3: E.tensor_tensor(out=ta3, in0=kv4[:, :, 0, :], in1=s3, op=ALU.mult), R=RR + [B_kro], W=[B_kta])
                            P.op("pool", lambda E, kv4=kv4, c3=c3, tb3=tb3: E.tensor_tensor(out=tb3, in0=kv4[:, :, 1, :], in1=c3, op=ALU.mult), R=RR + [B_kro], W=[B_ktb])
                            P.op("pool", lambda E, ko4=ko4, ta3=ta3, tb3=tb3: E.tensor_tensor(out=ko4[:, :, 1, :], in0=ta3, in1=tb3, op=ALU.add), R=[B_kta, B_ktb], W=[B_kro])
                            for dr in range(2):
                                P.op("dve", lambda E, dr=dr: E.tensor_tensor(out=kd[:, dr, :], in0=kro[:], in1=kdfull[:, dr, :], op=ALU.mult), R=[B_kro, B_kdf], W=[B_kd])
                            P.enabled = en0 and ('KV' in SP1) and ('T4' in SP1)
                            for dr in range(2):
                                pk, pkb = psb.next()
                                for g in range(3):
                                    nh = 3 if g < 2 else 2
                                    P.op("pe", lambda E, pk=pk, dr=dr, g=g, nh=nh, s=s: E.matmul(pk[0:32 * nh, g * 192:g * 192 + 64 * nh],
                                                                                                 kd[:, dr, g * 96:g * 96 + 32 * nh], vrt[:, s, g * 192:g * 192 + 64 * nh], start=True, stop=True),
                                         R=[B_kd, B_vrt], W=[pkb])
                                e_ = alt(["act", "dve"])
                                for g in range(3):
                                    nh = 3 if g < 2 else 2
                                    base = g * 192
                                    for jj in range(nh):
                                        evac(pk[32 * jj:32 * jj + 32, base + 64 * jj:base + 64 * jj + 64], kvt[32 * jj:32 * jj + 32, dr * 3 + g, s, :], [pkb], [B_kvt], eng=e_)
                    if cc == 1:
                        KT1 = os.environ.get('KT1', 'A,C')
                        if 'A' in KT1:
                            dma("pool", VA_d[HA + j * ST:HA + (j + 1) * ST, :].rearrange("(s p) c -> p s c", p=128), vaug[:], [B_vaug], [B_VA])
                        if 'C' in KT1:
                            dma("pool", VC_d[HC + j * ST:HC + (j + 1) * ST, :].rearrange("(s p) c -> p s c", p=128), vcaug[:], [B_vcaug], [B_VC])
                    elif cc == 2:
                        dma("pool", VR_d[ts, :].rearrange("(s p) c -> p s c", p=128), vrt[:], [B_vrt], [B_VR])
                    elif cc == 3:
                        dma("pool", GS_d[ts, :].rearrange("(s p) c -> p s c", p=128), gst[:], [B_gst], [B_GS])
                    elif cc == 4:
                        dma("pool", KV_d[:, :, j * SUB:(j + 1) * SUB, :], kvt[:], [B_kvt], [B_KV])
                P.enabled = en0

    PKR = PK_ROWS

    def exchange_kv():
        secs = [
            (768, [(slice(0, 768), slice(0, 1024), KA_d[:, HA + T - 1024:HA + T], B_KA)],
                  [(KA_d[:, 0:HA], 0, slice(0, 768), slice(0, 1024), B_KA)]),
            (768, [(slice(0, 768), slice(0, 1024), KA_d[:, HA:HA + 1024], B_KA)],
                  [(KA_d[:, HA + T:TA], 1, slice(0, 768), slice(0, 1024), B_KA)]),
            (1024, [(slice(0, 1024), slice(0, 780), VA_d[HA + T - 1024:HA + T, :], B_VA)],
                   [(VA_d[0:HA, :], 0, slice(0, 1024), slice(0, 780), B_VA)]),
            (1024, [(slice(0, 1024), slice(0, 780), VA_d[HA:HA + 1024, :], B_VA)],
                   [(VA_d[HA + T:TA, :], 1, slice(0, 1024), slice(0, 780), B_VA)]),
            (768, [(slice(0, 256), slice(0, 128), KCx_d[:, HC + T - 128:HC + T], B_KC),
                   (slice(256, 512), slice(0, 128), KCx_d[:, HC:HC + 128], B_KC),
                   (slice(512, 640), slice(0, 260), VC_d[HC + T - 128:HC + T, :], B_VC),
                   (slice(640, 768), slice(0, 260), VC_d[HC:HC + 128, :], B_VC)],
                  [(KCx_d[:, 0:HC], 0, slice(0, 256), slice(0, 128), B_KC),
                   (KCx_d[:, HC + T:TCx], 1, slice(256, 512), slice(0, 128), B_KC),
                   (VC_d[0:HC, :], 0, slice(512, 640), slice(0, 260), B_VC),
                   (VC_d[HC + T:TCx, :], 1, slice(640, 768), slice(0, 260), B_VC)]),
        ]
        for i, (rows, ins_, outs_) in enumerate(secs):
            pi_, po_, bi_, bo_ = pins[i], pouts[i], B_pins[i], B_pouts[i]
            for (rs_, cs_, src, bsrc) in ins_:
                dma("pool", pi_[rs_, cs_], src, [bsrc], [bi_])
            P.op("pool", lambda E, pi_=pi_, po_=po_: E.collective_compute("AllGather", ALU.bypass, replica_groups=[[0, 1], [2, 3], [4, 5], [6, 7]],
                                                                      ins=[pi_.opt()], outs=[po_.opt()]), R=[bi_], W=[bo_], dma=True, inc=1)
            for (dst, slot, rs_, cs_, bdst) in outs_:
                r0 = slot * rows
                dma("pool", dst, po_[r0 + rs_.start:r0 + rs_.stop, cs_], [bo_], [bdst])

    def retention(l):
        with Scope():
            kva, B_kva = P.sb("kva", [96, 6, NT, 64], F32), Buf("kva")
            Rb, B_Rb = P.sb("Rb", [96, 6, NT, 64], BF16), Buf("Rbf")
            Rcur, B_Rcur = [P.sb("Rcur%d" % i, [96, 3, 64], F32) for i in range(2)], [Buf("Rcur%d" % i) for i in range(2)]
            Rin, B_Rin = P.sb("Rin", [96, 6, 64], F32), Buf("Rin")
            gc, B_gc = P.sb("gc", [128, 6], F32), Buf("gc")
            gpow, B_gpow = P.sb("gpow", [128, 6, NT], F32), Buf("gpow")
            qdtab, B_qdt = P.sb("qdtab", [128, 6, 128], F32), Buf("qdtab")
            DT, B_DT = P.sb("DT", [128, 8, 128], F32), Buf("DT")
            dtmp, B_dtmp = P.sb("dtmp", [128, 128], F32), Buf("dtmp")
            dma("sp", kva[:], KV_d[:, :, :, :], [B_KV], [B_kva])
            lcol = lambda g, dr: l * 6 + g * 2 + dr
            P.op("act", lambda E: E.activation(out=gc[:], in_=lgp[:, l * 6:l * 6 + 6], func=AF.Exp, scale=128.0), R=[B_lgp], W=[B_gc])
            lg128, B_lg128 = P.sb("lg128", [128, 6], F32), Buf("lg128")
            P.op("dve", lambda E: E.tensor_scalar(out=lg128[:], in0=lgp[:, l * 6:l * 6 + 6], scalar1=128.0, scalar2=None, op0=ALU.mult), R=[B_lgp], W=[B_lg128])
            for g in range(3):
                for dr in range(2):
                    c = g * 2 + dr
                    P.op("act", lambda E, c=c, dr=dr: E.activation(out=gpow[:, c, :], in_=nidx[:, dr * NT:(dr + 1) * NT], func=AF.Exp, scale=lg128[:, c:c + 1]),
                         R=[B_nidx, B_lg128], W=[B_gpow])
                    P.op("act", lambda E, c=c, dr=dr: E.activation(out=qdtab[:, c, :], in_=qexp[:, dr * 128:(dr + 1) * 128], func=AF.Exp, scale=lgp[:, l * 6 + c:l * 6 + c + 1]),
                         R=[B_qexp, B_lgp], W=[B_qdt])
            for h in range(8):
                P.op("act", lambda E, h=h: E.activation(out=DT[:, h, :], in_=rtab[:, 0:128], func=AF.Exp, scale=lg8[:, (l * 2) * 8 + h:(l * 2) * 8 + h + 1]), R=[B_rtab, B_lg8], W=[B_DT])
                P.op("dve", lambda E, h=h: E.tensor_tensor(out=DT[:, h, :], in0=DT[:, h, :], in1=rtab[:, 128:256], op=ALU.mult), R=[B_DT, B_rtab], W=[B_DT])
                P.op("act", lambda E, h=h: E.activation(out=dtmp[:], in_=rtab[:, 256:384], func=AF.Exp, scale=lg8[:, (l * 2 + 1) * 8 + h:(l * 2 + 1) * 8 + h + 1]), R=[B_rtab, B_lg8, B_DT], W=[B_dtmp])
                P.op("dve", lambda E, h=h: E.tensor_tensor(out=dtmp[:], in0=dtmp[:], in1=rtab[:, 384:512], op=ALU.mult), R=[B_dtmp, B_rtab], W=[B_dtmp])
                P.op("dve", lambda E, h=h: E.tensor_tensor(out=DT[:, h, :], in0=DT[:, h, :], in1=dtmp[:], op=ALU.add), R=[B_DT, B_dtmp], W=[B_DT])
                P.op("dve", lambda E, h=h: E.tensor_scalar(out=DT[:, h, :], in0=DT[:, h, :], scalar1=DKS, scalar2=None, op0=ALU.mult), R=[B_DT], W=[B_DT])
            for dr, eng in ((0, "dve"), (1, "dve")):
                Rc, Bc = Rcur[dr], B_Rcur[dr]
                P.op(eng, lambda E, Rc=Rc: E.memset(Rc[:], 0.0), W=[Bc])
                order = range(NT) if dr == 0 else range(NT - 1, -1, -1)
                for n in order:
                    P.op(eng, lambda E, Rc=Rc, n=n, dr=dr: E.tensor_copy(Rb[:, dr * 3:dr * 3 + 3, n, :], Rc[:]), R=[Bc], W=[B_Rb])
                    for g in range(3):
                        c = g * 2 + dr
                        P.op(eng, lambda E, Rc=Rc, n=n, g=g, c=c, dr=dr: E.scalar_tensor_tensor(out=Rc[:, g, :], in0=Rc[:, g, :], scalar=gc[0:96, c:c + 1], in1=kva[:, dr * 3 + g, n, :],
                                                                                               op0=ALU.mult, op1=ALU.add), R=[Bc, B_gc, B_kva], W=[Bc])
                dma("pool", sin_[:, dr * 192:(dr + 1) * 192], Rc[:].rearrange("p g e -> p (g e)"), [Bc], [B_sin])
            P.op("pool", lambda E: E.collective_compute("AllGather", ALU.bypass, replica_groups=[[0, 1], [2, 3], [4, 5], [6, 7]],
                                                        ins=[sin_.opt()], outs=[sout.opt()]), R=[B_sin], W=[B_sout], dma=True, inc=1)
            dma("sp", Rin[:, 0:3, :], sout[0:96, 0:192].rearrange("p (g e) -> p g e", g=3), [B_sout], [B_Rin])
            dma("sp", Rin[:, 3:6, :], sout[96:192, 192:384].rearrange("p (g e) -> p g e", g=3), [B_sout], [B_Rin])
            for dr in range(2):
                P.op("dve", lambda E, dr=dr: E.tensor_scalar(out=Rin[:, dr * 3:dr * 3 + 3, :], in0=Rin[:, dr * 3:dr * 3 + 3, :], scalar1=flags[0:96, 4 + dr:5 + dr], scalar2=None, op0=ALU.mult),
                     R=[B_Rin, B_flags], W=[B_Rin])
            for dr, eng in ((0, "dve"), (1, "dve")):
                for g in range(3):
                    c = g * 2 + dr
                    for n in range(NT):
                        P.op(eng, lambda E, dr=dr, g=g, c=c, n=n: E.scalar_tensor_tensor(out=Rb[:, dr * 3 + g, n, :], in0=Rin[:, dr * 3 + g, :], scalar=gpow[0:96, c, n:n + 1], in1=Rb[:, dr * 3 + g, n, :],
                                                                                        op0=ALU.mult, op1=ALU.add), R=[B_Rin, B_gpow, B_Rb], W=[B_Rb])
            KR_ = os.environ.get('KR', 'pre,out,cc')
            P.enabled = P.enabled and 'out' in KR_
            qrt, B_qrt = P.sb("qrt", [128, 3, ST], BF16), Buf("qrt")
            krt_, B_krt = P.sb("krt_", [128, 3, ST], BF16), Buf("krt_")
            vr_, B_vr = P.sb("vr_", [128, SUB, 512], BF16), Buf("vr_")
            gs_t, B_gs = P.sb("gs_t", [128, SUB, 512], BF16), Buf("gs_t")
            qd, B_qd = P.sb("qd", [128, 6, 128], BF16), Buf("qd")
            wts, B_wts = P.sb("wts", [128, 8, 128], BF16), Buf("wts")
            osb, B_osb = P.sb("osb", [128, 512], F32), Buf("osb")
            osq, B_osq = P.sb("osq", [128, 512], F32), Buf("osq")
            st8, B_st8 = P.sb("st8", [128, 4, 8], F32), Buf("st8")
            on, B_on = P.sb("on", [128, 512], F32), Buf("on")
            mixr, B_mixr = P.sb("mixr", [128, 512], BF16), Buf("mixr")
            mro, B_mro = P.sb("mro", [128, 4, ST], BF16), Buf("mro")
            en_r = P.enabled
            KRO = os.environ.get('KRO', 'qd,S,O,G,TR').split(',')
            for j in range(NS):
                ts = slice(j * ST, (j + 1) * ST)
                dma("sp", qrt[:], QR_d[:, ts].rearrange("(g p) t -> p g t", p=128), [B_QR], [B_qrt])
                dma("sp", krt_[:], KR_d[:, ts].rearrange("(g p) t -> p g t", p=128), [B_KR], [B_krt])
                dma("sp", vr_[:], VR_d[ts, :].rearrange("(s p) c -> p s c", p=128), [B_VR], [B_vr])
                dma("sp", gs_t[:], GS_d[ts, :].rearrange("(s p) c -> p s c", p=128), [B_GS], [B_gs])
                for s in range(SUB):
                    n = j * SUB + s
                    tt = slice(s * 128, (s + 1) * 128)
                    P.enabled = en_r and 'qd' in KRO
                    for c in range(6):
                        P.op("pool", lambda E, c=c, tt=tt: E.tensor_tensor(out=qd[:, c, :], in0=qrt[:, c // 2, tt], in1=qdtab[:, c, :], op=ALU.mult), R=[B_qrt, B_qdt], W=[B_qd])
                    P.enabled = en_r and 'S' in KRO
                    pts = [psb.next() for _ in range(3)]
                    for h in range(8):
                        g, jj = divmod(h, 3)
                        pt, pb = pts[jj]
                        P.op("pe", lambda E, pt=pt, g=g, jj=jj, tt=tt: E.matmul(pt[:, g * 128:(g + 1) * 128], krt_[32 * jj:32 * jj + 32, g, tt], qrt[32 * jj:32 * jj + 32, g, tt], start=True, stop=True),
                             R=[B_qrt, B_krt], W=[pb])
                    for jj in range(3):
                        pt, pb = pts[jj]
                        ng = 3 if jj < 2 else 2
                        P.op("dve", lambda E, pt=pt, jj=jj, ng=ng: E.tensor_tensor(out=wts[:, jj:8:3, :], in0=pt[:, 0:ng * 128].rearrange("p (h t) -> p h t", h=ng), in1=DT[:, jj:8:3, :], op=ALU.mult),
                             R=[pb, B_DT], W=[B_wts])
                    P.enabled = en_r and 'O' in KRO
                    po, pob = psb.next()
                    for h in range(8):
                        g, jj = divmod(h, 3)
                        pr = slice(32 * jj, 32 * jj + 32)
                        P.op("pe", lambda E, po=po, h=h, s=s: E.matmul(po[:, h * 64:(h + 1) * 64], wts[:, h, :], vr_[:, s, h * 64:(h + 1) * 64], start=True, stop=False), R=[B_wts, B_vr], W=[pob])
                        P.op("pe", lambda E, po=po, h=h, g=g, pr=pr, n=n: E.matmul(po[:, h * 64:(h + 1) * 64], qd[pr, g * 2, :], Rb[pr, g, n, :], start=False, stop=False), R=[B_qd, B_Rb], W=[pob])
                        P.op("pe", lambda E, po=po, h=h, g=g, pr=pr, n=n: E.matmul(po[:, h * 64:(h + 1) * 64], qd[pr, g * 2 + 1, :], Rb[pr, 3 + g, n, :], start=False, stop=True), R=[B_qd, B_Rb], W=[pob])
                    P.enabled = en_r and 'G' in KRO
                    P.op("act", lambda E, po=po: E.activation(out=osb[:], in_=po[:], func=AF.Copy), R=[pob], W=[B_osb])
                    P.op("act", lambda E: E.activation(out=osq[:], in_=osb[:], func=AF.Square), R=[B_osb], W=[B_osq])
                    P.op("dve", lambda E: E.tensor_reduce(out=st8[:, 0, :], in_=osb[:].rearrange("p (h e) -> p h e", h=8), axis=AX.X, op=ALU.add), R=[B_osb], W=[B_st8])
                    P.op("dve", lambda E: E.tensor_reduce(out=st8[:, 1, :], in_=osq[:].rearrange("p (h e) -> p h e", h=8), axis=AX.X, op=ALU.add), R=[B_osq], W=[B_st8])
                    P.op("dve", lambda E: E.tensor_scalar(out=st8[:, 0, :], in0=st8[:, 0, :], scalar1=1.0 / 64, scalar2=None, op0=ALU.mult), R=[B_st8], W=[B_st8])
                    P.op("dve", lambda E: E.tensor_tensor(out=st8[:, 2, :], in0=st8[:, 0, :], in1=st8[:, 0, :], op=ALU.mult), R=[B_st8], W=[B_st8])
                    P.op("dve", lambda E: E.scalar_tensor_tensor(out=st8[:, 3, :], in0=st8[:, 1, :], scalar=1.0 / 64, in1=st8[:, 2, :], op0=ALU.mult, op1=ALU.subtract), R=[B_st8], W=[B_st8])
                    P.op("dve", lambda E: E.tensor_scalar(out=st8[:, 3, :], in0=st8[:, 3, :], scalar1=1.0, scalar2=GN_EPS, op0=ALU.mult, op1=ALU.add), R=[B_st8], W=[B_st8])
                    P.op("act", lambda E: E.activation(out=st8[:, 3, :], in_=st8[:, 3, :], func=AF.Sqrt), R=[B_st8], W=[B_st8])
                    P.op("dve", lambda E: E.reciprocal(st8[:, 3, :], st8[:, 3, :]), R=[B_st8], W=[B_st8])
                    for h in range(8):
                        P.op("pool", lambda E, h=h: E.tensor_scalar(out=on[:, h * 64:(h + 1) * 64], in0=osb[:, h * 64:(h + 1) * 64], scalar1=st8[:, 0, h:h + 1], scalar2=st8[:, 3, h:h + 1],
                                                                    op0=ALU.subtract, op1=ALU.mult), R=[B_osb, B_st8], W=[B_on])
                    P.op("pool", lambda E, s=s: E.tensor_tensor(out=mixr[:], in0=on[:], in1=gs_t[:, s, :], op=ALU.mult), R=[B_on, B_gs], W=[B_mixr])
                    P.enabled = en_r and 'TR' in KRO
                    ptb, ptbb = psbf.next()
                    for q in range(4):
                        P.op("pe", lambda E, ptb=ptb, q=q: E.transpose(ptb[:, q * 128:(q + 1) * 128], mixr[:, q * 128:(q + 1) * 128], identb[:]), R=[B_mixr, B_identb], W=[ptbb])
                    evac(ptb[:, 0:512].rearrange("p (q t) -> p q t", q=4), mro[:, :, tt], [ptbb], [B_mro])
                dma("pool", mixT_d[768:1280, ts].rearrange("(q p) t -> p q t", p=128), mro[:], [B_mro], [B_mix])
                P.enabled = en_r

    def attention(l):
        with Scope():
            qt, B_qt = P.sb("qt", [64, T], BF16), Buf("qt")
            kt, B_kt = P.sb("kt", [64, TA], BF16), Buf("kt")
            acc, B_acc = P.sb("acc", [65, T], F32), Buf("acc")
            mo, B_mo = P.sb("mo", [64, T], BF16), Buf("mo")
            racc, B_racc = P.sb("racc", [65, 512], F32), Buf("racc")
            ebr = rot("ebt", 2, [128, 384], BF16)
            vtr = rot("vt", 4, [128, 80, 65], BF16)
            ptr_ = rot("pt", 4, [128, 384], BF16)
            pt2r = rot("pt2", 6, [128, 384], BF16)
            pend = []
            LA = 3
            P.op("pool", lambda E: E.memset(racc[:], 0.0), W=[B_racc])
            jobs = []
            for h in range(12):
                jobs.append((QA_d[h * 64:(h + 1) * 64, :], KA_d[h * 64:(h + 1) * 64, :], TA, VA_d, 780, h * 65, HA, (0, 1, 2), h, (0, 1), None, h * 64, B_QA, B_KA, B_VA))
            for h in range(12):
                g = h // 3
                jobs.append((QC_d[h * 64:(h + 1) * 64, :], KCx_d[g * 64:(g + 1) * 64, :], TCx, VC_d, 260, g * 65, HC, (3,), h, (2, 3), l * 12 + h, 1280 + h * 64, B_QC, B_KC, B_VC))
            for (Qs, Ks, Text, Vd, rowlen, vcol0, H, vars_, h, fcols, sinkc, mrow, BQ, BK, BV) in jobs:
                dma("sp", qt[:], Qs, [BQ], [B_qt])
                dma("sp", kt[:, 0:Text], Ks, [BK], [B_kt])
                P.op("pool", lambda E: E.memset(acc[:], 0.0), W=[B_acc])
                for v in vars_:
                    dil, koff, qoff, NQ, rad = VARIANTS[v]
                    L = T // dil
                    nj = L // 128 + (1 if koff == -64 else 2)
                    ebt, ebb = ebr.next()
                    dma("sp", ebt[:, 0:NQ], EB_d[v, h, :, 0:NQ], [B_EB], [ebb])
                    for r in range(dil):
                        vt, vb = vtr.next()
                        off = (koff * dil + r + H) * rowlen + vcol0
                        nchunk = 16
                        for j0 in range(0, nj, nchunk):
                            j1 = min(nj, j0 + nchunk)
                            src = bass.AP(Vd.tensor, off + j0 * 128 * dil * rowlen, [[dil * rowlen, 128], [128 * dil * rowlen, j1 - j0], [1, 65]])
                            dma("sp", vt[:, j0:j1, :], src, [BV], [vb])
                        for j in range(nj):
                            k0 = 128 * j + koff
                            q_lo = max(0, 128 * j + qoff)
                            q_hi = min(L, 128 * j + qoff + NQ)
                            nq = q_hi - q_lo
                            if nq <= 0:
                                continue
                            qq0 = q_lo - (128 * j + qoff)
                            ks = k0 * dil + r + H
                            qs = q_lo * dil + r
                            if k0 < 0:
                                bcol, Bb = flags[:, fcols[0]:fcols[0] + 1], B_flags
                            elif k0 + 128 > L:
                                bcol, Bb = flags[:, fcols[1]:fcols[1] + 1], B_flags
                            else:
                                bcol, Bb = zcol[:, 0:1], B_zcol
                            ps_, psb_ = psb.next()
                            P.op("pe", lambda E, ps_=ps_, ks=ks, qs=qs, nq=nq, dil=dil: E.matmul(ps_[:, 0:nq], kt[0:64, ks:ks + 127 * dil + 1:dil], qt[0:64, qs:qs + (nq - 1) * dil + 1:dil], start=True, stop=True),
                                 R=[B_kt, B_qt], W=[psb_])
                            p1, p1b = ptr_.next()
                            P.op("act", lambda E, ps_=ps_, p1=p1, nq=nq, bcol=bcol: E.activation(out=p1[:, 0:nq], in_=ps_[:, 0:nq], func=AF.Exp, bias=bcol, scale=0.125), R=[psb_, Bb], W=[p1b])
                            p2, p2b = pt2r.next()
                            P.op("pool", lambda E, p1=p1, p2=p2, nq=nq, qq0=qq0, ebt=ebt: E.tensor_tensor(out=p2[:, 0:nq], in0=p1[:, 0:nq], in1=ebt[:, qq0:qq0 + nq], op=ALU.mult), R=[p1b, ebb], W=[p2b])

                            def st2(vt=vt, vb=vb, j=j, p2=p2, p2b=p2b, nq=nq, qs=qs, dil=dil):
                                po, pob = psb.next()
                                P.op("pe", lambda E, po=po, vt=vt, j=j, p2=p2, nq=nq: E.matmul(po[0:65, 0:nq], vt[:, j, :], p2[:, 0:nq], start=True, stop=True), R=[vb, p2b], W=[pob])
                                P.op("dve", lambda E, po=po, qs=qs, nq=nq, dil=dil: E.tensor_tensor(out=acc[:, qs:qs + (nq - 1) * dil + 1:dil], in0=acc[:, qs:qs + (nq - 1) * dil + 1:dil], in1=po[0:65, 0:nq], op=ALU.add),
                                     R=[B_acc, pob], W=[B_acc])
                            pend.append(st2)
                            while len(pend) > LA:
                                pend.pop(0)()
                while pend:
                    pend.pop(0)()
                if sinkc is not None:
                    P.op("dve", lambda E, sinkc=sinkc: E.tensor_scalar(out=acc[64:65, :], in0=acc[64:65, :], scalar1=sinke[64:65, sinkc:sinkc + 1], scalar2=None, op0=ALU.add), R=[B_acc, B_sinke], W=[B_acc])
                for c in range(T // 512):
                    cs = slice(c * 512, (c + 1) * 512)
                    P.op("dve", lambda E, cs=cs: E.reciprocal(racc[64:65, :], acc[64:65, cs]), R=[B_acc], W=[B_racc])
                    pb_, pbb_ = psb.next()
                    P.op("pe", lambda E, pb_=pb_: E.matmul(pb_[0:64, :], sel[0:65, :], racc[0:65, :], start=True, stop=True), R=[B_racc, B_sel], W=[pbb_])
                    P.op("dve", lambda E, pb_=pb_, cs=cs: E.tensor_tensor(out=mo[:, cs], in0=acc[0:64, cs], in1=pb_[0:64, :], op=ALU.mult), R=[B_acc, pbb_], W=[B_mo])
                dma("pool", mixT_d[mrow:mrow + 64, :], mo[:], [B_mo], [B_mix])

    mixT_v = mixT_d.rearrange("(k p) t -> p k t", p=128)

    def P3(l, last):
        with Scope():
            xt, xb = P.sb("xt3", [128, KC, ST], F32), Buf("xt3")
            mt, mb = P.sb("mt3", [128, KC, ST], BF16), Buf("mt3")
            h2, h2b = P.sb("h2", [128, KC, ST], BF16), Buf("h2")
            actT, actb = P.sb("actT", [128, FC, ST], BF16), Buf("actT")
            wfr = rot("wf3", 4, [128, KC * 128], BF16)
            wdr = rot("wd3", 2, [128, FC * 128], BF16)
            sqr = rot("sq3", 2, [128, ST], BF16)
            rsr = rot("rs3", 1, [128, ST], F32)
            sgr = rot("sg3", 2, [128, ST], F32)
            yor = rot("yo3", 2, [128, D], F32)
            for j in range(NS):
                ts = slice(j * ST, (j + 1) * ST)
                dma("sp", xt[:], xT_v[:, :, ts], [B_xT[j]], [xb])
                dma("sp", mt[:], mixT_v[:, :, ts], [B_mix], [mb])
                for oc in range(16):
                    wf, wb = wfr.next()
                    dma("sp", wf[:], wOb[l][oc], [B_w], [wb])
                    pt, pb = psb.next()
                    for kc in range(KC):
                        P.op("pe", lambda E, pt=pt, wf=wf, kc=kc: E.matmul(pt[:, 0:ST], wf[:, kc * 128:(kc + 1) * 128], mt[:, kc, :], start=(kc == 0), stop=(kc == KC - 1)), R=[wb, mb], W=[pb])
                    P.op("dve", lambda E, pt=pt, oc=oc: E.tensor_tensor(out=xt[:, oc, :], in0=xt[:, oc, :], in1=pt[:, 0:ST], op=ALU.add), R=[xb, pb], W=[xb])
                rs, rb = rms_stats(xt, xb, sqr, rsr)
                for kc in range(KC):
                    P.op(alt(["dve", "pool"]), lambda E, kc=kc, rs=rs: E.tensor_tensor(out=h2[:, kc, :], in0=xt[:, kc, :], in1=rs[:], op=ALU.mult), R=[xb, rb], W=[h2b])
                for fc in range(FC):
                    wg, wgb = wfr.next()
                    dma("sp", wg[:], wGb[l][fc], [B_w], [wgb])
                    wu, wub = wfr.next()
                    dma("sp", wu[:], wUb[l][fc], [B_w], [wub])
                    pg, pgb = psb.next()
                    for kc in range(KC):
                        P.op("pe", lambda E, pg=pg, wg=wg, kc=kc: E.matmul(pg[:, 0:ST], wg[:, kc * 128:(kc + 1) * 128], h2[:, kc, :], start=(kc == 0), stop=(kc == KC - 1)), R=[wgb, h2b], W=[pgb])
                    pu, pub = psb.next()
                    for kc in range(KC):
                        P.op("pe", lambda E, pu=pu, wu=wu, kc=kc: E.matmul(pu[:, 0:ST], wu[:, kc * 128:(kc + 1) * 128], h2[:, kc, :], start=(kc == 0), stop=(kc == KC - 1)), R=[wub, h2b], W=[pub])
                    sg, sgb = sgr.next()
                    P.op("act", lambda E, pg=pg, sg=sg: E.activation(out=sg[:], in_=pg[:, 0:ST], func=AF.Silu), R=[pgb], W=[sgb])
                    P.op("dve", lambda E, pu=pu, sg=sg, fc=fc: E.tensor_tensor(out=actT[:, fc, :], in0=sg[:], in1=pu[:, 0:ST], op=ALU.mult), R=[sgb, pub], W=[actb])
                for oc in range(16):
                    wd, wdb = wdr.next()
                    dma("sp", wd[:], wDb[l][oc], [B_w], [wdb])
                    pt, pb = psb.next()
                    for fc in range(FC):
                        P.op("pe", lambda E, pt=pt, wd=wd, fc=fc: E.matmul(pt[:, 0:ST], wd[:, fc * 128:(fc + 1) * 128], actT[:, fc, :], start=(fc == 0), stop=(fc == FC - 1)), R=[wdb, actb], W=[pb])
                    P.op("dve", lambda E, pt=pt, oc=oc: E.tensor_tensor(out=xt[:, oc, :], in0=xt[:, oc, :], in1=pt[:, 0:ST], op=ALU.add), R=[xb, pb], W=[xb])
                if not last:
                    dma("pool", xT_v[:, :, ts], xt[:], [xb], [B_xT[j]])
                else:
                    rs, rb = rms_stats(xt, xb, sqr, rsr)
                    for kc in range(KC):
                        P.op("dve", lambda E, kc=kc, rs=rs: E.scalar_tensor_tensor(out=xt[:, kc, :], in0=xt[:, kc, :], scalar=gtab[:, 4 * KC + kc:4 * KC + kc + 1], in1=rs[:],
                                                                                                 op0=ALU.mult, op1=ALU.mult), R=[xb, rb, B_gtab], W=[xb])
                    for s in range(SUB):
                        yo, yob = yor.next()
                        for g in range(4):
                            pt, pb = psb.next()
                            for q in range(4):
                                kc = 4 * g + q
                                P.op("pe", lambda E, pt=pt, q=q, kc=kc, s=s: E.transpose(pt[:, q * 128:(q + 1) * 128], xt[:, kc, s * 128:(s + 1) * 128], ident[:]), R=[xb, B_ident], W=[pb])
                            evac(pt[:, :], yo[:, g * 512:(g + 1) * 512], [pb], [yob])
                        dma("pool", y_out[j * ST + s * 128:j * ST + (s + 1) * 128, :], yo[:], [yob], [B_y])

    NLR = int(os.environ.get('KNL', NL))
    for l in range(NLR):
        P.enabled = 'P1' in PH
        P1(l)
        P.enabled = 'X' in PH
        exchange_kv()
        P.enabled = 'R' in PH
        retention(l)
        P.enabled = 'A' in PH
        attention(l)
        P.enabled = 'P3' in PH
        P3(l, l == NL - 1)
    P.enabled = True
    P.barrier()
    P.emit()
    return nc, es


def _fblocks(W, cols_list):
    K = W.shape[0]
    kc = K // 128
    out = np.zeros((len(cols_list), 128, kc * 128), np.float32)
    Wr = W.reshape(kc, 128, W.shape[1])
    for i, cols in enumerate(cols_list):
        cols = np.asarray(cols)
        blk = np.zeros((kc, 128, 128), np.float32)
        ok = cols >= 0
        blk[:, :, ok] = Wr[:, :, cols[ok]]
        out[i] = blk.transpose(1, 0, 2).reshape(128, kc * 128)
    return out


def _tblocks(W, cols):
    K = W.shape[0]
    kc = K // 128
    cols = np.asarray(cols)
    n = len(cols) // 512
    Wc = np.zeros((K, len(cols)), np.float32)
    ok = cols >= 0
    Wc[:, ok] = W[:, cols[ok]]
    return np.ascontiguousarray(Wc.reshape(kc, 128, n, 512).transpose(2, 1, 0, 3)).reshape(n, 128, kc * 512)


_CACHE = {}


def kernel(x_prompt, x_sample, rel_bias, norm1_g, w_in, ret_decay_fwd, ret_decay_bwd, attn_sink,
           w_out, norm2_g, w_gate, w_up, w_down, final_norm_g):
    f = lambda a: np.asarray(a, dtype=np.float32)
    x_prompt, x_sample = f(x_prompt), f(x_sample)
    T = x_prompt.shape[1]
    assert x_prompt.shape[0] == 4 and x_sample.shape[0] == 2 and x_sample.shape[1] == 2 * T
    if T not in _CACHE:
        _CACHE[T] = build(T)
    nc, es = _CACHE[T]
    w_in, w_out, w_gate, w_up, w_down = f(w_in), f(w_out), f(w_gate), f(w_up), f(w_down)
    ar = np.arange
    common = {}
    for l in range(NL):
        fl = [ar(oc * 128, (oc + 1) * 128) for oc in range(6)]
        fl += [768 + ar(oc * 128, (oc + 1) * 128) for oc in range(6)]
        fl += [3840 + ar(oc * 128, (oc + 1) * 128) for oc in range(6)]
        fl += [4608 + ar(oc * 128, (oc + 1) * 128) for oc in range(2)]
        for base in (2304, 2560):
            for g in range(3):
                cols = np.full(128, -1)
                nh = 3 if g < 2 else 2
                cols[:32 * nh] = base + g * 96 + ar(32 * nh)
                fl.append(cols)
        common["wFin%d" % l] = _fblocks(w_in[l], fl)
        tcols = np.concatenate([1536 + ar(768), 4864 + ar(256), 2816 + ar(512), 3328 + ar(512), 2560 + ar(256), np.full(256, -1)])
        common["wTin%d" % l] = _tblocks(w_in[l], tcols)
        common["wOut%d" % l] = _fblocks(w_out[l], [ar(oc * 128, (oc + 1) * 128) for oc in range(16)])
        common["wGa%d" % l] = _fblocks(w_gate[l], [ar(oc * 128, (oc + 1) * 128) for oc in range(FC)])
        common["wUp%d" % l] = _fblocks(w_up[l], [ar(oc * 128, (oc + 1) * 128) for oc in range(FC)])
        common["wDn%d" % l] = _fblocks(w_down[l], [ar(oc * 128, (oc + 1) * 128) for oc in range(16)])
    g1, g2, gf = f(norm1_g), f(norm2_g), f(final_norm_g)
    gt = [g1[0], g1[1], g2[0], g2[1], gf]
    common["gtab"] = np.concatenate([g.reshape(KC, 128).T for g in gt], axis=1).astype(np.float32)
    df, db = f(ret_decay_fwd), f(ret_decay_bwd)
    dec = np.ones((128, NL * 6), np.float32) * 8.0
    dec8 = np.zeros((128, NL * 16), np.float32)
    for l in range(NL):
        for g in range(3):
            for dr, dd in enumerate((df, db)):
                for p in range(96):
                    h = g * 3 + p // 32
                    if h < 8:
                        dec[p, l * 6 + g * 2 + dr] = dd[l, h]
        for dr, dd in enumerate((df, db)):
            dec8[:, (l * 2 + dr) * 8:(l * 2 + dr) * 8 + 8] = dd[l][None, :]
    common["dec"] = dec
    common["dec8"] = dec8
    common["sink"] = np.tile(f(attn_sink).reshape(1, NL * 12), (128, 1))
    common["relb"] = f(rel_bias)
    common.update(host_consts(T))
    if SMALLW:
        for k_ in list(common):
            if k_[:2] in ('wF', 'wT', 'wO', 'wG', 'wU', 'wD'):
                common[k_] = np.ascontiguousarray(common[k_][:1])
    in_maps = []
    for c in range(8):
        m = dict(common)
        if c < 4:
            m["x"] = np.ascontiguousarray(x_prompt[c])
            lc, rc, pos0 = 0.0, 0.0, 0
        else:
            sq, hf = divmod(c - 4, 2)
            m["x"] = np.ascontiguousarray(x_sample[sq, hf * T:(hf + 1) * T])
            lc, rc, pos0 = float(hf == 1), float(hf == 0), hf * T
        fl_ = np.zeros((128, 8), np.float32)
        nl, nr = (0.0 if lc else NEGB), (0.0 if rc else NEGB)
        fl_[0:64, 0] = nl
        fl_[64:128, 1] = nr
        fl_[:, 2] = nl
        fl_[:, 3] = nr
        fl_[:, 4] = lc
        fl_[:, 5] = rc
        m["flags"] = fl_
        cF, sF, cT, sT = rope_tables(pos0, T)
        m["cF"], m["sF"], m["cT"], m["sT"] = cF, sF, cT, sT
        in_maps.append(m)
    res = run_bass_kernel_spmd(nc, in_maps, core_ids=list(range(8)))
    if os.environ.get('KDBG'):
        kernel.dbg = res.results
    ys = [np.asarray(r["y"], dtype=np.float32) for r in res.results]
    y_prompt = np.stack(ys[0:4], axis=0)
    y_sample = np.stack([np.concatenate(ys[4:6], axis=0), np.concatenate(ys[6:8], axis=0)], axis=0)
    return (y_prompt, y_sample)
```
